# Optimizing a Trainium2 kernel written in Bass

```python
import math
import jax, jax.numpy as jnp
from jax import lax
import numpy as np

D_MODEL = 2048
BATCH = 16
SEQ = 2048
DEPTH = 2

CHUNK = 64
Q_BLOCK = 128
NORM_EPS = 1e-6
D_FF = ((8 * D_MODEL // 3 + 127) // 128) * 128

RET_WIDTH = D_MODEL // 2
RET_HEADS = 4
RET_HEAD_DIM = RET_WIDTH // RET_HEADS
RET_THETA = 10000.0

SSM_WIDTH = D_MODEL - RET_WIDTH
SSM_GROUP = 16
SSM_GROUPS = SSM_WIDTH // SSM_GROUP
SSM_STATE = 64
DT_MIN = 1e-3
DT_MAX = 1e-1

DIFF_HEAD_DIM = 128
DIFF_HEADS = D_MODEL // (2 * DIFF_HEAD_DIM)
ROPE_THETA = 500000.0
ROPE_FRAC = 4

N_EVEN = (DEPTH + 1) // 2
N_ODD = DEPTH // 2

kernel_name = "chunk_causal_retention_s5_diffattn_macaron"


def rms_norm(x, g):
    xf = x.astype(jnp.float32)
    y = xf * lax.rsqrt(jnp.mean(xf * xf, axis=-1, keepdims=True) + NORM_EPS)
    return (y * g.astype(jnp.float32)).astype(x.dtype)


def head_rms(x):
    xf = x.astype(jnp.float32)
    return (xf * lax.rsqrt(jnp.mean(xf * xf, axis=-1, keepdims=True) + NORM_EPS)).astype(x.dtype)


def swiglu(x, w_gate, w_up, w_down):
    return (jax.nn.silu(x @ w_gate) * (x @ w_up)) @ w_down


def rope_tables(seq, rot_dim, theta):
    inv = 1.0 / (theta ** (jnp.arange(0, rot_dim, 2, dtype=jnp.float32) / rot_dim))
    ang = jnp.arange(seq, dtype=jnp.float32)[:, None] * inv[None, :]
    return jnp.cos(ang), jnp.sin(ang)


def apply_rotary(x, cos, sin):
    half = cos.shape[-1]
    c = cos[:, None, :].astype(x.dtype)
    s = sin[:, None, :].astype(x.dtype)
    x1 = x[..., :half]
    x2 = x[..., half:2 * half]
    return jnp.concatenate([x1 * c - x2 * s, x2 * c + x1 * s, x[..., 2 * half:]], axis=-1)


def retention(q, k, v):
    b, s, h, d = q.shape
    nc = s // CHUNK
    dt = q.dtype
    log_g = jnp.log(1.0 - 2.0 ** (-5.0 - jnp.arange(h, dtype=jnp.float32)))
    idx = jnp.arange(CHUNK, dtype=jnp.float32)
    intra_decay = jnp.exp(log_g[:, None, None] * jnp.abs(idx[:, None] - idx[None, :])).astype(dt)
    q_decay = jnp.exp(log_g[None, :] * (idx[:, None] + 1.0)).astype(dt)[None, :, :, None]
    k_decay = jnp.exp(log_g[None, :] * (CHUNK - 1.0 - idx[:, None])).astype(dt)[None, :, :, None]
    chunk_decay = jnp.exp(log_g * CHUNK).astype(dt)[None, :, None, None]
    k = k * (d ** -0.5)
    qc = q.reshape(b, nc, CHUNK, h, d)
    kc = k.reshape(b, nc, CHUNK, h, d)
    vc = v.reshape(b, nc, CHUNK, h, d)
    scores = jnp.einsum('bnihd,bnjhd->bnhij', qc, kc) * intra_decay
    intra = jnp.einsum('bnhij,bnjhd->bnihd', scores, vc)

    def step(state, xs):
        q_i, k_i, v_i = xs
        inter = jnp.einsum('bihd,bhde->bihe', q_i * q_decay, state)
        state = state * chunk_decay + jnp.einsum('bjhd,bjhe->bhde', k_i * k_decay, v_i)
        return state, inter

    xs = (jnp.moveaxis(qc, 1, 0), jnp.moveaxis(kc, 1, 0), jnp.moveaxis(vc, 1, 0))
    _, inter = lax.scan(step, jnp.zeros((b, h, d, d), dt), xs)
    out = intra + jnp.moveaxis(inter, 0, 1)
    return out.reshape(b, s, h, d)


def s5_block(u, lam_re, lam_im, log_step, b_re, b_im, c_re, c_im, d_skip, w_glu, b_glu):
    bsz, s, _ = u.shape
    dt = u.dtype
    ug = u.reshape(bsz, s, SSM_GROUPS, SSM_GROUP)
    f32 = jnp.float32
    step = jnp.exp(log_step.astype(f32))[:, None]
    lr = lam_re.astype(f32)
    li = lam_im.astype(f32)
    mag = jnp.exp(lr * step)
    a_re = mag * jnp.cos(li * step)
    a_im = mag * jnp.sin(li * step)
    den = lr * lr + li * li
    nr = a_re - 1.0
    f_re = (nr * lr + a_im * li) / den
    f_im = (a_im * lr - nr * li) / den
    br = b_re.astype(f32)
    bi = b_im.astype(f32)
    bb_re = (f_re[..., None] * br - f_im[..., None] * bi).astype(dt)
    bb_im = (f_re[..., None] * bi + f_im[..., None] * br).astype(dt)
    x_re = jnp.einsum('bsgp,gnp->bsgn', ug, bb_re)
    x_im = jnp.einsum('bsgp,gnp->bsgn', ug, bb_im)
    shape = (1, s, SSM_GROUPS, SSM_STATE)
    a_re_s = jnp.broadcast_to(a_re.astype(dt)[None, None], shape)
    a_im_s = jnp.broadcast_to(a_im.astype(dt)[None, None], shape)

    def combine(left, right):
        ar1, ai1, br1, bi1 = left
        ar2, ai2, br2, bi2 = right
        return (ar1 * ar2 - ai1 * ai2,
                ar1 * ai2 + ai1 * ar2,
                ar2 * br1 - ai2 * bi1 + br2,
                ar2 * bi1 + ai2 * br1 + bi2)

    _, _, h_re, h_im = lax.associative_scan(combine, (a_re_s, a_im_s, x_re, x_im), axis=1)
    y = (jnp.einsum('bsgn,gpn->bsgp', h_re, c_re) - jnp.einsum('bsgn,gpn->bsgp', h_im, c_im))
    y = y.reshape(bsz, s, SSM_WIDTH) + d_skip * u
    z = jax.nn.gelu(y)
    return z * jax.nn.sigmoid(z @ w_glu + b_glu)


def retention_s5_mixer(h, w_in, w_out, lam_re, lam_im, log_step, b_re, b_im, c_re, c_im,
                       d_skip, w_glu, b_glu, ret_cos, ret_sin):
    b, s, _ = h.shape
    proj = h @ w_in
    q, k, v, g, u = jnp.split(proj, [RET_WIDTH, 2 * RET_WIDTH, 3 * RET_WIDTH, 4 * RET_WIDTH], axis=-1)
    q = apply_rotary(q.reshape(b, s, RET_HEADS, RET_HEAD_DIM), ret_cos, ret_sin)
    k = apply_rotary(k.reshape(b, s, RET_HEADS, RET_HEAD_DIM), ret_cos, ret_sin)
    v = v.reshape(b, s, RET_HEADS, RET_HEAD_DIM)
    y_a = head_rms(retention(q, k, v)).reshape(b, s, RET_WIDTH) * jax.nn.silu(g)
    y_b = s5_block(u, lam_re, lam_im, log_step, b_re, b_im, c_re, c_im, d_skip, w_glu, b_glu)
    return jnp.concatenate([y_a, y_b], axis=-1) @ w_out


def diff_attention(h, w_qkv, w_out, lq1, lk1, lq2, lk2, subln, cos, sin, lambda_init):
    b, s, _ = h.shape
    q, k, v = jnp.split(h @ w_qkv, 3, axis=-1)
    q = apply_rotary(q.reshape(b, s, 2 * DIFF_HEADS, DIFF_HEAD_DIM), cos, sin)
    k = apply_rotary(k.reshape(b, s, 2 * DIFF_HEADS, DIFF_HEAD_DIM), cos, sin)
    q = q.reshape(b, s, DIFF_HEADS, 2, DIFF_HEAD_DIM)
    k = k.reshape(b, s, DIFF_HEADS, 2, DIFF_HEAD_DIM)
    v = v.reshape(b, s, DIFF_HEADS, 2 * DIFF_HEAD_DIM)
    f32 = jnp.float32
    lam = (jnp.exp(jnp.sum(lq1.astype(f32) * lk1.astype(f32)))
           - jnp.exp(jnp.sum(lq2.astype(f32) * lk2.astype(f32))) + lambda_init)
    scale = DIFF_HEAD_DIM ** -0.5
    neg = jnp.finfo(f32).min
    outs = []
    for blk in range(s // Q_BLOCK):
        q0 = blk * Q_BLOCK
        kv_end = q0 + Q_BLOCK
        qb = q[:, q0:kv_end]
        kb = k[:, :kv_end]
        vb = v[:, :kv_end]
        sc = jnp.einsum('bqhcd,bkhcd->bhcqk', qb, kb).astype(f32) * scale
        q_chunk = jnp.arange(q0, kv_end) // CHUNK
        k_chunk = jnp.arange(kv_end) // CHUNK
        sc = jnp.where(q_chunk[:, None] >= k_chunk[None, :], sc, neg)
        p = jax.nn.softmax(sc, axis=-1)
        attn = p[:, :, 0] - lam * p[:, :, 1]
        outs.append(jnp.einsum('bhqk,bkhe->bqhe', attn.astype(v.dtype), vb))
    o = jnp.concatenate(outs, axis=1)
    o = rms_norm(o, subln) * (1.0 - lambda_init)
    return o.reshape(b, s, D_MODEL) @ w_out


def setup_inputs(seed: int = 0) -> dict:
    key = jax.random.key(seed)
    ks = jax.random.split(key, 32)
    f32 = jnp.float32
    D = D_MODEL

    def nrm(k, shape, scale):
        return jax.random.normal(k, shape, f32) * scale

    n_idx = jnp.arange(SSM_STATE, dtype=f32)
    return {
        "x": jax.random.normal(ks[0], (BATCH, SEQ, D), f32),
        "ffn_norm": 1.0 + nrm(ks[1], (DEPTH, 2, D), 0.01),
        "ffn_w_gate": nrm(ks[2], (DEPTH, 2, D, D_FF), D ** -0.5),
        "ffn_w_up": nrm(ks[3], (DEPTH, 2, D, D_FF), D ** -0.5),
        "ffn_w_down": nrm(ks[4], (DEPTH, 2, D_FF, D), D_FF ** -0.5),
        "mix_norm": 1.0 + nrm(ks[5], (DEPTH, D), 0.01),
        "ab_w_in": nrm(ks[6], (N_EVEN, D, 4 * RET_WIDTH + SSM_WIDTH), D ** -0.5),
        "ab_w_out": nrm(ks[7], (N_EVEN, RET_WIDTH + SSM_WIDTH, D), (RET_WIDTH + SSM_WIDTH) ** -0.5),
        "ssm_lambda_re": -0.5 + nrm(ks[8], (N_EVEN, SSM_GROUPS, SSM_STATE), 0.01),
        "ssm_lambda_im": math.pi * n_idx + nrm(ks[9], (N_EVEN, SSM_GROUPS, SSM_STATE), 0.01),
        "ssm_log_step": jax.random.uniform(ks[10], (N_EVEN, SSM_GROUPS), f32,
                                           math.log(DT_MIN), math.log(DT_MAX)),
        "ssm_b_re": nrm(ks[11], (N_EVEN, SSM_GROUPS, SSM_STATE, SSM_GROUP), (2 * SSM_GROUP) ** -0.5),
        "ssm_b_im": nrm(ks[12], (N_EVEN, SSM_GROUPS, SSM_STATE, SSM_GROUP), (2 * SSM_GROUP) ** -0.5),
        "ssm_c_re": nrm(ks[13], (N_EVEN, SSM_GROUPS, SSM_GROUP, SSM_STATE), SSM_STATE ** -0.5),
        "ssm_c_im": nrm(ks[14], (N_EVEN, SSM_GROUPS, SSM_GROUP, SSM_STATE), SSM_STATE ** -0.5),
        "ssm_d": nrm(ks[15], (N_EVEN, SSM_WIDTH), 1.0),
        "ssm_w_glu": nrm(ks[16], (N_EVEN, SSM_WIDTH, SSM_WIDTH), SSM_WIDTH ** -0.5),
        "ssm_b_glu": nrm(ks[17], (N_EVEN, SSM_WIDTH), 0.01),
        "c_w_qkv": nrm(ks[18], (N_ODD, D, 3 * D), D ** -0.5),
        "c_w_out": nrm(ks[19], (N_ODD, D, D), D ** -0.5),
        "c_lambda_q1": nrm(ks[20], (N_ODD, DIFF_HEAD_DIM), 0.1),
        "c_lambda_k1": nrm(ks[21], (N_ODD, DIFF_HEAD_DIM), 0.1),
        "c_lambda_q2": nrm(ks[22], (N_ODD, DIFF_HEAD_DIM), 0.1),
        "c_lambda_k2": nrm(ks[23], (N_ODD, DIFF_HEAD_DIM), 0.1),
        "c_subln": 1.0 + nrm(ks[24], (N_ODD, 2 * DIFF_HEAD_DIM), 0.01),
        "final_norm": 1.0 + nrm(ks[25], (D,), 0.01),
    }


def reference(x, ffn_norm, ffn_w_gate, ffn_w_up, ffn_w_down, mix_norm, ab_w_in, ab_w_out,
              ssm_lambda_re, ssm_lambda_im, ssm_log_step, ssm_b_re, ssm_b_im, ssm_c_re, ssm_c_im,
              ssm_d, ssm_w_glu, ssm_b_glu, c_w_qkv, c_w_out, c_lambda_q1, c_lambda_k1,
              c_lambda_q2, c_lambda_k2, c_subln, final_norm):
    s = x.shape[1]
    ret_cos, ret_sin = rope_tables(s, RET_HEAD_DIM, RET_THETA)
    att_cos, att_sin = rope_tables(s, DIFF_HEAD_DIM // ROPE_FRAC, ROPE_THETA)
    h = x
    for layer in range(DEPTH):
        i = layer // 2
        h = h + 0.5 * swiglu(rms_norm(h, ffn_norm[layer, 0]), ffn_w_gate[layer, 0],
                             ffn_w_up[layer, 0], ffn_w_down[layer, 0])
        hn = rms_norm(h, mix_norm[layer])
        if layer % 2 == 0:
            mix = retention_s5_mixer(hn, ab_w_in[i], ab_w_out[i], ssm_lambda_re[i], ssm_lambda_im[i],
                                     ssm_log_step[i], ssm_b_re[i], ssm_b_im[i], ssm_c_re[i],
                                     ssm_c_im[i], ssm_d[i], ssm_w_glu[i], ssm_b_glu[i],
                                     ret_cos, ret_sin)
        else:
            lambda_init = 0.8 - 0.6 * math.exp(-0.3 * layer)
            mix = diff_attention(hn, c_w_qkv[i], c_w_out[i], c_lambda_q1[i], c_lambda_k1[i],
                                 c_lambda_q2[i], c_lambda_k2[i], c_subln[i], att_cos, att_sin,
                                 lambda_init)
        h = h + mix
        h = h + 0.5 * swiglu(rms_norm(h, ffn_norm[layer, 1]), ffn_w_gate[layer, 1],
                             ffn_w_up[layer, 1], ffn_w_down[layer, 1])
    return rms_norm(h, final_norm)
```

```python
import math
import contextlib
import numpy as np
import concourse.bass as bass
import concourse.mybir as mybir
from concourse.bass_utils import run_bass_kernel_spmd

F32 = mybir.dt.float32
BF16 = mybir.dt.bfloat16
I32 = mybir.dt.int32
U8 = mybir.dt.uint8
ALU = mybir.AluOpType
AF = mybir.ActivationFunctionType
AX = mybir.AxisListType

D = 2048
DFF = 5504
NFF = DFF // 128
KD = D // 128
SEQ = 2048
NCORES = 8
TOK = 2 * SEQ
EPS = 1e-6

COMPUTE = ("pe", "act", "dve", "pool")
ENG_ATTR = {"pe": "tensor", "act": "scalar", "dve": "vector", "pool": "gpsimd", "sp": "sync"}


class Tok:
    __slots__ = ("name", "w", "r")

    def __init__(self, name=""):
        self.name = name
        self.w = []
        self.r = []


def toks(n, name=""):
    return [Tok(f"{name}{i}") for i in range(n)]


class Op:
    __slots__ = ("eng", "fn", "deps", "dma", "signal", "sem", "val", "know")

    def __init__(self, eng, fn, dma):
        self.eng = eng
        self.fn = fn
        self.dma = dma
        self.deps = []
        self.signal = False
        self.sem = None
        self.val = 0
        self.know = None


class Prog:
    def __init__(self, nc):
        self.nc = nc
        self.ops = []
        self.last = {}
        self.pending = {}
        self.dma_ops = []
        self.n_dma_sems = {"sp": 30, "pool": 16, "act": 8}

    def op(self, eng, fn, reads=(), writes=(), dma=False):
        o = Op(eng, fn, dma)
        deps = []
        for t in reads:
            deps.extend(t.w)
        for t in writes:
            deps.extend(t.w)
            for x in t.r:
                if x.dma or dma or x.eng != eng:
                    deps.append(x)
        if eng in self.pending:
            deps.extend(self.pending.pop(eng))
        seen = set()
        for d in deps:
            if id(d) in seen:
                continue
            seen.add(id(d))
            if (not d.dma) and (not dma) and d.eng == "pe" and eng == "pe":
                continue
            o.deps.append(d)
        for t in reads:
            if not dma:
                t.r = [x for x in t.r if x.dma or x.eng != eng]
            t.r.append(o)
        for t in writes:
            t.w = [o]
            t.r = []
        self.ops.append(o)
        if dma:
            self.dma_ops.append(o)
        else:
            self.last[eng] = o
        return o

    def dma(self, q, out, in_, reads=(), writes=(), **kw):
        return self.op(q, lambda e: e.dma_start(out=out, in_=in_, **kw), reads, writes, dma=True)

    def barrier(self):
        deps = list(self.last.values()) + self.dma_ops
        self.dma_ops = []
        for e in ENG_ATTR:
            self.pending[e] = list(deps) + self.pending.get(e, [])

    def finish(self):
        self.barrier()
        self.op("sp", lambda e: e.nop(), (), ())

    def emit(self):
        nc = self.nc
        ops = self.ops
        with contextlib.ExitStack() as st:
            sems = {e: st.enter_context(nc.semaphore("s_" + e)) for e in COMPUTE}
            dsems = {q: [st.enter_context(nc.semaphore(f"d_{q}{i}")) for i in range(n)]
                     for q, n in self.n_dma_sems.items()}
            dcnt = {q: [0] * n for q, n in self.n_dma_sems.items()}
            dnext = {q: 0 for q in self.n_dma_sems}
            dprev = {q: [None] * n for q, n in self.n_dma_sems.items()}
            for o in ops:
                if o.dma:
                    q = o.eng
                    i = dnext[q]
                    dnext[q] = (i + 1) % len(dsems[q])
                    prev = dprev[q][i]
                    if prev is not None:
                        o.deps.append(prev)
                    dcnt[q][i] += 16
                    o.sem = dsems[q][i]
                    o.val = dcnt[q][i]
                    dprev[q][i] = o
            for o in ops:
                for d in o.deps:
                    d.signal = True
            cnt = {e: 0 for e in COMPUTE}
            for o in ops:
                if (not o.dma) and o.signal:
                    cnt[o.eng] += 1
                    o.sem = sems[o.eng]
                    o.val = cnt[o.eng]
            know = {e: {} for e in ENG_ATTR}
            plan = {e: [] for e in ENG_ATTR}
            nw = 0
            for o in ops:
                kn = know[o.eng]
                for d in o.deps:
                    key = id(d.sem)
                    if kn.get(key, 0) >= d.val:
                        continue
                    plan[o.eng].append(("w", d.sem, d.val))
                    nw += 1
                    if d.know is not None:
                        for k2, v2 in d.know.items():
                            if kn.get(k2, 0) < v2:
                                kn[k2] = v2
                    if kn.get(key, 0) < d.val:
                        kn[key] = d.val
                plan[o.eng].append(("o", o))
                if o.dma or o.signal:
                    o.know = dict(kn)
                    if not o.dma:
                        o.know[id(o.sem)] = o.val
            self.n_waits = nw
            with nc.Block() as block:
                for ename, attr in ENG_ATTR.items():
                    lst = plan[ename]
                    if not lst:
                        continue

                    def body(eng, lst=lst):
                        for item in lst:
                            if item[0] == "w":
                                eng.wait_ge(item[1], item[2])
                            else:
                                o = item[1]
                                inst = o.fn(eng)
                                if o.dma:
                                    inst.then_inc(o.sem, 16)
                                elif o.signal:
                                    inst.then_inc(o.sem, 1)
                    getattr(block, attr)(body)


class Arena:
    def __init__(self, nc, nbytes, name="arena"):
        self.t = nc.alloc_sbuf_tensor(name, [128, nbytes], U8)
        self.n = nbytes
        self.off = 0
        self.marks = []

    def alloc(self, cols, dtype, parts=128):
        sz = {F32: 4, BF16: 2, I32: 4, U8: 1}[dtype]
        nb = cols * sz
        off = (self.off + 63) // 64 * 64
        if off + nb > self.n:
            raise RuntimeError(f"arena overflow: want {nb} at {off} of {self.n}")
        self.off = off + nb
        ap = self.t[0:parts, off:off + nb]
        if dtype != U8:
            ap = ap.bitcast(dtype)
        return ap

    def mark(self):
        self.marks.append(self.off)

    def release(self):
        self.off = self.marks.pop()


class Ctx:
    pass


def cast_op(P, i, out, in_, reads, writes):
    if i % 2 == 0:
        return P.op("act", lambda e: e.copy(out=out, in_=in_), reads, writes)
    return P.op("dve", lambda e: e.tensor_copy(out=out, in_=in_), reads, writes)


def load_gain(P, C, gain_dram):
    A = C.A
    g = A.alloc(KD, F32)
    t = Tok("gain")
    with C.nc.allow_non_contiguous_dma(reason="tiny gain load"):
        pass
    P.dma("sp", g, gain_dram.rearrange("(k p) -> p k", p=128), (), (t,),
          allow_slow_non_contiguous=True)
    return g, t


def norm_T(P, C, src, tok0, T, gain, gain_t, xnT, xn_toks, src_tok=None):
    A, ps = C.A, C.ps
    A.mark()
    xk = [A.alloc(2 * 512, F32) for _ in range(2)]
    xk_t = toks(2, "xk")
    sq = [A.alloc(2 * 512, BF16) for _ in range(2)]
    sq_t = toks(2, "sq")
    rstd = A.alloc(512, F32)
    rstd_t = Tok("rstd")
    ssq = ps[7][:, 0:512]
    ssq_t = C.ps_t[7]
    srcr = [src_tok] if src_tok is not None else []
    n = 0
    for blk in range(T // 512):
        c0 = tok0 + blk * 512
        for k2 in range(KD // 2):
            b = n % 2
            n += 1
            P.dma("sp", xk[b].rearrange("p (k t) -> p k t", k=2),
                  src[k2 * 256:(k2 + 1) * 256, c0:c0 + 512].rearrange("(k p) t -> p k t", p=128),
                  srcr, (xk_t[b],))
            P.op("act", lambda e, b=b: e.activation(out=sq[b], in_=xk[b], func=AF.Square),
                 (xk_t[b],), (sq_t[b],))
            for j in range(2):
                k = k2 * 2 + j
                P.op("pe", lambda e, b=b, j=j, k=k: e.matmul(
                    ssq, lhsT=C.ones_bf, rhs=sq[b][:, j * 512:(j + 1) * 512],
                    start=(k == 0), stop=(k == KD - 1)),
                    (sq_t[b], C.const_t), (ssq_t,) if k in (0, KD - 1) else ())
        P.op("dve", lambda e: e.tensor_scalar(out=rstd, in0=ssq, scalar1=1.0 / D, scalar2=EPS,
                                              op0=ALU.mult, op1=ALU.add), (ssq_t,), (rstd_t,))
        P.op("act", lambda e: e.activation(out=rstd, in_=rstd, func=AF.Sqrt), (), (rstd_t,))
        P.op("dve", lambda e: e.reciprocal(out=rstd, in_=rstd), (), (rstd_t,))
        for k2 in range(KD // 2):
            b = n % 2
            n += 1
            P.dma("sp", xk[b].rearrange("p (k t) -> p k t", k=2),
                  src[k2 * 256:(k2 + 1) * 256, c0:c0 + 512].rearrange("(k p) t -> p k t", p=128),
                  srcr, (xk_t[b],))
            for j in range(2):
                k = k2 * 2 + j
                P.op("dve", lambda e, b=b, j=j, k=k, blk=blk: e.scalar_tensor_tensor(
                    out=xnT[:, k, blk * 512:(blk + 1) * 512], in0=xk[b][:, j * 512:(j + 1) * 512],
                    scalar=gain[:, k:k + 1], in1=rstd, op0=ALU.mult, op1=ALU.mult),
                    (xk_t[b], rstd_t, gain_t), (xn_toks[blk],))
    A.release()


def stream_weight(P, C, w_dram_cols, K, ring, n, rows_per_piece=16):
    wb, wb_t = ring["wb"][ring["nb"] % len(ring["wb"])], ring["wb_t"][ring["nb"] % len(ring["wb"])]
    ring["nb"] += 1
    k0 = 0
    while k0 < K:
        kk = min(rows_per_piece, K - k0)
        s = ring["ns"] % len(ring["st"])
        ring["ns"] += 1
        st, st_t = ring["st"][s], ring["st_t"][s]
        P.dma("sp", st[:, 0:kk * 128].rearrange("p (k c) -> p k c", k=kk),
              w_dram_cols[k0 * 128:(k0 + kk) * 128, :].rearrange("(k p) c -> p k c", p=128),
              (), (st_t,))
        cast_op(P, ring["ns"], wb[:, k0 * 128:(k0 + kk) * 128], st[:, 0:kk * 128], (st_t,), (wb_t,))
        k0 += kk
    return wb, wb_t


def ffn_phase(P, C, h_in, h_out, gain_dram, wg, wu, wd, ntok, final_gain=None, out_final=None):
    A, ps, ps_t = C.A, C.ps, C.ps_t
    T = 1024
    NB = T // 512
    A.mark()
    gain, gain_t = load_gain(P, C, gain_dram)
    xnT = A.alloc(KD * T, BF16).rearrange("p (k t) -> p k t", k=KD)
    xn_t = toks(NB, "xn")
    hT = A.alloc(NFF * T, BF16).rearrange("p (c t) -> p c t", c=NFF)
    hT_t = [[Tok(f"h{c}_{b}") for b in range(NB)] for c in range(NFF)]
    ringB = dict(st=[A.alloc(16 * 128, F32) for _ in range(3)], st_t=toks(3, "st"),
                 wb=[A.alloc(16 * 128, BF16) for _ in range(4)], wb_t=toks(4, "wb"), ns=0, nb=0)
    ringC = dict(st=ringB["st"], st_t=ringB["st_t"],
                 wb=[A.alloc(NFF * 128, BF16) for _ in range(2)], wb_t=toks(2, "wd"), ns=0, nb=0)
    sg = [A.alloc(512, BF16) for _ in range(2)]
    sg_t = toks(2, "sg")
    xr = [A.alloc(512, F32) for _ in range(3)]
    xr_t = toks(3, "xr")
    nxr = 0
    nsg = 0
    for tile in range(ntok // T):
        tok0 = tile * T
        norm_T(P, C, h_in, tok0, T, gain, gain_t, xnT, xn_t)
        for c in range(NFF):
            wgb, wgb_t = stream_weight(P, C, wg[:, c * 128:(c + 1) * 128], KD, ringB, None)
            wub, wub_t = stream_weight(P, C, wu[:, c * 128:(c + 1) * 128], KD, ringB, None)
            for blk in range(NB):
                pg, pg_t = ps[2 * blk], ps_t[2 * blk]
                pu, pu_t = ps[2 * blk + 1], ps_t[2 * blk + 1]
                for (wb, wb_t, pp, pp_t) in ((wgb, wgb_t, pg, pg_t), (wub, wub_t, pu, pu_t)):
                    for k in range(KD):
                        P.op("pe", lambda e, wb=wb, pp=pp, k=k, blk=blk: e.matmul(
                            pp, lhsT=wb[:, k * 128:(k + 1) * 128], rhs=xnT[:, k, blk * 512:(blk + 1) * 512],
                            start=(k == 0), stop=(k == KD - 1)),
                            (wb_t, xn_t[blk]), (pp_t,) if k in (0, KD - 1) else ())
                s = nsg % 2
                nsg += 1
                P.op("act", lambda e, s=s, pg=pg: e.activation(out=sg[s], in_=pg, func=AF.Silu),
                     (pg_t,), (sg_t[s],))
                P.op("dve", lambda e, s=s, pu=pu, c=c, blk=blk: e.tensor_tensor(
                    out=hT[:, c, blk * 512:(blk + 1) * 512], in0=pu, in1=sg[s], op=ALU.mult),
                    (pu_t, sg_t[s]), (hT_t[c][blk],))
        for dc in range(KD):
            wdb, wdb_t = stream_weight(P, C, wd[:, dc * 128:(dc + 1) * 128], NFF, ringC, None)
            for blk in range(NB):
                pb = 4 + (dc * NB + blk) % 3
                po, po_t = ps[pb], ps_t[pb]
                r = nxr % 3
                nxr += 1
                c0 = tok0 + blk * 512
                P.dma("sp", xr[r], h_in[dc * 128:(dc + 1) * 128, c0:c0 + 512], (), (xr_t[r],))
                for c in range(NFF):
                    P.op("pe", lambda e, wdb=wdb, po=po, c=c, blk=blk: e.matmul(
                        po, lhsT=wdb[:, c * 128:(c + 1) * 128], rhs=hT[:, c, blk * 512:(blk + 1) * 512],
                        start=(c == 0), stop=(c == NFF - 1)),
                        (wdb_t, hT_t[c][blk]), (po_t,) if c in (0, NFF - 1) else ())
                P.op("dve", lambda e, po=po, r=r: e.scalar_tensor_tensor(
                    out=xr[r], in0=po, scalar=0.5, in1=xr[r], op0=ALU.mult, op1=ALU.add),
                    (po_t,), (xr_t[r],))
                P.dma("pool", h_out[dc * 128:(dc + 1) * 128, c0:c0 + 512], xr[r], (xr_t[r],), ())
    A.release()
    P.barrier()


def setup_common(nc, P):
    C = Ctx()
    C.nc = nc
    C.A = Arena(nc, 206 * 1024)
    C.dbg = None
    C.ps = []
    C.ps_t = toks(8, "ps")
    for i in range(8):
        t = nc.alloc_psum_tensor(f"ps{i}", [128, 512], F32)
        C.ps.append(t[:, :])
    A = C.A
    C.const_t = Tok("const")
    C.ones_bf = A.alloc(128, BF16)
    P.op("dve", lambda e: e.memset(C.ones_bf, 1.0), (), (C.const_t,))
    return C


def build_ffn_test(ntok):
    nc = bass.Bass("TRN2", target_bir_lowering=False)
    xT = nc.dram_tensor("xT", [D, ntok], F32, kind="ExternalInput").ap()
    g = nc.dram_tensor("g", [D], F32, kind="ExternalInput").ap()
    wg = nc.dram_tensor("wg", [D, DFF], F32, kind="ExternalInput").ap()
    wu = nc.dram_tensor("wu", [D, DFF], F32, kind="ExternalInput").ap()
    wd = nc.dram_tensor("wd", [DFF, D], F32, kind="ExternalInput").ap()
    oT = nc.dram_tensor("oT", [D, ntok], F32, kind="ExternalOutput").ap()
    P = Prog(nc)
    C = setup_common(nc, P)
    ffn_phase(P, C, xT, oT, g, wg, wu, wd, ntok)
    P.finish()
    P.emit()
    return nc, P


def make_ring(A, nst=3, nwb=3, wb_cols=16 * 128):
    return dict(st=[A.alloc(2048, F32) for _ in range(nst)], st_t=toks(nst, "st"),
                wb=[A.alloc(wb_cols, BF16) for _ in range(nwb)], wb_t=toks(nwb, "wb"), ns=0, nb=0)


def stream_w(P, C, w_cols, K, ncols, ring):
    i = ring["nb"] % len(ring["wb"])
    ring["nb"] += 1
    wb, wb_t = ring["wb"][i], ring["wb_t"][i]
    per = 2048 // ncols
    k0 = 0
    while k0 < K:
        kk = min(per, K - k0)
        s = ring["ns"] % len(ring["st"])
        ring["ns"] += 1
        st, st_t = ring["st"][s], ring["st_t"][s]
        P.dma("sp", st[:, 0:kk * ncols].rearrange("p (k c) -> p k c", k=kk),
              w_cols[k0 * 128:(k0 + kk) * 128, :].rearrange("(k p) c -> p k c", p=128),
              (), (st_t,))
        cast_op(P, ring["ns"], wb[:, k0 * ncols:(k0 + kk) * ncols], st[:, 0:kk * ncols], (st_t,), (wb_t,))
        k0 += kk
    return wb, wb_t


def load_rep(P, C, dram_vec, n, tok):
    t = C.A.alloc(n, F32)
    P.dma("sp", t, dram_vec.partition_broadcast(128), (), (tok,))
    return t


def out_proj_phase(P, C, yT, w_out, h_in, h_out, ntok):
    A, ps, ps_t = C.A, C.ps, C.ps_t
    T = 1024
    NB = T // 512
    A.mark()
    yb = A.alloc(KD * T, BF16).rearrange("p (k t) -> p k t", k=KD)
    yb_t = Tok("yb")
    ring = make_ring(A)
    xr = [A.alloc(512, F32) for _ in range(3)]
    xr_t = toks(3, "xr")
    nxr = 0
    for tile in range(ntok // T):
        tok0 = tile * T
        for k4 in range(4):
            P.dma("sp", yb[:, k4 * 4:(k4 + 1) * 4, :],
                  yT[k4 * 512:(k4 + 1) * 512, tok0:tok0 + T].rearrange("(k p) t -> p k t", p=128),
                  (C.yT_t,), (yb_t,))
        for dc in range(KD):
            wb, wb_t = stream_w(P, C, w_out[:, dc * 128:(dc + 1) * 128], KD, 128, ring)
            for blk in range(NB):
                pb = 4 + (dc * NB + blk) % 3
                po, po_t = ps[pb], ps_t[pb]
                r = nxr % 3
                nxr += 1
                c0 = tok0 + blk * 512
                P.dma("sp", xr[r], h_in[dc * 128:(dc + 1) * 128, c0:c0 + 512], (), (xr_t[r],))
                for k in range(KD):
                    P.op("pe", lambda e, wb=wb, po=po, k=k, blk=blk: e.matmul(
                        po, lhsT=wb[:, k * 128:(k + 1) * 128], rhs=yb[:, k, blk * 512:(blk + 1) * 512],
                        start=(k == 0), stop=(k == KD - 1)),
                        (wb_t, yb_t), (po_t,) if k in (0, KD - 1) else ())
                P.op("dve", lambda e, po=po, r=r: e.tensor_tensor(
                    out=xr[r], in0=po, in1=xr[r], op=ALU.add), (po_t,), (xr_t[r],))
                P.dma("pool", h_out[dc * 128:(dc + 1) * 128, c0:c0 + 512], xr[r], (xr_t[r],), ())
    A.release()
    P.barrier()


LAMBDA_INIT1 = 0.8 - 0.6 * math.exp(-0.3 * 1)
ATT_SCALE = 128 ** -0.5


def attn_phase(P, C, h_in, gain_dram, w_qkv, lq1, lk1, lq2, lk2, subln, cst, yT, ntok):
    nc, A, ps, ps_t = C.nc, C.A, C.ps, C.ps_t
    A.mark()
    ct = Tok("attc")
    gain, gain_t = load_gain(P, C, gain_dram)
    cosf = A.alloc(SEQ, F32)
    sinf = A.alloc(SEQ, F32)
    P.dma("sp", cosf, cst["att_cos"], (), (ct,))
    P.dma("sp", sinf, cst["att_sin"], (), (ct,))
    pm32 = A.alloc(128, F32)
    P.dma("sp", pm32, cst["att_pm"], (), (ct,))
    pm = A.alloc(128, BF16)
    P.op("dve", lambda e: e.tensor_copy(out=pm, in_=pm32), (ct,), (ct,))
    id32 = A.alloc(128, F32)
    P.dma("sp", id32, cst["ident"], (), (ct,))
    ident = A.alloc(128, BF16)
    P.op("dve", lambda e: e.tensor_copy(out=ident, in_=id32), (ct,), (ct,))
    lt = Tok("lam")
    lv = [load_rep(P, C, v[0], 128, lt) for v in (lq1, lk1, lq2, lk2)]
    lsum = A.alloc(2, F32)
    junk = A.alloc(128, F32)
    for i in range(2):
        P.op("dve", lambda e, i=i: e.tensor_tensor(out=junk, in0=lv[2 * i], in1=lv[2 * i + 1], op=ALU.mult),
             (lt,), (lt,))
        P.op("dve", lambda e, i=i: e.reduce_sum(out=lsum[:, i:i + 1], in_=junk, axis=AX.X), (), (lt,))
    P.op("act", lambda e: e.activation(out=lsum, in_=lsum, func=AF.Exp), (), (lt,))
    neglam = A.alloc(1, F32)
    P.op("dve", lambda e: e.tensor_tensor(out=neglam, in0=lsum[:, 1:2], in1=lsum[:, 0:1], op=ALU.subtract),
         (), (lt,))
    P.op("dve", lambda e: e.tensor_scalar(out=neglam, in0=neglam, scalar1=-LAMBDA_INIT1, scalar2=None,
                                          op0=ALU.add), (), (lt,))
    sub_rep = load_rep(P, C, subln[0], 256, lt)
    P.op("dve", lambda e: e.tensor_scalar(out=sub_rep, in0=sub_rep, scalar1=1.0 - LAMBDA_INIT1, scalar2=None,
                                          op0=ALU.mult), (), (lt,))

    hnT = A.alloc(KD * SEQ, BF16).rearrange("p (k t) -> p k t", k=KD)
    hn_t = toks(SEQ // 512, "hn")
    ring = make_ring(A, nst=3, nwb=3, wb_cols=16 * 256)
    qk = [A.alloc(2 * SEQ, BF16).rearrange("p (c t) -> p c t", c=2) for _ in range(2)]
    qk_t = [[toks(4, "q0"), toks(4, "q1")], [toks(4, "k0"), toks(4, "k1")]]
    VW = 258
    v_sb = A.alloc(16 * VW, BF16).rearrange("p (b e) -> p b e", b=16)
    v_t = toks(16, "v")
    x_sb = [A.alloc(512, BF16) for _ in range(2)]
    x_t = toks(2, "x")
    t1 = [A.alloc(512, F32) for _ in range(2)]
    t1_t = toks(2, "t1")
    t2 = [A.alloc(512, F32) for _ in range(2)]
    t2_t = toks(2, "t2")
    pT = [A.alloc(256, BF16) for _ in range(3)]
    pT_t = toks(3, "pT")
    acc = [A.alloc(256, F32) for _ in range(2)]
    acc_t = toks(2, "acc")
    rs = A.alloc(8, F32)
    rs_t = Tok("rs")
    y_sb = [A.alloc(256, BF16) for _ in range(2)]
    y_t = toks(2, "y")
    yst = A.alloc(2 * SEQ, BF16).rearrange("p (j t) -> p j t", j=2)
    yst_t = Tok("yst")
    sq_junk = A.alloc(256, F32)
    P.op("pool", lambda e: e.memset(v_sb[:, :, 256:257], 1.0), (), tuple(v_t))
    nx = 0
    npt = 0
    for s in range(ntok // SEQ):
        s0 = s * SEQ
        norm_T(P, C, h_in, s0, SEQ, gain, gain_t, hnT, hn_t)
        for h in range(8):
            for qi in range(2):
                for comp in range(2):
                    col = qi * D + (2 * h + comp) * 128
                    wb, wb_t = stream_w(P, C, w_qkv[:, col:col + 128], KD, 128, ring)
                    for blk in range(4):
                        pp, pp_t = ps[blk % 2], ps_t[blk % 2]
                        for k in range(KD):
                            P.op("pe", lambda e, wb=wb, pp=pp, k=k, blk=blk: e.matmul(
                                pp, lhsT=wb[:, k * 128:(k + 1) * 128], rhs=hnT[:, k, blk * 512:(blk + 1) * 512],
                                start=(k == 0), stop=(k == KD - 1)),
                                (wb_t, hn_t[blk]), (pp_t,) if k in (0, KD - 1) else ())
                        b = nx % 2
                        nx += 1
                        P.op("act", lambda e, b=b, pp=pp: e.copy(out=x_sb[b], in_=pp), (pp_t,), (x_t[b],))
                        P.op("pe", lambda e, b=b: e.matmul(ps[2], lhsT=pm, rhs=x_sb[b], start=True, stop=True),
                             (x_t[b], ct), (ps_t[2],))
                        tsl = slice(blk * 512, (blk + 1) * 512)
                        P.op("dve", lambda e, b=b, tsl=tsl: e.tensor_tensor(
                            out=t1[b], in0=ps[2], in1=sinf[:, tsl], op=ALU.mult), (ps_t[2], ct), (t1_t[b],))
                        P.op("dve", lambda e, b=b, pp=pp, tsl=tsl: e.tensor_tensor(
                            out=t2[b], in0=pp, in1=cosf[:, tsl], op=ALU.mult), (pp_t, ct), (t2_t[b],))
                        P.op("pool", lambda e, b=b, qi=qi, comp=comp, tsl=tsl: e.tensor_tensor(
                            out=qk[qi][:, comp, tsl], in0=t1[b], in1=t2[b], op=ALU.add),
                            (t1_t[b], t2_t[b]), (qk_t[qi][comp][blk],))
            col = 2 * D + h * 256
            wv, wv_t = stream_w(P, C, w_qkv[:, col:col + 256], KD, 256, ring)
            if C.dbg is not None and h == 0 and s == 0:
                P.dma("pool", C.dbg["wv"], wv, (wv_t,), ())
            for tb in range(16):
                pp, pp_t = ps[tb % 2], ps_t[tb % 2]
                for k in range(KD):
                    P.op("pe", lambda e, pp=pp, k=k, tb=tb, wv=wv: e.matmul(
                        pp[:, 0:256], lhsT=hnT[:, k, tb * 128:(tb + 1) * 128], rhs=wv[:, k * 256:(k + 1) * 256],
                        start=(k == 0), stop=(k == KD - 1)),
                        (wv_t, hn_t[tb // 4]), (pp_t,) if k in (0, KD - 1) else ())
                P.op("act", lambda e, pp=pp, tb=tb: e.copy(out=v_sb[:, tb, 0:256], in_=pp[:, 0:256]),
                     (pp_t,), (v_t[tb],))
            if C.dbg is not None and h == 0 and s == 0:
                P.dma("pool", C.dbg["q"], qk[0], tuple(qk_t[0][0] + qk_t[0][1]), ())
                P.dma("pool", C.dbg["k"], qk[1], tuple(qk_t[1][0] + qk_t[1][1]), ())
                P.dma("pool", C.dbg["v"], v_sb, tuple(v_t), ())
                P.dma("pool", C.dbg["lam"], neglam, (lt,), ())
            for qr in range(8):
                for comp in range(2):
                    pv = [ps[5], ps[6]]
                    pv_t = [ps_t[5], ps_t[6]]
                    for jb in range(2 * qr + 2):
                        a0 = max(0, jb - 2 * qr)
                        x0 = a0 * 128
                        sb = 3 + jb % 2
                        P.op("pe", lambda e, sb=sb, jb=jb, x0=x0, comp=comp, qr=qr: e.matmul(
                            ps[sb][:, x0:256], lhsT=qk[1][:, comp, jb * 128:(jb + 1) * 128],
                            rhs=qk[0][:, comp, qr * 256 + x0:(qr + 1) * 256], start=True, stop=True),
                            (qk_t[1][comp][jb // 4], qk_t[0][comp][qr // 2]), (ps_t[sb],))
                        pi = npt % 3
                        npt += 1
                        P.op("act", lambda e, sb=sb, pi=pi, x0=x0: e.activation(
                            out=pT[pi][:, x0:256], in_=ps[sb][:, x0:256], func=AF.Exp, scale=ATT_SCALE),
                            (ps_t[sb],), (pT_t[pi],))
                        if jb >= 2 * qr:
                            P.op("pool", lambda e, pi=pi, x0=x0: e.memset(pT[pi][64:128, x0:x0 + 64], 0.0),
                                 (), (pT_t[pi],))
                        for a in range(a0, 2):
                            last = (jb == 2 * qr + a)
                            P.op("pe", lambda e, a=a, pi=pi, jb=jb, last=last: e.matmul(
                                pv[a][:, 0:257], lhsT=pT[pi][:, a * 128:(a + 1) * 128], rhs=v_sb[:, jb, 0:257],
                                start=(jb == 0), stop=last),
                                (pT_t[pi], v_t[jb]), (pv_t[a],) if (jb == 0 or last) else ())
                    for a in range(2):
                        c = comp * 2 + a
                        P.op("dve", lambda e, a=a, c=c: e.reciprocal(out=rs[:, c:c + 1], in_=pv[a][:, 256:257]),
                             (pv_t[a],), (rs_t,))
                        if comp == 0:
                            P.op("act", lambda e, a=a, c=c: e.activation(
                                out=acc[a], in_=pv[a][:, 0:256], func=AF.Copy, scale=rs[:, c:c + 1]),
                                (pv_t[a], rs_t), (acc_t[a],))
                        else:
                            P.op("dve", lambda e, c=c: e.tensor_tensor(
                                out=rs[:, c:c + 1], in0=rs[:, c:c + 1], in1=neglam, op=ALU.mult), (lt,), (rs_t,))
                            P.op("dve", lambda e, a=a, c=c: e.scalar_tensor_tensor(
                                out=acc[a], in0=pv[a][:, 0:256], scalar=rs[:, c:c + 1], in1=acc[a],
                                op0=ALU.mult, op1=ALU.add), (pv_t[a], rs_t), (acc_t[a],))
                for a in range(2):
                    c = 4 + a
                    P.op("act", lambda e, a=a, c=c: e.activation(
                        out=sq_junk, in_=acc[a], func=AF.Square, accum_out=rs[:, c:c + 1]),
                        (acc_t[a],), (rs_t,))
                    P.op("dve", lambda e, c=c: e.tensor_scalar(
                        out=rs[:, c:c + 1], in0=rs[:, c:c + 1], scalar1=1.0 / 256, scalar2=EPS,
                        op0=ALU.mult, op1=ALU.add), (), (rs_t,))
                    P.op("act", lambda e, c=c: e.activation(out=rs[:, c:c + 1], in_=rs[:, c:c + 1], func=AF.Sqrt),
                         (), (rs_t,))
                    P.op("dve", lambda e, c=c: e.reciprocal(out=rs[:, c:c + 1], in_=rs[:, c:c + 1]), (), (rs_t,))
                    P.op("dve", lambda e, a=a, c=c: e.scalar_tensor_tensor(
                        out=y_sb[a], in0=acc[a], scalar=rs[:, c:c + 1], in1=sub_rep, op0=ALU.mult, op1=ALU.mult),
                        (acc_t[a], rs_t, lt), (y_t[a],))
                    ptr = ps[2].bitcast(BF16)
                    for j in range(2):
                        P.op("pe", lambda e, a=a, j=j, ptr=ptr: e.transpose(
                            ptr[:, j * 128:(j + 1) * 128], y_sb[a][:, j * 128:(j + 1) * 128], ident),
                            (y_t[a], ct), (ps_t[2],))
                    t0 = (qr * 2 + a) * 128
                    P.op("dve", lambda e, ptr=ptr, t0=t0: e.tensor_copy(
                        out=yst[:, :, t0:t0 + 128], in_=ptr[:, 0:256].rearrange("p (j t) -> p j t", j=2)),
                        (ps_t[2],), (yst_t,))
            P.dma("pool", yT[h * 256:(h + 1) * 256, s0:s0 + SEQ].rearrange("(j p) t -> p j t", p=128), yst,
                  (yst_t,), (C.yT_t,))
    A.release()
    P.barrier()


def host_consts():
    c = {}
    t = np.arange(SEQ, dtype=np.float32)
    inv = (1.0 / (np.float32(500000.0) ** (np.arange(0, 32, 2, dtype=np.float32) / np.float32(32)))).astype(np.float32)
    ang = t[:, None] * inv[None, :]
    cos, sin = np.cos(ang).astype(np.float32), np.sin(ang).astype(np.float32)
    cf = np.ones((128, SEQ), np.float32)
    sf = np.zeros((128, SEQ), np.float32)
    cf[0:16] = cos.T
    cf[16:32] = cos.T
    sf[0:16] = sin.T
    sf[16:32] = sin.T
    c["att_cos"], c["att_sin"] = cf, sf
    pm = np.zeros((128, 128), np.float32)
    for m in range(16):
        pm[m + 16, m] = -1.0
        pm[m, m + 16] = 1.0
    c["att_pm"] = pm
    c["ident"] = np.eye(128, dtype=np.float32)
    return c


CONST_SHAPES = {"att_cos": [128, SEQ], "att_sin": [128, SEQ], "att_pm": [128, 128], "ident": [128, 128]}


def build_attn_test(ntok):
    nc = bass.Bass("TRN2", target_bir_lowering=False)
    hT = nc.dram_tensor("hT", [D, ntok], F32, kind="ExternalInput").ap()
    g = nc.dram_tensor("g", [D], F32, kind="ExternalInput").ap()
    wqkv = nc.dram_tensor("wqkv", [D, 3 * D], F32, kind="ExternalInput").ap()
    wo = nc.dram_tensor("wo", [D, D], F32, kind="ExternalInput").ap()
    lam = [nc.dram_tensor(n, [1, 128], F32, kind="ExternalInput").ap() for n in ("lq1", "lk1", "lq2", "lk2")]
    subln = nc.dram_tensor("subln", [1, 256], F32, kind="ExternalInput").ap()
    cst = {k: nc.dram_tensor(k, v, F32, kind="ExternalInput").ap() for k, v in CONST_SHAPES.items()}
    yT = nc.dram_tensor("yT", [D, ntok], BF16, kind="ExternalOutput").ap()
    oT = nc.dram_tensor("oT", [D, ntok], F32, kind="ExternalOutput").ap()
    P = Prog(nc)
    C = setup_common(nc, P)
    C.yT_t = Tok("yT")
    C.dbg = {"q": nc.dram_tensor("dq", [128, 2, SEQ], BF16, kind="ExternalOutput").ap(),
             "k": nc.dram_tensor("dk", [128, 2, SEQ], BF16, kind="ExternalOutput").ap(),
             "v": nc.dram_tensor("dv", [128, 16, 258], BF16, kind="ExternalOutput").ap(),
             "wv": nc.dram_tensor("dwv", [128, 4096], BF16, kind="ExternalOutput").ap(),
             "lam": nc.dram_tensor("dlam", [128, 1], F32, kind="ExternalOutput").ap()}
    attn_phase(P, C, hT, g, wqkv, lam[0], lam[1], lam[2], lam[3], subln, cst, yT, ntok)
    out_proj_phase(P, C, yT, wo, hT, oT, ntok)
    P.finish()
    P.emit()
    return nc, P


def TT(P, eng, out, in0, in1, op, r=(), w=()):
    return P.op(eng, lambda e: e.tensor_tensor(out=out, in0=in0, in1=in1, op=op), r, w)


def TS(P, eng, out, in0, s1, s2, op0, op1=None, r=(), w=()):
    if op1 is None:
        return P.op(eng, lambda e: e.tensor_scalar(out=out, in0=in0, scalar1=s1, scalar2=None, op0=op0), r, w)
    return P.op(eng, lambda e: e.tensor_scalar(out=out, in0=in0, scalar1=s1, scalar2=s2, op0=op0, op1=op1), r, w)


def STT(P, out, in0, scalar, in1, op0, op1, r=(), w=()):
    return P.op("dve", lambda e: e.scalar_tensor_tensor(out=out, in0=in0, scalar=scalar, in1=in1, op0=op0, op1=op1), r, w)


def ACTV(P, out, in_, func, r=(), w=(), **kw):
    return P.op("act", lambda e: e.activation(out=out, in_=in_, func=func, **kw), r, w)


def CP(P, eng, out, in_, r=(), w=()):
    if eng == "act":
        return P.op("act", lambda e: e.copy(out=out, in_=in_), r, w)
    return P.op(eng, lambda e: e.tensor_copy(out=out, in_=in_), r, w)


def MM(P, out, lhsT, rhs, start, stop, r=(), w=()):
    return P.op("pe", lambda e: e.matmul(out, lhsT=lhsT, rhs=rhs, start=start, stop=stop), r, w)


def TR(P, out, in_, ident, r=(), w=()):
    return P.op("pe", lambda e: e.transpose(out, in_, ident), r, w)


def MSET(P, eng, ap, val, r=(), w=()):
    return P.op(eng, lambda e: e.memset(ap, val), r, w)


def rstd_inplace(P, x, t):
    ACTV(P, x, x, AF.Sqrt, (), (t,))
    P.op("dve", lambda e: e.reciprocal(out=x, in_=x), (), (t,))


RET_GAMMA = [1.0 - 2.0 ** (-5.0 - h) for h in range(4)]


def mix0a_phase(P, C, h_in, gain_dram, w_in, cst, yT, uT_d, Ud, ntok):
    nc, A, ps, ps_t = C.nc, C.A, C.ps, C.ps_t
    A.mark()
    ct = Tok("m0c")
    gain, gain_t = load_gain(P, C, gain_dram)
    cosf = A.alloc(SEQ, F32)
    sinf = A.alloc(SEQ, F32)
    P.dma("sp", cosf, cst["ret_cos"], (), (ct,))
    P.dma("sp", sinf, cst["ret_sin"], (), (ct,))
    id32 = A.alloc(128, F32)
    P.dma("sp", id32, cst["ident"], (), (ct,))
    ident = A.alloc(128, BF16)
    CP(P, "dve", ident, id32, (ct,), (ct,))
    tab = A.alloc(4 * 384, F32).rearrange("p (h y) -> p h y", h=4)
    P.dma("sp", tab, cst["ret_tab"].rearrange("h p y -> p h y"), (), (ct,))
    hnT = A.alloc(KD * SEQ, BF16).rearrange("p (k t) -> p k t", k=KD)
    hn_t = toks(SEQ // 512, "hn")
    ring = make_ring(A, nst=3, nwb=2, wb_cols=16 * 256)
    qk = [A.alloc(2 * SEQ, BF16).rearrange("p (c t) -> p c t", c=2) for _ in range(2)]
    qk_t = [toks(4, "rq"), toks(4, "rk")]
    sgT = A.alloc(2 * SEQ, BF16).rearrange("p (c t) -> p c t", c=2)
    sg_t = toks(4, "sg")
    v_sb = A.alloc(16 * 256, BF16).rearrange("p (b e) -> p b e", b=16)
    v_t = toks(16, "rv")
    tmp = [A.alloc(512, F32) for _ in range(4)]
    tmp_t = toks(4, "tmp")
    pT = [A.alloc(256, BF16) for _ in range(3)]
    pT_t = toks(3, "rpT")
    rs = A.alloc(4, F32)
    rs_t = Tok("rrs")
    sq_junk = A.alloc(256, F32)
    y_sb = [A.alloc(256, BF16) for _ in range(2)]
    y_t = toks(2, "ry")
    yst = A.alloc(2 * SEQ, BF16).rearrange("p (j t) -> p j t", j=2)
    yst_t = Tok("ryst")
    u_sb = [A.alloc(SEQ, BF16) for _ in range(1)]
    u_t = toks(2, "u")
    up_sb = [A.alloc(SEQ, BF16).rearrange("p (j m) -> p j m", j=8) for _ in range(1)]
    up_t = toks(2, "up")
    npt = 0
    for s in range(ntok // SEQ):
        s0 = s * SEQ
        norm_T(P, C, h_in, s0, SEQ, gain, gain_t, hnT, hn_t)
        for a in range(8):
            if getattr(C, "skip_u", False):
                break
            col = 4096 + a * 128
            wb, wb_t = stream_w(P, C, w_in[:, col:col + 128], KD, 128, ring)
            ub = 0
            for blk in range(4):
                pp, pp_t = ps[blk % 2], ps_t[blk % 2]
                for k in range(KD):
                    MM(P, pp, wb[:, k * 128:(k + 1) * 128], hnT[:, k, blk * 512:(blk + 1) * 512], k == 0, k == KD - 1,
                       (wb_t, hn_t[blk]), (pp_t,) if k in (0, KD - 1) else ())
                CP(P, "act", u_sb[ub][:, blk * 512:(blk + 1) * 512], pp, (pp_t,), (u_t[ub],))
                CP(P, "dve", up_sb[ub][:, :, blk * 64:(blk + 1) * 64],
                   u_sb[ub][:, blk * 512:(blk + 1) * 512].rearrange("p (m j) -> p j m", j=8),
                   (u_t[ub],), (up_t[ub],))
            P.dma("pool", uT_d[a * 128:(a + 1) * 128, s0:s0 + SEQ], u_sb[ub], (u_t[ub],), (C.uT_t,))
            for gg in range(8):
                if getattr(C, "skip_ud", False):
                    break
                P.dma("pool", Ud[a * 8 + gg, :, :, s * 256:(s + 1) * 256].rearrange("j p m -> p j m"),
                      up_sb[ub][gg * 16:(gg + 1) * 16, :, :], (up_t[ub],), (C.Ud_t,))
        for h in range(4):
            if getattr(C, "skip_ret", False):
                break
            lng = math.log(RET_GAMMA[h])
            for qi in range(2):
                wb, wb_t = stream_w(P, C, w_in[:, qi * 1024 + h * 256: qi * 1024 + (h + 1) * 256], KD, 256, ring)
                for blk in range(4):
                    pa, pa_t = ps[0], ps_t[0]
                    pb, pb_t = ps[1], ps_t[1]
                    for (pp, pp_t, j) in ((pa, pa_t, 0), (pb, pb_t, 1)):
                        for k in range(KD):
                            MM(P, pp, wb[:, k * 256 + j * 128:k * 256 + (j + 1) * 128],
                               hnT[:, k, blk * 512:(blk + 1) * 512], k == 0, k == KD - 1,
                               (wb_t, hn_t[blk]), (pp_t,) if k in (0, KD - 1) else ())
                    tsl = slice(blk * 512, (blk + 1) * 512)
                    TT(P, "dve", tmp[0], pa, cosf[:, tsl], ALU.mult, (pa_t, ct), (tmp_t[0],))
                    TT(P, "dve", tmp[1], pb, sinf[:, tsl], ALU.mult, (pb_t, ct), (tmp_t[1],))
                    TT(P, "dve", tmp[2], pb, cosf[:, tsl], ALU.mult, (pb_t, ct), (tmp_t[2],))
                    TT(P, "dve", tmp[3], pa, sinf[:, tsl], ALU.mult, (pa_t, ct), (tmp_t[3],))
                    TT(P, "pool", qk[qi][:, 0, tsl], tmp[0], tmp[1], ALU.subtract, (tmp_t[0], tmp_t[1]), (qk_t[qi][blk],))
                    TT(P, "pool", qk[qi][:, 1, tsl], tmp[2], tmp[3], ALU.add, (tmp_t[2], tmp_t[3]), (qk_t[qi][blk],))
            wb, wb_t = stream_w(P, C, w_in[:, 3072 + h * 256: 3072 + (h + 1) * 256], KD, 256, ring)
            for blk in range(4):
                for j in range(2):
                    pp, pp_t = ps[j], ps_t[j]
                    for k in range(KD):
                        MM(P, pp, wb[:, k * 256 + j * 128:k * 256 + (j + 1) * 128],
                           hnT[:, k, blk * 512:(blk + 1) * 512], k == 0, k == KD - 1,
                           (wb_t, hn_t[blk]), (pp_t,) if k in (0, KD - 1) else ())
                    ACTV(P, sgT[:, j, blk * 512:(blk + 1) * 512], pp, AF.Silu, (pp_t,), (sg_t[blk],))
            wv, wv_t = stream_w(P, C, w_in[:, 2048 + h * 256: 2048 + (h + 1) * 256], KD, 256, ring)
            for tb in range(16):
                pp, pp_t = ps[tb % 2], ps_t[tb % 2]
                for k in range(KD):
                    MM(P, pp[:, 0:256], hnT[:, k, tb * 128:(tb + 1) * 128], wv[:, k * 256:(k + 1) * 256],
                       k == 0, k == KD - 1, (wv_t, hn_t[tb // 4]), (pp_t,) if k in (0, KD - 1) else ())
                CP(P, "act", v_sb[:, tb, :], pp[:, 0:256], (pp_t,), (v_t[tb],))
            for qr in range(8):
                pv = [ps[5], ps[6]]
                pv_t = [ps_t[5], ps_t[6]]
                for jb in range(2 * qr + 2):
                    a0 = max(0, jb - 2 * qr)
                    x0 = a0 * 128
                    sb = 3 + jb % 2
                    for j in range(2):
                        MM(P, ps[sb][:, x0:256], qk[1][:, j, jb * 128:(jb + 1) * 128],
                           qk[0][:, j, qr * 256 + x0:(qr + 1) * 256], j == 0, j == 1,
                           (qk_t[1][jb // 4], qk_t[0][qr // 2]), (ps_t[sb],))
                    pi = npt % 3
                    npt += 1
                    d0 = 2 * qr - jb
                    if d0 >= 1:
                        sc = (1.0 / 16.0) * math.exp(lng * 128 * (d0 - 1))
                        y0 = 128
                    elif d0 == 0:
                        sc, y0 = 1.0 / 16.0, 0
                    else:
                        sc, y0 = 1.0 / 16.0, 0
                    STT(P, pT[pi][:, x0:256], ps[sb][:, x0:256], sc, tab[:, h, y0:y0 + 256 - x0], ALU.mult, ALU.mult,
                        (ps_t[sb], ct), (pT_t[pi],))
                    for a in range(a0, 2):
                        last = (jb == 2 * qr + a)
                        MM(P, pv[a][:, 0:256], pT[pi][:, a * 128:(a + 1) * 128], v_sb[:, jb, :], jb == 0, last,
                           (pT_t[pi], v_t[jb]), (pv_t[a],) if (jb == 0 or last) else ())
                for a in range(2):
                    ACTV(P, sq_junk, pv[a][:, 0:256], AF.Square, (pv_t[a],), (rs_t,), accum_out=rs[:, a:a + 1])
                    TS(P, "dve", rs[:, a:a + 1], rs[:, a:a + 1], 1.0 / 256, EPS, ALU.mult, ALU.add, (), (rs_t,))
                    rstd_inplace(P, rs[:, a:a + 1], rs_t)
                    ACTV(P, y_sb[a], pv[a][:, 0:256], AF.Copy, (pv_t[a], rs_t), (y_t[a],), scale=rs[:, a:a + 1])
                    ptr = ps[2].bitcast(BF16)
                    for j in range(2):
                        TR(P, ptr[:, j * 128:(j + 1) * 128], y_sb[a][:, j * 128:(j + 1) * 128], ident,
                           (y_t[a], ct), (ps_t[2],))
                    t0 = (qr * 2 + a) * 128
                    TT(P, "dve", yst[:, :, t0:t0 + 128], ptr[:, 0:256].rearrange("p (j t) -> p j t", j=2),
                       sgT[:, :, t0:t0 + 128], ALU.mult, (ps_t[2], sg_t[t0 // 512]), (yst_t,))
            P.dma("pool", yT[h * 256:(h + 1) * 256, s0:s0 + SEQ].rearrange("(j p) t -> p j t", p=128), yst,
                  (yst_t,), (C.yT_t,))
    A.release()
    P.barrier()


def host_consts_ret(c):
    t = np.arange(SEQ, dtype=np.float32)
    inv = (1.0 / (np.float32(10000.0) ** (np.arange(0, 256, 2, dtype=np.float32) / np.float32(256)))).astype(np.float32)
    ang = t[:, None] * inv[None, :]
    c["ret_cos"] = np.ascontiguousarray(np.cos(ang).astype(np.float32).T)
    c["ret_sin"] = np.ascontiguousarray(np.sin(ang).astype(np.float32).T)
    tab = np.zeros((4, 128, 384), np.float32)
    jj = np.arange(128)[:, None].astype(np.float64)
    for h in range(4):
        lg = math.log(RET_GAMMA[h])
        i = np.arange(128)[None, :].astype(np.float64)
        dg = np.exp(lg * np.abs(i - jj))
        dg[64:128, 0:64] = 0.0
        tab[h, :, 0:128] = dg
        y = np.arange(128, 384)[None, :].astype(np.float64)
        tab[h, :, 128:384] = np.exp(lg * (y - jj))
    c["ret_tab"] = tab
    return c


CONST_SHAPES.update({"ret_cos": [128, SEQ], "ret_sin": [128, SEQ], "ret_tab": [4, 128, 384]})


NEA, NEC = 71, 65
NE = NEA + NEC
GB = 4
S5_BARRIERS = False
PI = math.pi


def s5_powers(P, C, W, th, rho, ex, g0, tk):
    shp = [64, GB, NE]
    thb = th[:, g0:g0 + GB].unsqueeze(2).to_broadcast(shp)
    rhb = rho[:, g0:g0 + GB].unsqueeze(2).to_broadcast(shp)
    exb = ex.unsqueeze(1).to_broadcast(shp)
    ang, r, fx, kf, ki, Pr, Pi, mag = W["ang"], W["r"], W["fx"], W["kf"], W["ki"], W["Pr"], W["Pi"], W["mag"]
    TT(P, "dve", ang, thb, exb, ALU.mult, (tk,), (tk,))
    TT(P, "pool", mag, rhb, exb, ALU.mult, (tk,), (tk,))
    ACTV(P, mag, mag, AF.Exp, (), (tk,))
    for (dst, off) in ((Pi, 0.0), (Pr, PI / 2)):
        if off != 0.0:
            TS(P, "dve", ang, ang, off, None, ALU.add, None, (), (tk,))
        TS(P, "dve", r, ang, 1.0 / (2 * PI), 32.5, ALU.mult, ALU.add, (), (tk,))
        CP(P, "dve", ki, r, (), (tk,))
        CP(P, "dve", kf, ki, (), (tk,))
        TS(P, "dve", kf, kf, -32.0, -2 * PI, ALU.add, ALU.mult, (), (tk,))
        TT(P, "dve", r, kf, ang, ALU.add, (), (tk,))
        TS(P, "dve", fx, r, -PI, 2 * PI, ALU.is_lt, ALU.mult, (), (tk,))
        TT(P, "dve", r, r, fx, ALU.add, (), (tk,))
        TS(P, "dve", fx, r, PI, -2 * PI, ALU.is_gt, ALU.mult, (), (tk,))
        TT(P, "dve", r, r, fx, ALU.add, (), (tk,))
        TS(P, "dve", r, r, -3.141592, 3.141592, ALU.max, ALU.min, (), (tk,))
        ACTV(P, r, r, AF.Sin, (), (tk,))
        TT(P, "dve", dst, r, mag, ALU.mult, (), (tk,))


def s5_phase(P, C, prm, cst, Ud, Yd):
    nc, A, ps, ps_t = C.nc, C.A, C.ps, C.ps_t
    A.mark()
    tk = Tok("s5p")
    N = 64

    def al(cols, dt=F32):
        return A.alloc(cols, dt)[0:64]

    lr, li, lst = al(64), al(64), al(64)
    for q4_ in range(4):
        gs = slice(q4_ * 16, (q4_ + 1) * 16)
        P.dma("sp", lr[:, gs], prm["lam_re"][0][gs, :].rearrange("g n -> n g"), (), (tk,), allow_slow_non_contiguous=True)
        P.dma("sp", li[:, gs], prm["lam_im"][0][gs, :].rearrange("g n -> n g"), (), (tk,), allow_slow_non_contiguous=True)
    P.dma("sp", lst, prm["log_step"][0].partition_broadcast(64), (), (tk,))
    ex = al(NE)
    P.dma("sp", ex, cst["s5_ex"][0].partition_broadcast(64), (), (tk,))
    id32 = A.alloc(128, F32)
    P.dma("sp", id32, cst["ident"], (), (tk,))
    ident = A.alloc(128, BF16)
    CP(P, "dve", ident, id32, (tk,), (tk,))
    mask0 = A.alloc(128, F32)
    P.dma("sp", mask0, cst["s5_mask"], (), (tk,))
    ACTV(P, lst, lst, AF.Exp, (tk,), (tk,))
    rho, th = al(64), al(64)
    TT(P, "dve", rho, lr, lst, ALU.mult, (), (tk,))
    TT(P, "dve", th, li, lst, ALU.mult, (), (tk,))
    bbr = al(1024).rearrange("p (g q) -> p g q", g=64)
    bbi = al(1024).rearrange("p (g q) -> p g q", g=64)
    cr = al(1024).rearrange("p (g q) -> p g q", g=64)
    ci = al(1024).rearrange("p (g q) -> p g q", g=64)
    A.mark()
    cn = A.alloc(8 * 64, F32).rearrange("p (t n) -> p t n", t=8)
    for (src, dst) in ((prm["c_re"], cr), (prm["c_im"], ci)):
        P.dma("sp", cn, src[0].rearrange("(t g) p n -> (g p) t n", t=8), (), (tk,))
        for tt in range(8):
            TR(P, ps[0][0:64, 0:128], cn[:, tt, :], id32, (tk,), (ps_t[0],))
            CP(P, "dve", dst[:, tt * 8:(tt + 1) * 8, :], ps[0][0:64, 0:128].rearrange("p (g q) -> p g q", g=8),
               (ps_t[0],), (tk,))
    A.release()
    W = {k: al(GB * NE).rearrange("p (g e) -> p g e", g=GB) for k in ("ang", "r", "fx", "kf", "Pr", "Pi", "mag")}
    W["ki"] = al(GB * NE, I32).rearrange("p (g e) -> p g e", g=GB)
    a1r, a1i, a64r, a64i = al(64), al(64), al(64), al(64)
    for gb in range(64 // GB):
        s5_powers(P, C, W, th, rho, ex, gb * GB, tk)
        CP(P, "dve", a1r[:, gb * GB:(gb + 1) * GB], W["Pr"][:, :, NEA + 1], (), (tk,))
        CP(P, "dve", a1i[:, gb * GB:(gb + 1) * GB], W["Pi"][:, :, NEA + 1], (), (tk,))
        CP(P, "dve", a64r[:, gb * GB:(gb + 1) * GB], W["Pr"][:, :, NEA + 64], (), (tk,))
        CP(P, "dve", a64i[:, gb * GB:(gb + 1) * GB], W["Pi"][:, :, NEA + 64], (), (tk,))
        if C.dbg is not None and gb == 0:
            P.dma("pool", C.dbg["pr"], W["Pr"], (tk,), ())
            P.dma("pool", C.dbg["pi"], W["Pi"], (tk,), ())
    den, t1, t2, fr, fi = al(64), al(64), al(64), al(64), al(64)
    TT(P, "dve", den, lr, lr, ALU.mult, (), (tk,))
    TT(P, "dve", t1, li, li, ALU.mult, (), (tk,))
    TT(P, "dve", den, den, t1, ALU.add, (), (tk,))
    P.op("dve", lambda e: e.reciprocal(out=den, in_=den), (), (tk,))
    TS(P, "dve", a1r, a1r, -1.0, None, ALU.add, None, (), (tk,))
    TT(P, "dve", t1, a1r, lr, ALU.mult, (), (tk,))
    TT(P, "dve", t2, a1i, li, ALU.mult, (), (tk,))
    TT(P, "dve", fr, t1, t2, ALU.add, (), (tk,))
    TT(P, "dve", fr, fr, den, ALU.mult, (), (tk,))
    TT(P, "dve", t1, a1i, lr, ALU.mult, (), (tk,))
    TT(P, "dve", t2, a1r, li, ALU.mult, (), (tk,))
    TT(P, "dve", fi, t1, t2, ALU.subtract, (), (tk,))
    TT(P, "dve", fi, fi, den, ALU.mult, (), (tk,))
    A.mark()
    br = al(1024).rearrange("p (g q) -> p g q", g=64)
    bi = al(1024).rearrange("p (g q) -> p g q", g=64)
    for q8_ in range(8):
        gs = slice(q8_ * 8, (q8_ + 1) * 8)
        P.dma("sp", br[:, gs, :], prm["b_re"][0][gs].rearrange("g n p -> n g p"), (), (tk,))
        P.dma("sp", bi[:, gs, :], prm["b_im"][0][gs].rearrange("g n p -> n g p"), (), (tk,))
    u1 = al(1024).rearrange("p (g q) -> p g q", g=64)
    u2 = al(1024).rearrange("p (g q) -> p g q", g=64)
    frb = fr.unsqueeze(2).to_broadcast([64, 64, 16])
    fib = fi.unsqueeze(2).to_broadcast([64, 64, 16])
    TT(P, "dve", u1, br, frb, ALU.mult, (), (tk,))
    TT(P, "dve", u2, bi, fib, ALU.mult, (), (tk,))
    TT(P, "dve", bbr, u1, u2, ALU.subtract, (), (tk,))
    TT(P, "dve", u1, bi, frb, ALU.mult, (), (tk,))
    TT(P, "dve", u2, br, fib, ALU.mult, (), (tk,))
    TT(P, "dve", bbi, u1, u2, ALU.add, (), (tk,))
    A.release()
    if C.dbg is not None:
        P.dma("pool", C.dbg["fr"], fr, (tk,), ())
        P.dma("pool", C.dbg["fi"], fi, (tk,), ())
        P.dma("pool", C.dbg["bbr"], bbr, (tk,), ())
        P.dma("pool", C.dbg["cr"], cr, (tk,), ())

    Hr = al(64 * 64, BF16).rearrange("p (g c) -> p g c", g=64)
    Hi = al(64 * 64, BF16).rearrange("p (g c) -> p g c", g=64)
    H_t = Tok("H")
    A.mark()
    Sall = al(2 * 64 * 64).rearrange("p (r g c) -> p r g c", r=2, g=64)
    S_t = Tok("Sall")
    A.mark()
    U = A.alloc(64 * 512, BF16).rearrange("p (g m) -> p g m", g=64)
    U_t = Tok("U")
    for g8 in range(8):
        P.dma("sp", U[:, g8 * 8:(g8 + 1) * 8, :], Ud[g8 * 8:(g8 + 1) * 8].rearrange("g j p m -> (j p) g m"),
              (C.Ud_t,), (U_t,))
    tm = [al(NEA * 16).rearrange("p (e q) -> p e q", q=16) for _ in range(4)]
    tm_t = toks(4, "tm")
    ABr = [al(NEA * 16, BF16).rearrange("p (e q) -> p e q", q=16) for _ in range(2)]
    ABi = [al(NEA * 16, BF16).rearrange("p (e q) -> p e q", q=16) for _ in range(2)]
    AB_t = toks(2, "AB")
    Gsb = [A.alloc(16 * 64, BF16).rearrange("p (j n) -> p j n", j=16) for _ in range(2)]
    G_t = toks(2, "G")
    shpA = [64, NEA, 16]
    for g in range(64):
        gi = g % GB
        if gi == 0:
            s5_powers(P, C, W, th, rho, ex, g, tk)
        b = g % 2
        PrA = W["Pr"][:, gi, 0:NEA].unsqueeze(2).to_broadcast(shpA)
        PiA = W["Pi"][:, gi, 0:NEA].unsqueeze(2).to_broadcast(shpA)
        bR = bbr[:, g, :].unsqueeze(1).to_broadcast(shpA)
        bI = bbi[:, g, :].unsqueeze(1).to_broadcast(shpA)
        TT(P, "dve", tm[0], PrA, bR, ALU.mult, (tk,), (tm_t[0],))
        TT(P, "dve", tm[1], PiA, bI, ALU.mult, (tk,), (tm_t[1],))
        TT(P, "dve", tm[2], PrA, bI, ALU.mult, (tk,), (tm_t[2],))
        TT(P, "dve", tm[3], PiA, bR, ALU.mult, (tk,), (tm_t[3],))
        TT(P, "pool", ABr[b], tm[0], tm[1], ALU.subtract, (tm_t[0], tm_t[1]), (AB_t[b],))
        TT(P, "pool", ABi[b], tm[2], tm[3], ALU.add, (tm_t[2], tm_t[3]), (AB_t[b],))
        ptr = ps[2].bitcast(BF16)
        for jj in range(8):
            for ri, AB in enumerate((ABr[b], ABi[b])):
                TR(P, ptr[:, (2 * jj + ri) * 64:(2 * jj + ri + 1) * 64],
                   AB[:, 8 * jj:8 * jj + 8, :].rearrange("p e q -> p (e q)"), ident[0:64, 0:64],
                   (AB_t[b], tk), (ps_t[2],))
        CP(P, "act", Gsb[b], ptr[:, 0:1024].rearrange("p (j n) -> p j n", j=16), (ps_t[2],), (G_t[b],))
        pS, pS_t = ps[3 + g % 2], ps_t[3 + g % 2]
        for ri in range(2):
            for jj in range(8):
                MM(P, pS[0:64, ri * 64:(ri + 1) * 64], Gsb[b][:, 2 * jj + ri, :], U[:, g, jj::8], jj == 0, jj == 7,
                   (G_t[b], U_t), (pS_t,))
        CP(P, "dve", Sall[:, :, g, :], pS[0:64, 0:128].rearrange("p (r c) -> p r c", r=2), (pS_t,), (S_t,))
        if S5_BARRIERS:
            P.barrier()
    A.release()
    if C.dbg is not None:
        P.dma("pool", C.dbg["S"], Sall, (S_t,), ())
    S2 = al(2 * 64 * 64).rearrange("p (r g c) -> p r g c", r=2, g=64)
    q = [al(64 * 64).rearrange("p (g c) -> p g c", g=64) for _ in range(2)]
    cur, nxt = Sall, S2

    def v4(x, r, lo, hi):
        return x[:, r, :, :].rearrange("p g (s c) -> p g s c", s=2)[:, :, :, lo:hi]

    def q4(x, n):
        return x.rearrange("p g (s c) -> p g s c", s=2)[:, :, :, 0:n]

    Ar, Ai = a64r, a64i
    for lev in range(5):
        d = 1 << lev
        n = 32 - d
        shp = [64, 64, 2, n]
        Arb = Ar.unsqueeze(2).unsqueeze(3).to_broadcast(shp)
        Aib = Ai.unsqueeze(2).unsqueeze(3).to_broadcast(shp)
        CP(P, "dve", v4(nxt, 0, 0, d), v4(cur, 0, 0, d), (S_t, tk), (S_t,))
        CP(P, "dve", v4(nxt, 1, 0, d), v4(cur, 1, 0, d), (), (S_t,))
        TT(P, "dve", q4(q[0], n), v4(cur, 0, 0, n), Arb, ALU.mult, (), (S_t,))
        TT(P, "dve", q4(q[1], n), v4(cur, 1, 0, n), Aib, ALU.mult, (), (S_t,))
        TT(P, "dve", v4(nxt, 0, d, 32), v4(cur, 0, d, 32), q4(q[0], n), ALU.add, (), (S_t,))
        TT(P, "dve", v4(nxt, 0, d, 32), v4(nxt, 0, d, 32), q4(q[1], n), ALU.subtract, (), (S_t,))
        TT(P, "dve", q4(q[0], n), v4(cur, 1, 0, n), Arb, ALU.mult, (), (S_t,))
        TT(P, "dve", q4(q[1], n), v4(cur, 0, 0, n), Aib, ALU.mult, (), (S_t,))
        TT(P, "dve", v4(nxt, 1, d, 32), v4(cur, 1, d, 32), q4(q[0], n), ALU.add, (), (S_t,))
        TT(P, "dve", v4(nxt, 1, d, 32), v4(nxt, 1, d, 32), q4(q[1], n), ALU.add, (), (S_t,))
        cur, nxt = nxt, cur
        if lev < 4:
            TT(P, "dve", t1, Ar, Ar, ALU.mult, (), (tk,))
            TT(P, "dve", t2, Ai, Ai, ALU.mult, (), (tk,))
            TT(P, "dve", den, Ar, Ai, ALU.mult, (), (tk,))
            TT(P, "dve", Ar, t1, t2, ALU.subtract, (), (tk,))
            TS(P, "dve", Ai, den, 2.0, None, ALU.mult, None, (), (tk,))

    def h4(x, lo, hi):
        return x.rearrange("p g (s c) -> p g s c", s=2)[:, :, :, lo:hi]
    MSET(P, "pool", Hr, 0.0, (), (H_t,))
    MSET(P, "pool", Hi, 0.0, (), (H_t,))
    CP(P, "dve", h4(Hr, 1, 32), v4(cur, 0, 0, 31), (S_t,), (H_t,))
    TS(P, "dve", h4(Hi, 1, 32), v4(cur, 1, 0, 31), -1.0, None, ALU.mult, None, (S_t,), (H_t,))
    if C.dbg is not None:
        P.dma("pool", C.dbg["Hr"], Hr, (H_t,), ())
        P.dma("pool", C.dbg["Hi"], Hi, (H_t,), ())
    A.release()
    P.barrier()
    A.mark()
    U = A.alloc(64 * 512, BF16).rearrange("p (g m) -> p g m", g=64)
    U_t = Tok("U2")
    for g8 in range(8):
        P.dma("sp", U[:, g8 * 8:(g8 + 1) * 8, :], Ud[g8 * 8:(g8 + 1) * 8].rearrange("g j p m -> (j p) g m"),
              (C.Ud_t,), (U_t,))
    Yrb = [al(128, BF16) for _ in range(2)]
    Yib = [al(128, BF16) for _ in range(2)]
    Yp_t = toks(2, "Yp")
    ty = [al(128).rearrange("p (e q) -> p e q", q=16) for _ in range(4)]
    ty_t = toks(4, "ty")
    shpY = [64, 8, 16]
    shpC = [64, NEC, 16]
    tc_ = [al(NEC * 16).rearrange("p (e q) -> p e q", q=16) for _ in range(4)]
    tc_t = toks(4, "tc")
    CAr = [al(NEC * 16, BF16).rearrange("p (e q) -> p e q", q=16) for _ in range(2)]
    CAi = [al(NEC * 16, BF16).rearrange("p (e q) -> p e q", q=16) for _ in range(2)]
    CA_t = toks(2, "CA")
    Tsb = [A.alloc(8 * 128, BF16).rearrange("p (d c) -> p d c", d=8) for _ in range(2)]
    T_t = toks(2, "T")
    Ysb = [A.alloc(8 * 512, F32).rearrange("p (g m) -> p g m", g=8) for _ in range(2)]
    Yo_t = toks(2, "Yo")
    ytmp = [A.alloc(512, F32) for _ in range(2)]
    ytmp_t = toks(2, "ytmp")
    for g in range(64):
        gi = g % GB
        if gi == 0:
            s5_powers(P, C, W, th, rho, ex, g, tk)
        b = g % 2
        PrY = W["Pr"][:, gi, 63:71].unsqueeze(2).to_broadcast(shpY)
        PiY = W["Pi"][:, gi, 63:71].unsqueeze(2).to_broadcast(shpY)
        bR = bbr[:, g, :].unsqueeze(1).to_broadcast(shpY)
        bI = bbi[:, g, :].unsqueeze(1).to_broadcast(shpY)
        TT(P, "dve", ty[0], PrY, bR, ALU.mult, (tk,), (ty_t[0],))
        TT(P, "dve", ty[1], PiY, bI, ALU.mult, (tk,), (ty_t[1],))
        TT(P, "dve", ty[2], PrY, bI, ALU.mult, (tk,), (ty_t[2],))
        TT(P, "dve", ty[3], PiY, bR, ALU.mult, (tk,), (ty_t[3],))
        TT(P, "pool", Yrb[b].rearrange("p (e q) -> p e q", q=16), ty[0], ty[1], ALU.subtract,
           (ty_t[0], ty_t[1]), (Yp_t[b],))
        STT(P, Yib[b].rearrange("p (e q) -> p e q", q=16), ty[2], -1.0, ty[3], ALU.mult, ALU.subtract,
            (ty_t[2], ty_t[3]), (Yp_t[b],))
        PrC = W["Pr"][:, gi, NEA:NE].unsqueeze(2).to_broadcast(shpC)
        PiC = W["Pi"][:, gi, NEA:NE].unsqueeze(2).to_broadcast(shpC)
        cR = cr[:, g, :].unsqueeze(1).to_broadcast(shpC)
        cI = ci[:, g, :].unsqueeze(1).to_broadcast(shpC)
        TT(P, "dve", tc_[0], PrC, cR, ALU.mult, (tk,), (tc_t[0],))
        TT(P, "dve", tc_[1], PiC, cI, ALU.mult, (tk,), (tc_t[1],))
        TT(P, "dve", tc_[2], PrC, cI, ALU.mult, (tk,), (tc_t[2],))
        TT(P, "dve", tc_[3], PiC, cR, ALU.mult, (tk,), (tc_t[3],))
        TT(P, "pool", CAr[b], tc_[0], tc_[1], ALU.subtract, (tc_t[0], tc_t[1]), (CA_t[b],))
        TT(P, "pool", CAi[b], tc_[2], tc_[3], ALU.add, (tc_t[2], tc_t[3]), (CA_t[b],))
        for half in range(2):
            pT_, pT_t = ps[half], ps_t[half]
            MM(P, pT_, Yrb[b], CAr[b][:, 32 * half:32 * half + 32, :].rearrange("p e q -> p (e q)"),
               True, False, (Yp_t[b], CA_t[b]), (pT_t,))
            MM(P, pT_, Yib[b], CAi[b][:, 32 * half:32 * half + 32, :].rearrange("p e q -> p (e q)"),
               False, True, (Yp_t[b], CA_t[b]), (pT_t,))
            CP(P, "act", Tsb[b][:, 4 * half:4 * half + 4, :], pT_.rearrange("p (d c) -> p d c", d=4),
               (pT_t,), (T_t[b],))
        TT(P, "dve", Tsb[b][:, 0, :], Tsb[b][:, 0, :], mask0, ALU.mult, (tk,), (T_t[b],))
        pY, pY_t = ps[4 + g % 2], ps_t[4 + g % 2]
        for jji in range(8):
            o = pY[:, jji * 64:(jji + 1) * 64]
            MM(P, o, CAr[b][:, 8 * jji + 1:8 * jji + 9, :].rearrange("p e q -> p (e q)"), Hr[:, g, :], True, False,
               (CA_t[b], H_t), (pY_t,))
            MM(P, o, CAi[b][:, 8 * jji + 1:8 * jji + 9, :].rearrange("p e q -> p (e q)"), Hi[:, g, :], False, False,
               (CA_t[b], H_t), ())
            for jjo in range(jji + 1):
                MM(P, o, Tsb[b][:, jji - jjo, :], U[:, g, jjo::8], False, jjo == jji,
                   (T_t[b], U_t), (pY_t,) if (jji == 7 and jjo == 7) else ())
        yb = (g // 8) % 2
        CP(P, "act", ytmp[b], pY, (pY_t,), (ytmp_t[b],))
        CP(P, "pool", Ysb[yb][:, g % 8, :].rearrange("p (c j) -> p j c", j=8),
           ytmp[b].rearrange("p (j c) -> p j c", j=8), (ytmp_t[b],), (Yo_t[yb],))
        if S5_BARRIERS:
            P.barrier()
        if g % 8 == 7:
            g0 = g - 7
            P.dma("pool", Yd[g0:g0 + 8].rearrange("g i p m -> (i p) g m"), Ysb[yb], (Yo_t[yb],), (C.Yd_t,))
    A.release()
    A.release()
    P.barrier()


def s5_post_phase(P, C, prm, Yd, uT_d, yT, ntok):
    nc, A, ps, ps_t = C.nc, C.A, C.ps, C.ps_t
    A.mark()
    tk = Tok("s5q")
    dsk = A.alloc(8, F32)
    bgl = A.alloc(8, F32)
    P.dma("sp", dsk, prm["d"][0].rearrange("(a p) -> p a", p=128), (), (tk,), allow_slow_non_contiguous=True)
    P.dma("sp", bgl, prm["b_glu"][0].rearrange("(a p) -> p a", p=128), (), (tk,), allow_slow_non_contiguous=True)
    zT = A.alloc(8 * SEQ, BF16).rearrange("p (a t) -> p a t", a=8)
    z_t = toks(8, "z")
    ring = make_ring(A, nst=3, nwb=3, wb_cols=8 * 128)
    ysb = [A.alloc(SEQ, F32).rearrange("p (j m) -> p j m", j=8) for _ in range(2)]
    ys_t = toks(2, "ys")
    usb = [A.alloc(SEQ, BF16) for _ in range(2)]
    us_t = toks(2, "us")
    zp = [A.alloc(SEQ, F32) for _ in range(2)]
    zp_t = toks(2, "zp")
    w_ = [A.alloc(SEQ, F32) for _ in range(2)]
    w_t = toks(2, "w")
    gate = [A.alloc(512, BF16) for _ in range(2)]
    gate_t = toks(2, "gate")
    ybst = [A.alloc(SEQ, BF16) for _ in range(2)]
    yb_t = toks(2, "yb")
    ng = 0
    for s in range(ntok // SEQ):
        s0 = s * SEQ
        for a in range(8):
            b = a % 2
            for gg in range(8):
                P.dma("sp", ysb[b][gg * 16:(gg + 1) * 16, :, :],
                      Yd[a * 8 + gg, :, :, s * 256:(s + 1) * 256].rearrange("j p m -> p j m"), (C.Yd_t,), (ys_t[b],))
            P.dma("sp", usb[b], uT_d[a * 128:(a + 1) * 128, s0:s0 + SEQ], (C.uT_t,), (us_t[b],))
            STT(P, zp[b].rearrange("p (m j) -> p j m", j=8), usb[b].rearrange("p (m j) -> p j m", j=8),
                dsk[:, a:a + 1], ysb[b], ALU.mult, ALU.add, (us_t[b], ys_t[b], tk), (zp_t[b],))
            TT(P, "pool", w_[b], zp[b], zp[b], ALU.mult, (zp_t[b],), (w_t[b],))
            TS(P, "dve", w_[b], w_[b], 0.044715, 1.0, ALU.mult, ALU.add, (), (w_t[b],))
            TT(P, "pool", w_[b], w_[b], zp[b], ALU.mult, (zp_t[b],), (w_t[b],))
            ACTV(P, w_[b], w_[b], AF.Sigmoid, (), (w_t[b],), scale=1.5957691216057308)
            TT(P, "dve", zT[:, a, :], zp[b], w_[b], ALU.mult, (zp_t[b], w_t[b]), (z_t[a],))
        for oc in range(8):
            wb, wb_t = stream_w(P, C, prm["w_glu"][0][:, oc * 128:(oc + 1) * 128], 8, 128, ring)
            yb = oc % 2
            for blk in range(4):
                pp, pp_t = ps[blk % 2], ps_t[blk % 2]
                for k in range(8):
                    MM(P, pp, wb[:, k * 128:(k + 1) * 128], zT[:, k, blk * 512:(blk + 1) * 512], k == 0, k == 7,
                       (wb_t, z_t[k]), (pp_t,) if k in (0, 7) else ())
                gi = ng % 2
                ng += 1
                ACTV(P, gate[gi], pp, AF.Sigmoid, (pp_t, tk), (gate_t[gi],), bias=bgl[:, oc:oc + 1])
                TT(P, "dve", ybst[yb][:, blk * 512:(blk + 1) * 512], zT[:, oc, blk * 512:(blk + 1) * 512], gate[gi],
                   ALU.mult, (gate_t[gi], z_t[oc]), (yb_t[yb],))
            P.dma("pool", yT[1024 + oc * 128:1024 + (oc + 1) * 128, s0:s0 + SEQ], ybst[yb], (yb_t[yb],), (C.yT_t,))
    A.release()
    P.barrier()


def host_consts_s5(c):
    ex = np.concatenate([63.0 - np.arange(NEA), np.arange(NEC)]).astype(np.float32)
    c["s5_ex"] = ex[None, :]
    m = np.zeros((128, 128), np.float32)
    for j in range(8):
        for i in range(8):
            if i >= j:
                m[j * 16:(j + 1) * 16, i * 16:(i + 1) * 16] = 1.0
    c["s5_mask"] = m
    return c


CONST_SHAPES.update({"s5_ex": [1, NE], "s5_mask": [128, 128]})


def all_host_consts():
    c = host_consts()
    host_consts_ret(c)
    host_consts_s5(c)
    return c


def build_mix0_test(ntok, phases=(1, 1, 1)):
    nc = bass.Bass("TRN2", target_bir_lowering=False)
    hT = nc.dram_tensor("hT", [D, ntok], F32, kind="ExternalInput").ap()
    g = nc.dram_tensor("g", [D], F32, kind="ExternalInput").ap()
    w_in = nc.dram_tensor("w_in", [D, 5120], F32, kind="ExternalInput").ap()
    wo = nc.dram_tensor("wo", [D, D], F32, kind="ExternalInput").ap()
    prm = {}
    for n, shp in (("lam_re", [1, 64, 64]), ("lam_im", [1, 64, 64]), ("log_step", [1, 64]),
                   ("b_re", [1, 64, 64, 16]), ("b_im", [1, 64, 64, 16]), ("c_re", [1, 64, 16, 64]),
                   ("c_im", [1, 64, 16, 64]), ("d", [1, 1024]), ("w_glu", [1, 1024, 1024]), ("b_glu", [1, 1024])):
        prm[n] = nc.dram_tensor(n, shp, F32, kind="ExternalInput").ap()
    cst = {k: nc.dram_tensor(k, v, F32, kind="ExternalInput").ap() for k, v in CONST_SHAPES.items()
           if not k.startswith("att_")}
    yT = nc.dram_tensor("yT", [D, ntok], BF16, kind="ExternalOutput").ap()
    uT_d = nc.dram_tensor("uT_d", [1024, ntok], BF16, kind="ExternalOutput").ap()
    nm = ntok // 8
    Ud = nc.dram_tensor("Ud", [64, 8, 16, 512], BF16, kind="ExternalOutput").ap()
    Yd = nc.dram_tensor("Yd", [64, 8, 16, 512], F32, kind="ExternalOutput").ap()
    oT = nc.dram_tensor("oT", [D, ntok], F32, kind="ExternalOutput").ap()
    P = Prog(nc)
    C = setup_common(nc, P)
    C.yT_t, C.uT_t, C.Ud_t, C.Yd_t = Tok("yT"), Tok("uT"), Tok("Ud"), Tok("Yd")
    import os
    C.skip_ud = bool(os.environ.get("SKIP_UD"))
    C.skip_ret = bool(os.environ.get("SKIP_RET"))
    C.skip_u = bool(os.environ.get("SKIP_U"))
    def dd(n, shp, dt=F32):
        return nc.dram_tensor(n, shp, dt, kind="ExternalOutput").ap()
    C.dbg = {"pr": dd("d_pr", [64, GB, NE]), "pi": dd("d_pi", [64, GB, NE]), "fr": dd("d_fr", [64, 64]),
             "fi": dd("d_fi", [64, 64]), "bbr": dd("d_bbr", [64, 64, 16]), "cr": dd("d_cr", [64, 64, 16]),
             "S": dd("d_S", [64, 2, 64, 64]), "Hr": dd("d_Hr", [64, 64, 64], BF16), "Hi": dd("d_Hi", [64, 64, 64], BF16)}
    if phases[0]:
        mix0a_phase(P, C, hT, g, w_in, cst, yT, uT_d, Ud, ntok)
    if phases[1]:
        s5_phase(P, C, prm, cst, Ud, Yd)
    if phases[2]:
        s5_post_phase(P, C, prm, Yd, uT_d, yT, ntok)
    out_proj_phase(P, C, yT, wo, hT, oT, ntok)
    P.finish()
    P.emit()
    return nc, P


def final_norm_phase(P, C, h_in, gain_dram, outT, ntok):
    A, ps, ps_t = C.A, C.ps, C.ps_t
    A.mark()
    gain, gain_t = load_gain(P, C, gain_dram)
    xk = [A.alloc(2 * 512, F32) for _ in range(3)]
    xk_t = toks(3, "fxk")
    sq = [A.alloc(2 * 512, BF16) for _ in range(2)]
    sq_t = toks(2, "fsq")
    rstd = A.alloc(512, F32)
    rstd_t = Tok("frstd")
    ssq, ssq_t = ps[7][:, 0:512], ps_t[7]
    n = 0
    for blk in range(ntok // 512):
        c0 = blk * 512
        for k2 in range(KD // 2):
            b = n % 3
            n += 1
            P.dma("sp", xk[b].rearrange("p (k t) -> p k t", k=2),
                  h_in[k2 * 256:(k2 + 1) * 256, c0:c0 + 512].rearrange("(k p) t -> p k t", p=128), (), (xk_t[b],))
            s = k2 % 2
            ACTV(P, sq[s], xk[b], AF.Square, (xk_t[b],), (sq_t[s],))
            for j in range(2):
                k = k2 * 2 + j
                MM(P, ssq, C.ones_bf, sq[s][:, j * 512:(j + 1) * 512], k == 0, k == KD - 1,
                   (sq_t[s], C.const_t), (ssq_t,) if k in (0, KD - 1) else ())
        TS(P, "dve", rstd, ssq, 1.0 / D, EPS, ALU.mult, ALU.add, (ssq_t,), (rstd_t,))
        rstd_inplace(P, rstd, rstd_t)
        for k2 in range(KD // 2):
            b = n % 3
            n += 1
            P.dma("sp", xk[b].rearrange("p (k t) -> p k t", k=2),
                  h_in[k2 * 256:(k2 + 1) * 256, c0:c0 + 512].rearrange("(k p) t -> p k t", p=128), (), (xk_t[b],))
            for j in range(2):
                k = k2 * 2 + j
                STT(P, xk[b][:, j * 512:(j + 1) * 512], xk[b][:, j * 512:(j + 1) * 512], gain[:, k:k + 1], rstd,
                    ALU.mult, ALU.mult, (xk_t[b], rstd_t, gain_t), (xk_t[b],))
            P.dma("pool", outT[k2 * 256:(k2 + 1) * 256, c0:c0 + 512].rearrange("(k p) t -> p k t", p=128),
                  xk[b].rearrange("p (k t) -> p k t", k=2), (xk_t[b],), ())
    A.release()
    P.barrier()


IN_SHAPES = {
    "ffn_norm": [2, 2, D], "ffn_w_gate": [2, 2, D, DFF], "ffn_w_up": [2, 2, D, DFF], "ffn_w_down": [2, 2, DFF, D],
    "mix_norm": [2, D], "ab_w_in": [1, D, 5120], "ab_w_out": [1, D, D],
    "ssm_lambda_re": [1, 64, 64], "ssm_lambda_im": [1, 64, 64], "ssm_log_step": [1, 64],
    "ssm_b_re": [1, 64, 64, 16], "ssm_b_im": [1, 64, 64, 16], "ssm_c_re": [1, 64, 16, 64], "ssm_c_im": [1, 64, 16, 64],
    "ssm_d": [1, 1024], "ssm_w_glu": [1, 1024, 1024], "ssm_b_glu": [1, 1024],
    "c_w_qkv": [1, D, 3 * D], "c_w_out": [1, D, D], "c_lambda_q1": [1, 128], "c_lambda_k1": [1, 128],
    "c_lambda_q2": [1, 128], "c_lambda_k2": [1, 128], "c_subln": [1, 256], "final_norm": [D],
}
SCRATCH_KIND = "Internal"


def build_full(ntok=TOK):
    nc = bass.Bass("TRN2", target_bir_lowering=False)
    xT = nc.dram_tensor("xT", [D, ntok], F32, kind="ExternalInput").ap()
    w = {k: nc.dram_tensor(k, v, F32, kind="ExternalInput").ap() for k, v in IN_SHAPES.items()}
    cst = {k: nc.dram_tensor(k, v, F32, kind="ExternalInput").ap() for k, v in CONST_SHAPES.items()}
    outT = nc.dram_tensor("outT", [D, ntok], F32, kind="ExternalOutput").ap()
    hA = nc.dram_tensor("hA", [D, ntok], F32, kind=SCRATCH_KIND).ap()
    hB = nc.dram_tensor("hB", [D, ntok], F32, kind=SCRATCH_KIND).ap()
    yT = nc.dram_tensor("yT", [D, ntok], BF16, kind=SCRATCH_KIND).ap()
    uT_d = nc.dram_tensor("uT_d", [1024, ntok], BF16, kind=SCRATCH_KIND).ap()
    Ud = nc.dram_tensor("Ud", [64, 8, 16, 512], BF16, kind=SCRATCH_KIND).ap()
    Yd = nc.dram_tensor("Yd", [64, 8, 16, 512], F32, kind=SCRATCH_KIND).ap()
    P = Prog(nc)
    C = setup_common(nc, P)
    C.yT_t, C.uT_t, C.Ud_t, C.Yd_t = Tok("yT"), Tok("uT"), Tok("Ud"), Tok("Yd")
    prm = {"lam_re": w["ssm_lambda_re"], "lam_im": w["ssm_lambda_im"], "log_step": w["ssm_log_step"],
           "b_re": w["ssm_b_re"], "b_im": w["ssm_b_im"], "c_re": w["ssm_c_re"], "c_im": w["ssm_c_im"],
           "d": w["ssm_d"], "w_glu": w["ssm_w_glu"], "b_glu": w["ssm_b_glu"]}

    def ffn(l, j, src, dst):
        ffn_phase(P, C, src, dst, w["ffn_norm"][l, j], w["ffn_w_gate"][l, j], w["ffn_w_up"][l, j],
                  w["ffn_w_down"][l, j], ntok)

    ffn(0, 0, xT, hA)
    mix0a_phase(P, C, hA, w["mix_norm"][0], w["ab_w_in"][0], cst, yT, uT_d, Ud, ntok)
    s5_phase(P, C, prm, cst, Ud, Yd)
    s5_post_phase(P, C, prm, Yd, uT_d, yT, ntok)
    out_proj_phase(P, C, yT, w["ab_w_out"][0], hA, hB, ntok)
    ffn(0, 1, hB, hA)
    ffn(1, 0, hA, hB)
    attn_phase(P, C, hB, w["mix_norm"][1], w["c_w_qkv"][0], w["c_lambda_q1"], w["c_lambda_k1"], w["c_lambda_q2"],
               w["c_lambda_k2"], w["c_subln"], cst, yT, ntok)
    out_proj_phase(P, C, yT, w["c_w_out"][0], hB, hA, ntok)
    ffn(1, 1, hA, hB)
    final_norm_phase(P, C, hB, w["final_norm"], outT, ntok)
    P.finish()
    P.emit()
    return nc, P


_CACHE = {}


def kernel(**inputs):
    x = np.asarray(inputs["x"], dtype=np.float32)
    B = x.shape[0]
    per = B // NCORES
    if "nc" not in _CACHE:
        _CACHE["nc"] = build_full(per * SEQ)[0]
        _CACHE["cst"] = all_host_consts()
    nc = _CACHE["nc"]
    shared = {k: np.ascontiguousarray(np.asarray(inputs[k], dtype=np.float32)) for k in IN_SHAPES}
    shared.update(_CACHE["cst"])
    in_maps = []
    for c in range(NCORES):
        xc = x[c * per:(c + 1) * per].reshape(per * SEQ, D)
        m = dict(shared)
        m["xT"] = np.ascontiguousarray(xc.T)
        in_maps.append(m)
    res = run_bass_kernel_spmd(nc, in_maps, core_ids=list(range(NCORES)))
    out = np.empty((B, SEQ, D), np.float32)
    for c in range(NCORES):
        oT = np.asarray(res.results[c]["outT"])
        out[c * per:(c + 1) * per] = np.ascontiguousarray(oT.T).reshape(per, SEQ, D)
    return out
```

```python
import math
import contextlib
import numpy as np
import concourse.bass as bass
import concourse.mybir as mybir
from concourse.bass_utils import run_bass_kernel_spmd

F32 = mybir.dt.float32
BF16 = mybir.dt.bfloat16
I32 = mybir.dt.int32
U8 = mybir.dt.uint8
ALU = mybir.AluOpType
AF = mybir.ActivationFunctionType
AX = mybir.AxisListType

D = 2048
DFF = 5504
NFF = DFF // 128
KD = D // 128
SEQ = 2048
NCORES = 8
TOK = 2 * SEQ
EPS = 1e-6

COMPUTE = ("pe", "act", "dve", "pool")
ENG_ATTR = {"pe": "tensor", "act": "scalar", "dve": "vector", "pool": "gpsimd", "sp": "sync"}


class Tok:
    __slots__ = ("name", "w", "r")

    def __init__(self, name=""):
        self.name = name
        self.w = []
        self.r = []


def toks(n, name=""):
    return [Tok(f"{name}{i}") for i in range(n)]


class Op:
    __slots__ = ("eng", "fn", "deps", "dma", "signal", "sem", "val", "know")

    def __init__(self, eng, fn, dma):
        self.eng = eng
        self.fn = fn
        self.dma = dma
        self.deps = []
        self.signal = False
        self.sem = None
        self.val = 0
        self.know = None


class Prog:
    def __init__(self, nc):
        self.nc = nc
        self.ops = []
        self.last = {}
        self.pending = {}
        self.dma_ops = []
        self.n_dma_sems = {"sp": 30, "pool": 16, "act": 8}

    def op(self, eng, fn, reads=(), writes=(), dma=False):
        o = Op(eng, fn, dma)
        deps = []
        for t in reads:
            deps.extend(t.w)
        for t in writes:
            deps.extend(t.w)
            for x in t.r:
                if x.dma or dma or x.eng != eng:
                    deps.append(x)
        if eng in self.pending:
            deps.extend(self.pending.pop(eng))
        seen = set()
        for d in deps:
            if id(d) in seen:
                continue
            seen.add(id(d))
            if (not d.dma) and (not dma) and d.eng == "pe" and eng == "pe":
                continue
            o.deps.append(d)
        for t in reads:
            if not dma:
                t.r = [x for x in t.r if x.dma or x.eng != eng]
            t.r.append(o)
        for t in writes:
            t.w = [o]
            t.r = []
        self.ops.append(o)
        if dma:
            self.dma_ops.append(o)
        else:
            self.last[eng] = o
        return o

    def dma(self, q, out, in_, reads=(), writes=(), **kw):
        return self.op(q, lambda e: e.dma_start(out=out, in_=in_, **kw), reads, writes, dma=True)

    def barrier(self):
        deps = list(self.last.values()) + self.dma_ops
        self.dma_ops = []
        for e in ENG_ATTR:
            self.pending[e] = list(deps) + self.pending.get(e, [])

    def finish(self):
        self.barrier()
        self.op("sp", lambda e: e.nop(), (), ())

    def emit(self):
        nc = self.nc
        ops = self.ops
        with contextlib.ExitStack() as st:
            sems = {e: st.enter_context(nc.semaphore("s_" + e)) for e in COMPUTE}
            dsems = {q: [st.enter_context(nc.semaphore(f"d_{q}{i}")) for i in range(n)]
                     for q, n in self.n_dma_sems.items()}
            dcnt = {q: [0] * n for q, n in self.n_dma_sems.items()}
            dnext = {q: 0 for q in self.n_dma_sems}
            dprev = {q: [None] * n for q, n in self.n_dma_sems.items()}
            for o in ops:
                if o.dma:
                    q = o.eng
                    i = dnext[q]
                    dnext[q] = (i + 1) % len(dsems[q])
                    prev = dprev[q][i]
                    if prev is not None:
                        o.deps.append(prev)
                    dcnt[q][i] += 16
                    o.sem = dsems[q][i]
                    o.val = dcnt[q][i]
                    dprev[q][i] = o
            for o in ops:
                for d in o.deps:
                    d.signal = True
            cnt = {e: 0 for e in COMPUTE}
            for o in ops:
                if (not o.dma) and o.signal:
                    cnt[o.eng] += 1
                    o.sem = sems[o.eng]
                    o.val = cnt[o.eng]
            know = {e: {} for e in ENG_ATTR}
            plan = {e: [] for e in ENG_ATTR}
            nw = 0
            for o in ops:
                kn = know[o.eng]
                for d in o.deps:
                    key = id(d.sem)
                    if kn.get(key, 0) >= d.val:
                        continue
                    plan[o.eng].append(("w", d.sem, d.val))
                    nw += 1
                    if d.know is not None:
                        for k2, v2 in d.know.items():
                            if kn.get(k2, 0) < v2:
                                kn[k2] = v2
                    if kn.get(key, 0) < d.val:
                        kn[key] = d.val
                plan[o.eng].append(("o", o))
                if o.dma or o.signal:
                    o.know = dict(kn)
                    if not o.dma:
                        o.know[id(o.sem)] = o.val
            self.n_waits = nw
            with nc.Block() as block:
                for ename, attr in ENG_ATTR.items():
                    lst = plan[ename]
                    if not lst:
                        continue

                    def body(eng, lst=lst):
                        for item in lst:
                            if item[0] == "w":
                                eng.wait_ge(item[1], item[2])
                            else:
                                o = item[1]
                                inst = o.fn(eng)
                                if o.dma:
                                    inst.then_inc(o.sem, 16)
                                elif o.signal:
                                    inst.then_inc(o.sem, 1)
                    getattr(block, attr)(body)


class Arena:
    def __init__(self, nc, nbytes, name="arena"):
        self.t = nc.alloc_sbuf_tensor(name, [128, nbytes], U8)
        self.n = nbytes
        self.off = 0
        self.marks = []

    def alloc(self, cols, dtype, parts=128):
        sz = {F32: 4, BF16: 2, I32: 4, U8: 1}[dtype]
        nb = cols * sz
        off = (self.off + 63) // 64 * 64
        if off + nb > self.n:
            raise RuntimeError(f"arena overflow: want {nb} at {off} of {self.n}")
        self.off = off + nb
        ap = self.t[0:parts, off:off + nb]
        if dtype != U8:
            ap = ap.bitcast(dtype)
        return ap

    def mark(self):
        self.marks.append(self.off)

    def release(self):
        self.off = self.marks.pop()


class Ctx:
    pass


def cast_op(P, i, out, in_, reads, writes):
    if i % 2 == 0:
        return P.op("act", lambda e: e.copy(out=out, in_=in_), reads, writes)
    return P.op("dve", lambda e: e.tensor_copy(out=out, in_=in_), reads, writes)


def load_gain(P, C, gain_dram):
    A = C.A
    g = A.alloc(KD, F32)
    t = Tok("gain")
    with C.nc.allow_non_contiguous_dma(reason="tiny gain load"):
        pass
    P.dma("sp", g, gain_dram.rearrange("(k p) -> p k", p=128), (), (t,),
          allow_slow_non_contiguous=True)
    return g, t


def norm_T(P, C, src, tok0, T, gain, gain_t, xnT, xn_toks, src_tok=None):
    A, ps = C.A, C.ps
    A.mark()
    xk = [A.alloc(2 * 512, F32) for _ in range(2)]
    xk_t = toks(2, "xk")
    sq = [A.alloc(2 * 512, BF16) for _ in range(2)]
    sq_t = toks(2, "sq")
    rstd = A.alloc(512, F32)
    rstd_t = Tok("rstd")
    ssq = ps[7][:, 0:512]
    ssq_t = C.ps_t[7]
    srcr = [src_tok] if src_tok is not None else []
    n = 0
    for blk in range(T // 512):
        c0 = tok0 + blk * 512
        for k2 in range(KD // 2):
            b = n % 2
            n += 1
            P.dma("sp", xk[b].rearrange("p (k t) -> p k t", k=2),
                  src[k2 * 256:(k2 + 1) * 256, c0:c0 + 512].rearrange("(k p) t -> p k t", p=128),
                  srcr, (xk_t[b],))
            P.op("act", lambda e, b=b: e.activation(out=sq[b], in_=xk[b], func=AF.Square),
                 (xk_t[b],), (sq_t[b],))
            for j in range(2):
                k = k2 * 2 + j
                P.op("pe", lambda e, b=b, j=j, k=k: e.matmul(
                    ssq, lhsT=C.ones_bf, rhs=sq[b][:, j * 512:(j + 1) * 512],
                    start=(k == 0), stop=(k == KD - 1)),
                    (sq_t[b], C.const_t), (ssq_t,) if k in (0, KD - 1) else ())
        P.op("dve", lambda e: e.tensor_scalar(out=rstd, in0=ssq, scalar1=1.0 / D, scalar2=EPS,
                                              op0=ALU.mult, op1=ALU.add), (ssq_t,), (rstd_t,))
        P.op("act", lambda e: e.activation(out=rstd, in_=rstd, func=AF.Sqrt), (), (rstd_t,))
        P.op("dve", lambda e: e.reciprocal(out=rstd, in_=rstd), (), (rstd_t,))
        for k2 in range(KD // 2):
            b = n % 2
            n += 1
            P.dma("sp", xk[b].rearrange("p (k t) -> p k t", k=2),
                  src[k2 * 256:(k2 + 1) * 256, c0:c0 + 512].rearrange("(k p) t -> p k t", p=128),
                  srcr, (xk_t[b],))
            for j in range(2):
                k = k2 * 2 + j
                P.op("dve", lambda e, b=b, j=j, k=k, blk=blk: e.scalar_tensor_tensor(
                    out=xnT[:, k, blk * 512:(blk + 1) * 512], in0=xk[b][:, j * 512:(j + 1) * 512],
                    scalar=gain[:, k:k + 1], in1=rstd, op0=ALU.mult, op1=ALU.mult),
                    (xk_t[b], rstd_t, gain_t), (xn_toks[blk],))
    A.release()


def stream_weight(P, C, w_dram_cols, K, ring, n, rows_per_piece=16):
    wb, wb_t = ring["wb"][ring["nb"] % len(ring["wb"])], ring["wb_t"][ring["nb"] % len(ring["wb"])]
    ring["nb"] += 1
    k0 = 0
    while k0 < K:
        kk = min(rows_per_piece, K - k0)
        s = ring["ns"] % len(ring["st"])
        ring["ns"] += 1
        st, st_t = ring["st"][s], ring["st_t"][s]
        P.dma("sp", st[:, 0:kk * 128].rearrange("p (k c) -> p k c", k=kk),
              w_dram_cols[k0 * 128:(k0 + kk) * 128, :].rearrange("(k p) c -> p k c", p=128),
              (), (st_t,))
        cast_op(P, ring["ns"], wb[:, k0 * 128:(k0 + kk) * 128], st[:, 0:kk * 128], (st_t,), (wb_t,))
        k0 += kk
    return wb, wb_t


def ffn_phase(P, C, h_in, h_out, gain_dram, wg, wu, wd, ntok, final_gain=None, out_final=None):
    A, ps, ps_t = C.A, C.ps, C.ps_t
    T = 1024
    NB = T // 512
    A.mark()
    gain, gain_t = load_gain(P, C, gain_dram)
    xnT = A.alloc(KD * T, BF16).rearrange("p (k t) -> p k t", k=KD)
    xn_t = toks(NB, "xn")
    hT = A.alloc(NFF * T, BF16).rearrange("p (c t) -> p c t", c=NFF)
    hT_t = [[Tok(f"h{c}_{b}") for b in range(NB)] for c in range(NFF)]
    ringB = dict(st=[A.alloc(16 * 128, F32) for _ in range(3)], st_t=toks(3, "st"),
                 wb=[A.alloc(16 * 128, BF16) for _ in range(4)], wb_t=toks(4, "wb"), ns=0, nb=0)
    ringC = dict(st=ringB["st"], st_t=ringB["st_t"],
                 wb=[A.alloc(NFF * 128, BF16) for _ in range(2)], wb_t=toks(2, "wd"), ns=0, nb=0)
    sg = [A.alloc(512, BF16) for _ in range(2)]
    sg_t = toks(2, "sg")
    xr = [A.alloc(512, F32) for _ in range(3)]
    xr_t = toks(3, "xr")
    nxr = 0
    nsg = 0
    for tile in range(ntok // T):
        tok0 = tile * T
        norm_T(P, C, h_in, tok0, T, gain, gain_t, xnT, xn_t)
        for c in range(NFF):
            wgb, wgb_t = stream_weight(P, C, wg[:, c * 128:(c + 1) * 128], KD, ringB, None)
            wub, wub_t = stream_weight(P, C, wu[:, c * 128:(c + 1) * 128], KD, ringB, None)
            for blk in range(NB):
                pg, pg_t = ps[2 * blk], ps_t[2 * blk]
                pu, pu_t = ps[2 * blk + 1], ps_t[2 * blk + 1]
                for (wb, wb_t, pp, pp_t) in ((wgb, wgb_t, pg, pg_t), (wub, wub_t, pu, pu_t)):
                    for k in range(KD):
                        P.op("pe", lambda e, wb=wb, pp=pp, k=k, blk=blk: e.matmul(
                            pp, lhsT=wb[:, k * 128:(k + 1) * 128], rhs=xnT[:, k, blk * 512:(blk + 1) * 512],
                            start=(k == 0), stop=(k == KD - 1)),
                            (wb_t, xn_t[blk]), (pp_t,) if k in (0, KD - 1) else ())
                s = nsg % 2
                nsg += 1
                P.op("act", lambda e, s=s, pg=pg: e.activation(out=sg[s], in_=pg, func=AF.Silu),
                     (pg_t,), (sg_t[s],))
                P.op("dve", lambda e, s=s, pu=pu, c=c, blk=blk: e.tensor_tensor(
                    out=hT[:, c, blk * 512:(blk + 1) * 512], in0=pu, in1=sg[s], op=ALU.mult),
                    (pu_t, sg_t[s]), (hT_t[c][blk],))
        for dc in range(KD):
            wdb, wdb_t = stream_weight(P, C, wd[:, dc * 128:(dc + 1) * 128], NFF, ringC, None)
            for blk in range(NB):
                pb = 4 + (dc * NB + blk) % 3
                po, po_t = ps[pb], ps_t[pb]
                r = nxr % 3
                nxr += 1
                c0 = tok0 + blk * 512
                P.dma("sp", xr[r], h_in[dc * 128:(dc + 1) * 128, c0:c0 + 512], (), (xr_t[r],))
                for c in range(NFF):
                    P.op("pe", lambda e, wdb=wdb, po=po, c=c, blk=blk: e.matmul(
                        po, lhsT=wdb[:, c * 128:(c + 1) * 128], rhs=hT[:, c, blk * 512:(blk + 1) * 512],
                        start=(c == 0), stop=(c == NFF - 1)),
                        (wdb_t, hT_t[c][blk]), (po_t,) if c in (0, NFF - 1) else ())
                P.op("dve", lambda e, po=po, r=r: e.scalar_tensor_tensor(
                    out=xr[r], in0=po, scalar=0.5, in1=xr[r], op0=ALU.mult, op1=ALU.add),
                    (po_t,), (xr_t[r],))
                P.dma("pool", h_out[dc * 128:(dc + 1) * 128, c0:c0 + 512], xr[r], (xr_t[r],), ())
    A.release()
    P.barrier()


def setup_common(nc, P):
    C = Ctx()
    C.nc = nc
    C.A = Arena(nc, 206 * 1024)
    C.dbg = None
    C.ps = []
    C.ps_t = toks(8, "ps")
    for i in range(8):
        t = nc.alloc_psum_tensor(f"ps{i}", [128, 512], F32)
        C.ps.append(t[:, :])
    A = C.A
    C.const_t = Tok("const")
    C.ones_bf = A.alloc(128, BF16)
    P.op("dve", lambda e: e.memset(C.ones_bf, 1.0), (), (C.const_t,))
    return C


def build_ffn_test(ntok):
    nc = bass.Bass("TRN2", target_bir_lowering=False)
    xT = nc.dram_tensor("xT", [D, ntok], F32, kind="ExternalInput").ap()
    g = nc.dram_tensor("g", [D], F32, kind="ExternalInput").ap()
    wg = nc.dram_tensor("wg", [D, DFF], F32, kind="ExternalInput").ap()
    wu = nc.dram_tensor("wu", [D, DFF], F32, kind="ExternalInput").ap()
    wd = nc.dram_tensor("wd", [DFF, D], F32, kind="ExternalInput").ap()
    oT = nc.dram_tensor("oT", [D, ntok], F32, kind="ExternalOutput").ap()
    P = Prog(nc)
    C = setup_common(nc, P)
    ffn_phase(P, C, xT, oT, g, wg, wu, wd, ntok)
    P.finish()
    P.emit()
    return nc, P


def make_ring(A, nst=3, nwb=3, wb_cols=16 * 128):
    return dict(st=[A.alloc(2048, F32) for _ in range(nst)], st_t=toks(nst, "st"),
                wb=[A.alloc(wb_cols, BF16) for _ in range(nwb)], wb_t=toks(nwb, "wb"), ns=0, nb=0)


def stream_w(P, C, w_cols, K, ncols, ring):
    i = ring["nb"] % len(ring["wb"])
    ring["nb"] += 1
    wb, wb_t = ring["wb"][i], ring["wb_t"][i]
    per = 2048 // ncols
    k0 = 0
    while k0 < K:
        kk = min(per, K - k0)
        s = ring["ns"] % len(ring["st"])
        ring["ns"] += 1
        st, st_t = ring["st"][s], ring["st_t"][s]
        P.dma("sp", st[:, 0:kk * ncols].rearrange("p (k c) -> p k c", k=kk),
              w_cols[k0 * 128:(k0 + kk) * 128, :].rearrange("(k p) c -> p k c", p=128),
              (), (st_t,))
        cast_op(P, ring["ns"], wb[:, k0 * ncols:(k0 + kk) * ncols], st[:, 0:kk * ncols], (st_t,), (wb_t,))
        k0 += kk
    return wb, wb_t


def load_rep(P, C, dram_vec, n, tok):
    t = C.A.alloc(n, F32)
    P.dma("sp", t, dram_vec.partition_broadcast(128), (), (tok,))
    return t


def out_proj_phase(P, C, yT, w_out, h_in, h_out, ntok):
    A, ps, ps_t = C.A, C.ps, C.ps_t
    T = 1024
    NB = T // 512
    A.mark()
    yb = A.alloc(KD * T, BF16).rearrange("p (k t) -> p k t", k=KD)
    yb_t = Tok("yb")
    ring = make_ring(A)
    xr = [A.alloc(512, F32) for _ in range(3)]
    xr_t = toks(3, "xr")
    nxr = 0
    for tile in range(ntok // T):
        tok0 = tile * T
        for k4 in range(4):
            P.dma("sp", yb[:, k4 * 4:(k4 + 1) * 4, :],
                  yT[k4 * 512:(k4 + 1) * 512, tok0:tok0 + T].rearrange("(k p) t -> p k t", p=128),
                  (C.yT_t,), (yb_t,))
        for dc in range(KD):
            wb, wb_t = stream_w(P, C, w_out[:, dc * 128:(dc + 1) * 128], KD, 128, ring)
            for blk in range(NB):
                pb = 4 + (dc * NB + blk) % 3
                po, po_t = ps[pb], ps_t[pb]
                r = nxr % 3
                nxr += 1
                c0 = tok0 + blk * 512
                P.dma("sp", xr[r], h_in[dc * 128:(dc + 1) * 128, c0:c0 + 512], (), (xr_t[r],))
                for k in range(KD):
                    P.op("pe", lambda e, wb=wb, po=po, k=k, blk=blk: e.matmul(
                        po, lhsT=wb[:, k * 128:(k + 1) * 128], rhs=yb[:, k, blk * 512:(blk + 1) * 512],
                        start=(k == 0), stop=(k == KD - 1)),
                        (wb_t, yb_t), (po_t,) if k in (0, KD - 1) else ())
                P.op("dve", lambda e, po=po, r=r: e.tensor_tensor(
                    out=xr[r], in0=po, in1=xr[r], op=ALU.add), (po_t,), (xr_t[r],))
                P.dma("pool", h_out[dc * 128:(dc + 1) * 128, c0:c0 + 512], xr[r], (xr_t[r],), ())
    A.release()
    P.barrier()


LAMBDA_INIT1 = 0.8 - 0.6 * math.exp(-0.3 * 1)
ATT_SCALE = 128 ** -0.5


def attn_phase(P, C, h_in, gain_dram, w_qkv, lq1, lk1, lq2, lk2, subln, cst, yT, ntok):
    nc, A, ps, ps_t = C.nc, C.A, C.ps, C.ps_t
    A.mark()
    ct = Tok("attc")
    gain, gain_t = load_gain(P, C, gain_dram)
    cosf = A.alloc(SEQ, F32)
    sinf = A.alloc(SEQ, F32)
    P.dma("sp", cosf, cst["att_cos"], (), (ct,))
    P.dma("sp", sinf, cst["att_sin"], (), (ct,))
    pm32 = A.alloc(128, F32)
    P.dma("sp", pm32, cst["att_pm"], (), (ct,))
    pm = A.alloc(128, BF16)
    P.op("dve", lambda e: e.tensor_copy(out=pm, in_=pm32), (ct,), (ct,))
    id32 = A.alloc(128, F32)
    P.dma("sp", id32, cst["ident"], (), (ct,))
    ident = A.alloc(128, BF16)
    P.op("dve", lambda e: e.tensor_copy(out=ident, in_=id32), (ct,), (ct,))
    lt = Tok("lam")
    lv = [load_rep(P, C, v[0], 128, lt) for v in (lq1, lk1, lq2, lk2)]
    lsum = A.alloc(2, F32)
    junk = A.alloc(128, F32)
    for i in range(2):
        P.op("dve", lambda e, i=i: e.tensor_tensor(out=junk, in0=lv[2 * i], in1=lv[2 * i + 1], op=ALU.mult),
             (lt,), (lt,))
        P.op("dve", lambda e, i=i: e.reduce_sum(out=lsum[:, i:i + 1], in_=junk, axis=AX.X), (), (lt,))
    P.op("act", lambda e: e.activation(out=lsum, in_=lsum, func=AF.Exp), (), (lt,))
    neglam = A.alloc(1, F32)
    P.op("dve", lambda e: e.tensor_tensor(out=neglam, in0=lsum[:, 1:2], in1=lsum[:, 0:1], op=ALU.subtract),
         (), (lt,))
    P.op("dve", lambda e: e.tensor_scalar(out=neglam, in0=neglam, scalar1=-LAMBDA_INIT1, scalar2=None,
                                          op0=ALU.add), (), (lt,))
    sub_rep = load_rep(P, C, subln[0], 256, lt)
    P.op("dve", lambda e: e.tensor_scalar(out=sub_rep, in0=sub_rep, scalar1=1.0 - LAMBDA_INIT1, scalar2=None,
                                          op0=ALU.mult), (), (lt,))

    hnT = A.alloc(KD * SEQ, BF16).rearrange("p (k t) -> p k t", k=KD)
    hn_t = toks(SEQ // 512, "hn")
    ring = make_ring(A, nst=3, nwb=3, wb_cols=16 * 256)
    qk = [A.alloc(2 * SEQ, BF16).rearrange("p (c t) -> p c t", c=2) for _ in range(2)]
    qk_t = [[toks(4, "q0"), toks(4, "q1")], [toks(4, "k0"), toks(4, "k1")]]
    VW = 258
    v_sb = A.alloc(16 * VW, BF16).rearrange("p (b e) -> p b e", b=16)
    v_t = toks(16, "v")
    x_sb = [A.alloc(512, BF16) for _ in range(2)]
    x_t = toks(2, "x")
    t1 = [A.alloc(512, F32) for _ in range(2)]
    t1_t = toks(2, "t1")
    t2 = [A.alloc(512, F32) for _ in range(2)]
    t2_t = toks(2, "t2")
    pT = [A.alloc(256, BF16) for _ in range(6)]
    pT_t = toks(6, "pT")
    acc = [A.alloc(256, F32) for _ in range(2)]
    acc_t = toks(2, "acc")
    rs = A.alloc(8, F32)
    rs_t = Tok("rs")
    y_sb = [A.alloc(256, BF16) for _ in range(2)]
    y_t = toks(2, "y")
    yst = A.alloc(2 * SEQ, BF16).rearrange("p (j t) -> p j t", j=2)
    yst_t = Tok("yst")
    sq_junk = A.alloc(256, F32)
    P.op("pool", lambda e: e.memset(v_sb[:, :, 256:257], 1.0), (), tuple(v_t))
    nx = 0
    npt = 0
    for s in range(ntok // SEQ):
        s0 = s * SEQ
        norm_T(P, C, h_in, s0, SEQ, gain, gain_t, hnT, hn_t)
        for h in range(8):
            for qi in range(2):
                for comp in range(2):
                    col = qi * D + (2 * h + comp) * 128
                    wb, wb_t = stream_w(P, C, w_qkv[:, col:col + 128], KD, 128, ring)
                    for blk in range(4):
                        pp, pp_t = ps[blk % 2], ps_t[blk % 2]
                        for k in range(KD):
                            P.op("pe", lambda e, wb=wb, pp=pp, k=k, blk=blk: e.matmul(
                                pp, lhsT=wb[:, k * 128:(k + 1) * 128], rhs=hnT[:, k, blk * 512:(blk + 1) * 512],
                                start=(k == 0), stop=(k == KD - 1)),
                                (wb_t, hn_t[blk]), (pp_t,) if k in (0, KD - 1) else ())
                        b = nx % 2
                        nx += 1
                        P.op("act", lambda e, b=b, pp=pp: e.copy(out=x_sb[b], in_=pp), (pp_t,), (x_t[b],))
                        P.op("pe", lambda e, b=b: e.matmul(ps[2], lhsT=pm, rhs=x_sb[b], start=True, stop=True),
                             (x_t[b], ct), (ps_t[2],))
                        tsl = slice(blk * 512, (blk + 1) * 512)
                        P.op("dve", lambda e, b=b, tsl=tsl: e.tensor_tensor(
                            out=t1[b], in0=ps[2], in1=sinf[:, tsl], op=ALU.mult), (ps_t[2], ct), (t1_t[b],))
                        P.op("dve", lambda e, b=b, pp=pp, tsl=tsl: e.tensor_tensor(
                            out=t2[b], in0=pp, in1=cosf[:, tsl], op=ALU.mult), (pp_t, ct), (t2_t[b],))
                        P.op("pool", lambda e, b=b, qi=qi, comp=comp, tsl=tsl: e.tensor_tensor(
                            out=qk[qi][:, comp, tsl], in0=t1[b], in1=t2[b], op=ALU.add),
                            (t1_t[b], t2_t[b]), (qk_t[qi][comp][blk],))
            col = 2 * D + h * 256
            wv, wv_t = stream_w(P, C, w_qkv[:, col:col + 256], KD, 256, ring)
            if C.dbg is not None and h == 0 and s == 0:
                P.dma("pool", C.dbg["wv"], wv, (wv_t,), ())
            for tb in range(16):
                pp, pp_t = ps[tb % 2], ps_t[tb % 2]
                for k in range(KD):
                    P.op("pe", lambda e, pp=pp, k=k, tb=tb, wv=wv: e.matmul(
                        pp[:, 0:256], lhsT=hnT[:, k, tb * 128:(tb + 1) * 128], rhs=wv[:, k * 256:(k + 1) * 256],
                        start=(k == 0), stop=(k == KD - 1)),
                        (wv_t, hn_t[tb // 4]), (pp_t,) if k in (0, KD - 1) else ())
                P.op("act", lambda e, pp=pp, tb=tb: e.copy(out=v_sb[:, tb, 0:256], in_=pp[:, 0:256]),
                     (pp_t,), (v_t[tb],))
            if C.dbg is not None and h == 0 and s == 0:
                P.dma("pool", C.dbg["q"], qk[0], tuple(qk_t[0][0] + qk_t[0][1]), ())
                P.dma("pool", C.dbg["k"], qk[1], tuple(qk_t[1][0] + qk_t[1][1]), ())
                P.dma("pool", C.dbg["v"], v_sb, tuple(v_t), ())
                P.dma("pool", C.dbg["lam"], neglam, (lt,), ())
            tasks = [(qr, comp, jb) for qr in range(8) for comp in range(2) for jb in range(2 * qr + 2)]
            LA = 3
            SB = (0, 3, 4, 1)
            pvb = {0: (5, 6), 1: (7, 2)}

            def emit_S(i, h=h):
                qr, comp, jb = tasks[i]
                a0_ = max(0, jb - 2 * qr)
                x0 = a0_ * 128
                sb = SB[i % len(SB)]
                MM(P, ps[sb][:, x0:256], qk[1][:, comp, jb * 128:(jb + 1) * 128],
                   qk[0][:, comp, qr * 256 + x0:(qr + 1) * 256], True, True,
                   (qk_t[1][comp][jb // 4], qk_t[0][comp][qr // 2]), (ps_t[sb],))
                pi = i % len(pT)
                ACTV(P, pT[pi][:, x0:256], ps[sb][:, x0:256], AF.Exp, (ps_t[sb],), (pT_t[pi],), scale=ATT_SCALE)
                if jb >= 2 * qr:
                    MSET(P, "pool", pT[pi][64:128, x0:x0 + 64], 0.0, (), (pT_t[pi],))

            def emit_PV(i, h=h):
                qr, comp, jb = tasks[i]
                a0_ = max(0, jb - 2 * qr)
                pi = i % len(pT)
                pv = [ps[pvb[comp][0]], ps[pvb[comp][1]]]
                pv_t = [ps_t[pvb[comp][0]], ps_t[pvb[comp][1]]]
                for a in range(a0_, 2):
                    last = (jb == 2 * qr + a)
                    MM(P, pv[a][:, 0:257], pT[pi][:, a * 128:(a + 1) * 128], v_sb[:, jb, 0:257], jb == 0, last,
                       (pT_t[pi], v_t[jb]), (pv_t[a],) if (jb == 0 or last) else ())
                if jb != 2 * qr + 1:
                    return
                for a in range(2):
                    c = comp * 2 + a
                    P.op("dve", lambda e, a=a, c=c, pv=pv: e.reciprocal(out=rs[:, c:c + 1], in_=pv[a][:, 256:257]),
                         (pv_t[a],), (rs_t,))
                    if comp == 0:
                        ACTV(P, acc[a], pv[a][:, 0:256], AF.Copy, (pv_t[a], rs_t), (acc_t[a],), scale=rs[:, c:c + 1])
                    else:
                        TT(P, "dve", rs[:, c:c + 1], rs[:, c:c + 1], neglam, ALU.mult, (lt,), (rs_t,))
                        STT(P, acc[a], pv[a][:, 0:256], rs[:, c:c + 1], acc[a], ALU.mult, ALU.add,
                            (pv_t[a], rs_t), (acc_t[a],))
                if comp == 0:
                    return
                for a in range(2):
                    c = 4 + a
                    ACTV(P, sq_junk, acc[a], AF.Square, (acc_t[a],), (rs_t,), accum_out=rs[:, c:c + 1])
                    TS(P, "dve", rs[:, c:c + 1], rs[:, c:c + 1], 1.0 / 256, EPS, ALU.mult, ALU.add, (), (rs_t,))
                    rstd_inplace(P, rs[:, c:c + 1], rs_t)
                    STT(P, y_sb[a], acc[a], rs[:, c:c + 1], sub_rep, ALU.mult, ALU.mult, (acc_t[a], rs_t, lt), (y_t[a],))
                    ptr = ps[2].bitcast(BF16)
                    for j in range(2):
                        TR(P, ptr[:, 640 + j * 128:640 + (j + 1) * 128], y_sb[a][:, j * 128:(j + 1) * 128], ident,
                           (y_t[a], ct), (ps_t[2],))
                    t0 = (qr * 2 + a) * 128
                    CP(P, "dve", yst[:, :, t0:t0 + 128], ptr[:, 640:896].rearrange("p (j t) -> p j t", j=2),
                       (ps_t[2],), (yst_t,))

            for i in range(len(tasks) + LA):
                if i < len(tasks):
                    emit_S(i)
                if i >= LA:
                    emit_PV(i - LA)
            P.dma("pool", yT[h * 256:(h + 1) * 256, s0:s0 + SEQ].rearrange("(j p) t -> p j t", p=128), yst,
                  (yst_t,), (C.yT_t,))
    A.release()
    P.barrier()


def host_consts():
    c = {}
    t = np.arange(SEQ, dtype=np.float32)
    inv = (1.0 / (np.float32(500000.0) ** (np.arange(0, 32, 2, dtype=np.float32) / np.float32(32)))).astype(np.float32)
    ang = t[:, None] * inv[None, :]
    cos, sin = np.cos(ang).astype(np.float32), np.sin(ang).astype(np.float32)
    cf = np.ones((128, SEQ), np.float32)
    sf = np.zeros((128, SEQ), np.float32)
    cf[0:16] = cos.T
    cf[16:32] = cos.T
    sf[0:16] = sin.T
    sf[16:32] = sin.T
    c["att_cos"], c["att_sin"] = cf, sf
    pm = np.zeros((128, 128), np.float32)
    for m in range(16):
        pm[m + 16, m] = -1.0
        pm[m, m + 16] = 1.0
    c["att_pm"] = pm
    c["ident"] = np.eye(128, dtype=np.float32)
    return c


CONST_SHAPES = {"att_cos": [128, SEQ], "att_sin": [128, SEQ], "att_pm": [128, 128], "ident": [128, 128]}


def build_attn_test(ntok):
    nc = bass.Bass("TRN2", target_bir_lowering=False)
    hT = nc.dram_tensor("hT", [D, ntok], F32, kind="ExternalInput").ap()
    g = nc.dram_tensor("g", [D], F32, kind="ExternalInput").ap()
    wqkv = nc.dram_tensor("wqkv", [D, 3 * D], F32, kind="ExternalInput").ap()
    wo = nc.dram_tensor("wo", [D, D], F32, kind="ExternalInput").ap()
    lam = [nc.dram_tensor(n, [1, 128], F32, kind="ExternalInput").ap() for n in ("lq1", "lk1", "lq2", "lk2")]
    subln = nc.dram_tensor("subln", [1, 256], F32, kind="ExternalInput").ap()
    cst = {k: nc.dram_tensor(k, v, F32, kind="ExternalInput").ap() for k, v in CONST_SHAPES.items()
           if k in ("att_cos", "att_sin", "att_pm", "ident")}
    yT = nc.dram_tensor("yT", [D, ntok], BF16, kind="ExternalOutput").ap()
    oT = nc.dram_tensor("oT", [D, ntok], F32, kind="ExternalOutput").ap()
    P = Prog(nc)
    C = setup_common(nc, P)
    C.yT_t = Tok("yT")
    C.dbg = {"q": nc.dram_tensor("dq", [128, 2, SEQ], BF16, kind="ExternalOutput").ap(),
             "k": nc.dram_tensor("dk", [128, 2, SEQ], BF16, kind="ExternalOutput").ap(),
             "v": nc.dram_tensor("dv", [128, 16, 258], BF16, kind="ExternalOutput").ap(),
             "wv": nc.dram_tensor("dwv", [128, 4096], BF16, kind="ExternalOutput").ap(),
             "lam": nc.dram_tensor("dlam", [128, 1], F32, kind="ExternalOutput").ap()}
    attn_phase(P, C, hT, g, wqkv, lam[0], lam[1], lam[2], lam[3], subln, cst, yT, ntok)
    out_proj_phase(P, C, yT, wo, hT, oT, ntok)
    P.finish()
    P.emit()
    return nc, P


def TT(P, eng, out, in0, in1, op, r=(), w=()):
    return P.op(eng, lambda e: e.tensor_tensor(out=out, in0=in0, in1=in1, op=op), r, w)


def TS(P, eng, out, in0, s1, s2, op0, op1=None, r=(), w=()):
    if op1 is None:
        return P.op(eng, lambda e: e.tensor_scalar(out=out, in0=in0, scalar1=s1, scalar2=None, op0=op0), r, w)
    return P.op(eng, lambda e: e.tensor_scalar(out=out, in0=in0, scalar1=s1, scalar2=s2, op0=op0, op1=op1), r, w)


def STT(P, out, in0, scalar, in1, op0, op1, r=(), w=()):
    return P.op("dve", lambda e: e.scalar_tensor_tensor(out=out, in0=in0, scalar=scalar, in1=in1, op0=op0, op1=op1), r, w)


def ACTV(P, out, in_, func, r=(), w=(), **kw):
    return P.op("act", lambda e: e.activation(out=out, in_=in_, func=func, **kw), r, w)


def CP(P, eng, out, in_, r=(), w=()):
    if eng == "act":
        return P.op("act", lambda e: e.copy(out=out, in_=in_), r, w)
    return P.op(eng, lambda e: e.tensor_copy(out=out, in_=in_), r, w)


def MM(P, out, lhsT, rhs, start, stop, r=(), w=()):
    return P.op("pe", lambda e: e.matmul(out, lhsT=lhsT, rhs=rhs, start=start, stop=stop), r, w)


def TR(P, out, in_, ident, r=(), w=()):
    return P.op("pe", lambda e: e.transpose(out, in_, ident), r, w)


def MSET(P, eng, ap, val, r=(), w=()):
    return P.op(eng, lambda e: e.memset(ap, val), r, w)


def rstd_inplace(P, x, t):
    ACTV(P, x, x, AF.Sqrt, (), (t,))
    P.op("dve", lambda e: e.reciprocal(out=x, in_=x), (), (t,))


RET_GAMMA = [1.0 - 2.0 ** (-5.0 - h) for h in range(4)]


def mix0a_phase(P, C, h_in, gain_dram, w_in, cst, yT, uT_d, Ud, ntok):
    nc, A, ps, ps_t = C.nc, C.A, C.ps, C.ps_t
    A.mark()
    ct = Tok("m0c")
    gain, gain_t = load_gain(P, C, gain_dram)
    cosf = A.alloc(SEQ, F32)
    sinf = A.alloc(SEQ, F32)
    P.dma("sp", cosf, cst["ret_cos"], (), (ct,))
    P.dma("sp", sinf, cst["ret_sin"], (), (ct,))
    id32 = A.alloc(128, F32)
    P.dma("sp", id32, cst["ident"], (), (ct,))
    ident = A.alloc(128, BF16)
    CP(P, "dve", ident, id32, (ct,), (ct,))
    tab = A.alloc(4 * 384, F32).rearrange("p (h y) -> p h y", h=4)
    P.dma("sp", tab, cst["ret_tab"].rearrange("h p y -> p h y"), (), (ct,))
    hnT = A.alloc(KD * SEQ, BF16).rearrange("p (k t) -> p k t", k=KD)
    hn_t = toks(SEQ // 512, "hn")
    ring = make_ring(A, nst=3, nwb=2, wb_cols=16 * 256)
    qk = [A.alloc(2 * SEQ, BF16).rearrange("p (c t) -> p c t", c=2) for _ in range(2)]
    qk_t = [toks(4, "rq"), toks(4, "rk")]
    sgT = A.alloc(2 * SEQ, BF16).rearrange("p (c t) -> p c t", c=2)
    sg_t = toks(4, "sg")
    v_sb = A.alloc(16 * 256, BF16).rearrange("p (b e) -> p b e", b=16)
    v_t = toks(16, "rv")
    tmp = [A.alloc(512, F32) for _ in range(4)]
    tmp_t = toks(4, "tmp")
    pT = [A.alloc(256, BF16) for _ in range(5)]
    pT_t = toks(5, "rpT")
    rs = A.alloc(4, F32)
    rs_t = Tok("rrs")
    sq_junk = A.alloc(256, F32)
    y_sb = [A.alloc(256, BF16) for _ in range(2)]
    y_t = toks(2, "ry")
    yst = A.alloc(2 * SEQ, BF16).rearrange("p (j t) -> p j t", j=2)
    yst_t = Tok("ryst")
    u_sb = [A.alloc(SEQ, BF16) for _ in range(1)]
    u_t = toks(2, "u")
    up_sb = [A.alloc(SEQ, BF16).rearrange("p (j m) -> p j m", j=8) for _ in range(1)]
    up_t = toks(2, "up")
    npt = 0
    for s in range(ntok // SEQ):
        s0 = s * SEQ
        norm_T(P, C, h_in, s0, SEQ, gain, gain_t, hnT, hn_t)
        for a in range(8):
            if getattr(C, "skip_u", False):
                break
            col = 4096 + a * 128
            wb, wb_t = stream_w(P, C, w_in[:, col:col + 128], KD, 128, ring)
            ub = 0
            for blk in range(4):
                pp, pp_t = ps[blk % 2], ps_t[blk % 2]
                for k in range(KD):
                    MM(P, pp, wb[:, k * 128:(k + 1) * 128], hnT[:, k, blk * 512:(blk + 1) * 512], k == 0, k == KD - 1,
                       (wb_t, hn_t[blk]), (pp_t,) if k in (0, KD - 1) else ())
                CP(P, "act", u_sb[ub][:, blk * 512:(blk + 1) * 512], pp, (pp_t,), (u_t[ub],))
                CP(P, "dve", up_sb[ub][:, :, blk * 64:(blk + 1) * 64],
                   u_sb[ub][:, blk * 512:(blk + 1) * 512].rearrange("p (m j) -> p j m", j=8),
                   (u_t[ub],), (up_t[ub],))
            P.dma("pool", uT_d[a * 128:(a + 1) * 128, s0:s0 + SEQ], u_sb[ub], (u_t[ub],), (C.uT_t,))
            for gg in range(8):
                if getattr(C, "skip_ud", False):
                    break
                P.dma("pool", Ud[a * 8 + gg, :, :, s * 256:(s + 1) * 256].rearrange("j p m -> p j m"),
                      up_sb[ub][gg * 16:(gg + 1) * 16, :, :], (up_t[ub],), (C.Ud_t,))
        for h in range(4):
            if getattr(C, "skip_ret", False):
                break
            lng = math.log(RET_GAMMA[h])
            for qi in range(2):
                wb, wb_t = stream_w(P, C, w_in[:, qi * 1024 + h * 256: qi * 1024 + (h + 1) * 256], KD, 256, ring)
                for blk in range(4):
                    pa, pa_t = ps[0], ps_t[0]
                    pb, pb_t = ps[1], ps_t[1]
                    for (pp, pp_t, j) in ((pa, pa_t, 0), (pb, pb_t, 1)):
                        for k in range(KD):
                            MM(P, pp, wb[:, k * 256 + j * 128:k * 256 + (j + 1) * 128],
                               hnT[:, k, blk * 512:(blk + 1) * 512], k == 0, k == KD - 1,
                               (wb_t, hn_t[blk]), (pp_t,) if k in (0, KD - 1) else ())
                    tsl = slice(blk * 512, (blk + 1) * 512)
                    TT(P, "dve", tmp[0], pa, cosf[:, tsl], ALU.mult, (pa_t, ct), (tmp_t[0],))
                    TT(P, "dve", tmp[1], pb, sinf[:, tsl], ALU.mult, (pb_t, ct), (tmp_t[1],))
                    TT(P, "dve", tmp[2], pb, cosf[:, tsl], ALU.mult, (pb_t, ct), (tmp_t[2],))
                    TT(P, "dve", tmp[3], pa, sinf[:, tsl], ALU.mult, (pa_t, ct), (tmp_t[3],))
                    TT(P, "pool", qk[qi][:, 0, tsl], tmp[0], tmp[1], ALU.subtract, (tmp_t[0], tmp_t[1]), (qk_t[qi][blk],))
                    TT(P, "pool", qk[qi][:, 1, tsl], tmp[2], tmp[3], ALU.add, (tmp_t[2], tmp_t[3]), (qk_t[qi][blk],))
            wb, wb_t = stream_w(P, C, w_in[:, 3072 + h * 256: 3072 + (h + 1) * 256], KD, 256, ring)
            for blk in range(4):
                for j in range(2):
                    pp, pp_t = ps[j], ps_t[j]
                    for k in range(KD):
                        MM(P, pp, wb[:, k * 256 + j * 128:k * 256 + (j + 1) * 128],
                           hnT[:, k, blk * 512:(blk + 1) * 512], k == 0, k == KD - 1,
                           (wb_t, hn_t[blk]), (pp_t,) if k in (0, KD - 1) else ())
                    ACTV(P, sgT[:, j, blk * 512:(blk + 1) * 512], pp, AF.Silu, (pp_t,), (sg_t[blk],))
            wv, wv_t = stream_w(P, C, w_in[:, 2048 + h * 256: 2048 + (h + 1) * 256], KD, 256, ring)
            for tb in range(16):
                pp, pp_t = ps[tb % 2], ps_t[tb % 2]
                for k in range(KD):
                    MM(P, pp[:, 0:256], hnT[:, k, tb * 128:(tb + 1) * 128], wv[:, k * 256:(k + 1) * 256],
                       k == 0, k == KD - 1, (wv_t, hn_t[tb // 4]), (pp_t,) if k in (0, KD - 1) else ())
                CP(P, "act", v_sb[:, tb, :], pp[:, 0:256], (pp_t,), (v_t[tb],))
            rtasks = [(qr, jb) for qr in range(8) for jb in range(2 * qr + 2)]
            RLA = 2
            RSB = (3, 4, 0)
            rpvb = ((5, 6), (7, 1))

            def r_emit_S(i, h=h, lng=lng):
                qr, jb = rtasks[i]
                a0_ = max(0, jb - 2 * qr)
                x0 = a0_ * 128
                sb = RSB[i % len(RSB)]
                for j in range(2):
                    MM(P, ps[sb][:, x0:256], qk[1][:, j, jb * 128:(jb + 1) * 128],
                       qk[0][:, j, qr * 256 + x0:(qr + 1) * 256], j == 0, j == 1,
                       (qk_t[1][jb // 4], qk_t[0][qr // 2]), (ps_t[sb],))
                pi = i % len(pT)
                d0 = 2 * qr - jb
                if d0 >= 1:
                    sc, y0 = (1.0 / 16.0) * math.exp(lng * 128 * (d0 - 1)), 128
                else:
                    sc, y0 = 1.0 / 16.0, 0
                STT(P, pT[pi][:, x0:256], ps[sb][:, x0:256], sc, tab[:, h, y0:y0 + 256 - x0], ALU.mult, ALU.mult,
                    (ps_t[sb], ct), (pT_t[pi],))

            def r_emit_PV(i, h=h):
                qr, jb = rtasks[i]
                a0_ = max(0, jb - 2 * qr)
                pi = i % len(pT)
                bk = rpvb[qr % 2]
                pv = [ps[bk[0]], ps[bk[1]]]
                pv_t = [ps_t[bk[0]], ps_t[bk[1]]]
                for a in range(a0_, 2):
                    last = (jb == 2 * qr + a)
                    MM(P, pv[a][:, 0:256], pT[pi][:, a * 128:(a + 1) * 128], v_sb[:, jb, :], jb == 0, last,
                       (pT_t[pi], v_t[jb]), (pv_t[a],) if (jb == 0 or last) else ())
                if jb != 2 * qr + 1:
                    return
                for a in range(2):
                    ACTV(P, sq_junk, pv[a][:, 0:256], AF.Square, (pv_t[a],), (rs_t,), accum_out=rs[:, a:a + 1])
                    TS(P, "dve", rs[:, a:a + 1], rs[:, a:a + 1], 1.0 / 256, EPS, ALU.mult, ALU.add, (), (rs_t,))
                    rstd_inplace(P, rs[:, a:a + 1], rs_t)
                    ACTV(P, y_sb[a], pv[a][:, 0:256], AF.Copy, (pv_t[a], rs_t), (y_t[a],), scale=rs[:, a:a + 1])
                    ptr = ps[2].bitcast(BF16)
                    for j in range(2):
                        TR(P, ptr[:, j * 128:(j + 1) * 128], y_sb[a][:, j * 128:(j + 1) * 128], ident,
                           (y_t[a], ct), (ps_t[2],))
                    t0 = (qr * 2 + a) * 128
                    TT(P, "dve", yst[:, :, t0:t0 + 128], ptr[:, 0:256].rearrange("p (j t) -> p j t", j=2),
                       sgT[:, :, t0:t0 + 128], ALU.mult, (ps_t[2], sg_t[t0 // 512]), (yst_t,))

            for i in range(len(rtasks) + RLA):
                if i < len(rtasks):
                    r_emit_S(i)
                if i >= RLA:
                    r_emit_PV(i - RLA)
            P.dma("pool", yT[h * 256:(h + 1) * 256, s0:s0 + SEQ].rearrange("(j p) t -> p j t", p=128), yst,
                  (yst_t,), (C.yT_t,))
    A.release()
    P.barrier()


def host_consts_ret(c):
    t = np.arange(SEQ, dtype=np.float32)
    inv = (1.0 / (np.float32(10000.0) ** (np.arange(0, 256, 2, dtype=np.float32) / np.float32(256)))).astype(np.float32)
    ang = t[:, None] * inv[None, :]
    c["ret_cos"] = np.ascontiguousarray(np.cos(ang).astype(np.float32).T)
    c["ret_sin"] = np.ascontiguousarray(np.sin(ang).astype(np.float32).T)
    tab = np.zeros((4, 128, 384), np.float32)
    jj = np.arange(128)[:, None].astype(np.float64)
    for h in range(4):
        lg = math.log(RET_GAMMA[h])
        i = np.arange(128)[None, :].astype(np.float64)
        dg = np.exp(lg * np.abs(i - jj))
        dg[64:128, 0:64] = 0.0
        tab[h, :, 0:128] = dg
        y = np.arange(128, 384)[None, :].astype(np.float64)
        tab[h, :, 128:384] = np.exp(lg * (y - jj))
    c["ret_tab"] = tab
    return c


CONST_SHAPES.update({"ret_cos": [128, SEQ], "ret_sin": [128, SEQ], "ret_tab": [4, 128, 384]})


NEA, NEC = 71, 65
NE = NEA + NEC
GB = 4
S5_BARRIERS = False
PI = math.pi


def s5_powers(P, C, W, th, rho, ex, g0, tk):
    shp = [64, GB, NE]
    thb = th[:, g0:g0 + GB].unsqueeze(2).to_broadcast(shp)
    rhb = rho[:, g0:g0 + GB].unsqueeze(2).to_broadcast(shp)
    exb = ex.unsqueeze(1).to_broadcast(shp)
    ang, r, fx, kf, ki, Pr, Pi, mag = W["ang"], W["r"], W["fx"], W["kf"], W["ki"], W["Pr"], W["Pi"], W["mag"]
    TT(P, "dve", ang, thb, exb, ALU.mult, (tk,), (tk,))
    TT(P, "pool", mag, rhb, exb, ALU.mult, (tk,), (tk,))
    ACTV(P, mag, mag, AF.Exp, (), (tk,))
    for (dst, off) in ((Pi, 0.0), (Pr, PI / 2)):
        if off != 0.0:
            TS(P, "dve", ang, ang, off, None, ALU.add, None, (), (tk,))
        TS(P, "dve", r, ang, 1.0 / (2 * PI), 32.5, ALU.mult, ALU.add, (), (tk,))
        CP(P, "dve", ki, r, (), (tk,))
        CP(P, "dve", kf, ki, (), (tk,))
        TS(P, "dve", kf, kf, -32.0, -2 * PI, ALU.add, ALU.mult, (), (tk,))
        TT(P, "dve", r, kf, ang, ALU.add, (), (tk,))
        TS(P, "dve", fx, r, -PI, 2 * PI, ALU.is_lt, ALU.mult, (), (tk,))
        TT(P, "dve", r, r, fx, ALU.add, (), (tk,))
        TS(P, "dve", fx, r, PI, -2 * PI, ALU.is_gt, ALU.mult, (), (tk,))
        TT(P, "dve", r, r, fx, ALU.add, (), (tk,))
        TS(P, "dve", r, r, -3.141592, 3.141592, ALU.max, ALU.min, (), (tk,))
        ACTV(P, r, r, AF.Sin, (), (tk,))
        TT(P, "dve", dst, r, mag, ALU.mult, (), (tk,))


def s5_phase(P, C, prm, cst, Ud, Yd):
    nc, A, ps, ps_t = C.nc, C.A, C.ps, C.ps_t
    A.mark()
    tk = Tok("s5p")
    N = 64

    def al(cols, dt=F32):
        return A.alloc(cols, dt)[0:64]

    lr, li, lst = al(64), al(64), al(64)
    for q4_ in range(4):
        gs = slice(q4_ * 16, (q4_ + 1) * 16)
        P.dma("sp", lr[:, gs], prm["lam_re"][0][gs, :].rearrange("g n -> n g"), (), (tk,), allow_slow_non_contiguous=True)
        P.dma("sp", li[:, gs], prm["lam_im"][0][gs, :].rearrange("g n -> n g"), (), (tk,), allow_slow_non_contiguous=True)
    P.dma("sp", lst, prm["log_step"][0].partition_broadcast(64), (), (tk,))
    ex = al(NE)
    P.dma("sp", ex, cst["s5_ex"][0].partition_broadcast(64), (), (tk,))
    id32 = A.alloc(128, F32)
    P.dma("sp", id32, cst["ident"], (), (tk,))
    ident = A.alloc(128, BF16)
    CP(P, "dve", ident, id32, (tk,), (tk,))
    mask0 = A.alloc(128, F32)
    P.dma("sp", mask0, cst["s5_mask"], (), (tk,))
    ACTV(P, lst, lst, AF.Exp, (tk,), (tk,))
    rho, th = al(64), al(64)
    TT(P, "dve", rho, lr, lst, ALU.mult, (), (tk,))
    TT(P, "dve", th, li, lst, ALU.mult, (), (tk,))
    bbr = al(1024).rearrange("p (g q) -> p g q", g=64)
    bbi = al(1024).rearrange("p (g q) -> p g q", g=64)
    cr = al(1024).rearrange("p (g q) -> p g q", g=64)
    ci = al(1024).rearrange("p (g q) -> p g q", g=64)
    A.mark()
    cn = A.alloc(8 * 64, F32).rearrange("p (t n) -> p t n", t=8)
    for (src, dst) in ((prm["c_re"], cr), (prm["c_im"], ci)):
        P.dma("sp", cn, src[0].rearrange("(t g) p n -> (g p) t n", t=8), (), (tk,))
        for tt in range(8):
            TR(P, ps[0][0:64, 0:128], cn[:, tt, :], id32, (tk,), (ps_t[0],))
            CP(P, "dve", dst[:, tt * 8:(tt + 1) * 8, :], ps[0][0:64, 0:128].rearrange("p (g q) -> p g q", g=8),
               (ps_t[0],), (tk,))
    A.release()
    W = {k: al(GB * NE).rearrange("p (g e) -> p g e", g=GB) for k in ("ang", "r", "fx", "kf", "Pr", "Pi", "mag")}
    W["ki"] = al(GB * NE, I32).rearrange("p (g e) -> p g e", g=GB)
    a1r, a1i, a64r, a64i = al(64), al(64), al(64), al(64)
    for gb in range(64 // GB):
        s5_powers(P, C, W, th, rho, ex, gb * GB, tk)
        CP(P, "dve", a1r[:, gb * GB:(gb + 1) * GB], W["Pr"][:, :, NEA + 1], (), (tk,))
        CP(P, "dve", a1i[:, gb * GB:(gb + 1) * GB], W["Pi"][:, :, NEA + 1], (), (tk,))
        CP(P, "dve", a64r[:, gb * GB:(gb + 1) * GB], W["Pr"][:, :, NEA + 64], (), (tk,))
        CP(P, "dve", a64i[:, gb * GB:(gb + 1) * GB], W["Pi"][:, :, NEA + 64], (), (tk,))
        if C.dbg is not None and gb == 0:
            P.dma("pool", C.dbg["pr"], W["Pr"], (tk,), ())
            P.dma("pool", C.dbg["pi"], W["Pi"], (tk,), ())
    den, t1, t2, fr, fi = al(64), al(64), al(64), al(64), al(64)
    TT(P, "dve", den, lr, lr, ALU.mult, (), (tk,))
    TT(P, "dve", t1, li, li, ALU.mult, (), (tk,))
    TT(P, "dve", den, den, t1, ALU.add, (), (tk,))
    P.op("dve", lambda e: e.reciprocal(out=den, in_=den), (), (tk,))
    TS(P, "dve", a1r, a1r, -1.0, None, ALU.add, None, (), (tk,))
    TT(P, "dve", t1, a1r, lr, ALU.mult, (), (tk,))
    TT(P, "dve", t2, a1i, li, ALU.mult, (), (tk,))
    TT(P, "dve", fr, t1, t2, ALU.add, (), (tk,))
    TT(P, "dve", fr, fr, den, ALU.mult, (), (tk,))
    TT(P, "dve", t1, a1i, lr, ALU.mult, (), (tk,))
    TT(P, "dve", t2, a1r, li, ALU.mult, (), (tk,))
    TT(P, "dve", fi, t1, t2, ALU.subtract, (), (tk,))
    TT(P, "dve", fi, fi, den, ALU.mult, (), (tk,))
    A.mark()
    br = al(1024).rearrange("p (g q) -> p g q", g=64)
    bi = al(1024).rearrange("p (g q) -> p g q", g=64)
    for q8_ in range(8):
        gs = slice(q8_ * 8, (q8_ + 1) * 8)
        P.dma("sp", br[:, gs, :], prm["b_re"][0][gs].rearrange("g n p -> n g p"), (), (tk,))
        P.dma("sp", bi[:, gs, :], prm["b_im"][0][gs].rearrange("g n p -> n g p"), (), (tk,))
    u1 = al(1024).rearrange("p (g q) -> p g q", g=64)
    u2 = al(1024).rearrange("p (g q) -> p g q", g=64)
    frb = fr.unsqueeze(2).to_broadcast([64, 64, 16])
    fib = fi.unsqueeze(2).to_broadcast([64, 64, 16])
    TT(P, "dve", u1, br, frb, ALU.mult, (), (tk,))
    TT(P, "dve", u2, bi, fib, ALU.mult, (), (tk,))
    TT(P, "dve", bbr, u1, u2, ALU.subtract, (), (tk,))
    TT(P, "dve", u1, bi, frb, ALU.mult, (), (tk,))
    TT(P, "dve", u2, br, fib, ALU.mult, (), (tk,))
    TT(P, "dve", bbi, u1, u2, ALU.add, (), (tk,))
    A.release()
    if C.dbg is not None:
        P.dma("pool", C.dbg["fr"], fr, (tk,), ())
        P.dma("pool", C.dbg["fi"], fi, (tk,), ())
        P.dma("pool", C.dbg["bbr"], bbr, (tk,), ())
        P.dma("pool", C.dbg["cr"], cr, (tk,), ())

    Hr = al(64 * 64, BF16).rearrange("p (g c) -> p g c", g=64)
    Hi = al(64 * 64, BF16).rearrange("p (g c) -> p g c", g=64)
    H_t = Tok("H")
    A.mark()
    Sall = al(2 * 64 * 64).rearrange("p (r g c) -> p r g c", r=2, g=64)
    S_t = Tok("Sall")
    A.mark()
    U = A.alloc(64 * 512, BF16).rearrange("p (g m) -> p g m", g=64)
    U_t = Tok("U")
    for g8 in range(8):
        P.dma("sp", U[:, g8 * 8:(g8 + 1) * 8, :], Ud[g8 * 8:(g8 + 1) * 8].rearrange("g j p m -> (j p) g m"),
              (C.Ud_t,), (U_t,))
    tm = [al(NEA * 16).rearrange("p (e q) -> p e q", q=16) for _ in range(4)]
    tm_t = toks(4, "tm")
    ABr = [al(NEA * 16, BF16).rearrange("p (e q) -> p e q", q=16) for _ in range(2)]
    ABi = [al(NEA * 16, BF16).rearrange("p (e q) -> p e q", q=16) for _ in range(2)]
    AB_t = toks(2, "AB")
    Gsb = [A.alloc(16 * 64, BF16).rearrange("p (j n) -> p j n", j=16) for _ in range(2)]
    G_t = toks(2, "G")
    shpA = [64, NEA, 16]
    for g in range(64):
        gi = g % GB
        if gi == 0:
            s5_powers(P, C, W, th, rho, ex, g, tk)
        b = g % 2
        PrA = W["Pr"][:, gi, 0:NEA].unsqueeze(2).to_broadcast(shpA)
        PiA = W["Pi"][:, gi, 0:NEA].unsqueeze(2).to_broadcast(shpA)
        bR = bbr[:, g, :].unsqueeze(1).to_broadcast(shpA)
        bI = bbi[:, g, :].unsqueeze(1).to_broadcast(shpA)
        TT(P, "dve", tm[0], PrA, bR, ALU.mult, (tk,), (tm_t[0],))
        TT(P, "dve", tm[1], PiA, bI, ALU.mult, (tk,), (tm_t[1],))
        TT(P, "dve", tm[2], PrA, bI, ALU.mult, (tk,), (tm_t[2],))
        TT(P, "dve", tm[3], PiA, bR, ALU.mult, (tk,), (tm_t[3],))
        TT(P, "pool", ABr[b], tm[0], tm[1], ALU.subtract, (tm_t[0], tm_t[1]), (AB_t[b],))
        TT(P, "pool", ABi[b], tm[2], tm[3], ALU.add, (tm_t[2], tm_t[3]), (AB_t[b],))
        ptr = ps[2].bitcast(BF16)
        for jj in range(8):
            for ri, AB in enumerate((ABr[b], ABi[b])):
                TR(P, ptr[:, (2 * jj + ri) * 64:(2 * jj + ri + 1) * 64],
                   AB[:, 8 * jj:8 * jj + 8, :].rearrange("p e q -> p (e q)"), ident[0:64, 0:64],
                   (AB_t[b], tk), (ps_t[2],))
        CP(P, "act", Gsb[b], ptr[:, 0:1024].rearrange("p (j n) -> p j n", j=16), (ps_t[2],), (G_t[b],))
        pS, pS_t = ps[3 + g % 2], ps_t[3 + g % 2]
        for ri in range(2):
            for jj in range(8):
                MM(P, pS[0:64, ri * 64:(ri + 1) * 64], Gsb[b][:, 2 * jj + ri, :], U[:, g, jj::8], jj == 0, jj == 7,
                   (G_t[b], U_t), (pS_t,))
        CP(P, "dve", Sall[:, :, g, :], pS[0:64, 0:128].rearrange("p (r c) -> p r c", r=2), (pS_t,), (S_t,))
        if S5_BARRIERS:
            P.barrier()
    A.release()
    if C.dbg is not None:
        P.dma("pool", C.dbg["S"], Sall, (S_t,), ())
    S2 = al(2 * 64 * 64).rearrange("p (r g c) -> p r g c", r=2, g=64)
    q = [al(64 * 64).rearrange("p (g c) -> p g c", g=64) for _ in range(2)]
    cur, nxt = Sall, S2

    def v4(x, r, lo, hi):
        return x[:, r, :, :].rearrange("p g (s c) -> p g s c", s=2)[:, :, :, lo:hi]

    def q4(x, n):
        return x.rearrange("p g (s c) -> p g s c", s=2)[:, :, :, 0:n]

    Ar, Ai = a64r, a64i
    for lev in range(5):
        d = 1 << lev
        n = 32 - d
        shp = [64, 64, 2, n]
        Arb = Ar.unsqueeze(2).unsqueeze(3).to_broadcast(shp)
        Aib = Ai.unsqueeze(2).unsqueeze(3).to_broadcast(shp)
        CP(P, "dve", v4(nxt, 0, 0, d), v4(cur, 0, 0, d), (S_t, tk), (S_t,))
        CP(P, "dve", v4(nxt, 1, 0, d), v4(cur, 1, 0, d), (), (S_t,))
        TT(P, "dve", q4(q[0], n), v4(cur, 0, 0, n), Arb, ALU.mult, (), (S_t,))
        TT(P, "dve", q4(q[1], n), v4(cur, 1, 0, n), Aib, ALU.mult, (), (S_t,))
        TT(P, "dve", v4(nxt, 0, d, 32), v4(cur, 0, d, 32), q4(q[0], n), ALU.add, (), (S_t,))
        TT(P, "dve", v4(nxt, 0, d, 32), v4(nxt, 0, d, 32), q4(q[1], n), ALU.subtract, (), (S_t,))
        TT(P, "dve", q4(q[0], n), v4(cur, 1, 0, n), Arb, ALU.mult, (), (S_t,))
        TT(P, "dve", q4(q[1], n), v4(cur, 0, 0, n), Aib, ALU.mult, (), (S_t,))
        TT(P, "dve", v4(nxt, 1, d, 32), v4(cur, 1, d, 32), q4(q[0], n), ALU.add, (), (S_t,))
        TT(P, "dve", v4(nxt, 1, d, 32), v4(nxt, 1, d, 32), q4(q[1], n), ALU.add, (), (S_t,))
        cur, nxt = nxt, cur
        if lev < 4:
            TT(P, "dve", t1, Ar, Ar, ALU.mult, (), (tk,))
            TT(P, "dve", t2, Ai, Ai, ALU.mult, (), (tk,))
            TT(P, "dve", den, Ar, Ai, ALU.mult, (), (tk,))
            TT(P, "dve", Ar, t1, t2, ALU.subtract, (), (tk,))
            TS(P, "dve", Ai, den, 2.0, None, ALU.mult, None, (), (tk,))

    def h4(x, lo, hi):
        return x.rearrange("p g (s c) -> p g s c", s=2)[:, :, :, lo:hi]
    MSET(P, "pool", Hr, 0.0, (), (H_t,))
    MSET(P, "pool", Hi, 0.0, (), (H_t,))
    CP(P, "dve", h4(Hr, 1, 32), v4(cur, 0, 0, 31), (S_t,), (H_t,))
    TS(P, "dve", h4(Hi, 1, 32), v4(cur, 1, 0, 31), -1.0, None, ALU.mult, None, (S_t,), (H_t,))
    if C.dbg is not None:
        P.dma("pool", C.dbg["Hr"], Hr, (H_t,), ())
        P.dma("pool", C.dbg["Hi"], Hi, (H_t,), ())
    A.release()
    P.barrier()
    A.mark()
    U = A.alloc(64 * 512, BF16).rearrange("p (g m) -> p g m", g=64)
    U_t = Tok("U2")
    for g8 in range(8):
        P.dma("sp", U[:, g8 * 8:(g8 + 1) * 8, :], Ud[g8 * 8:(g8 + 1) * 8].rearrange("g j p m -> (j p) g m"),
              (C.Ud_t,), (U_t,))
    Yrb = [al(128, BF16) for _ in range(2)]
    Yib = [al(128, BF16) for _ in range(2)]
    Yp_t = toks(2, "Yp")
    ty = [al(128).rearrange("p (e q) -> p e q", q=16) for _ in range(4)]
    ty_t = toks(4, "ty")
    shpY = [64, 8, 16]
    shpC = [64, NEC, 16]
    tc_ = [al(NEC * 16).rearrange("p (e q) -> p e q", q=16) for _ in range(4)]
    tc_t = toks(4, "tc")
    CAr = [al(NEC * 16, BF16).rearrange("p (e q) -> p e q", q=16) for _ in range(2)]
    CAi = [al(NEC * 16, BF16).rearrange("p (e q) -> p e q", q=16) for _ in range(2)]
    CA_t = toks(2, "CA")
    Tsb = [A.alloc(8 * 128, BF16).rearrange("p (d c) -> p d c", d=8) for _ in range(2)]
    T_t = toks(2, "T")
    Ysb = [A.alloc(8 * 512, F32).rearrange("p (g m) -> p g m", g=8) for _ in range(2)]
    Yo_t = toks(2, "Yo")
    ytmp = [A.alloc(512, F32) for _ in range(2)]
    ytmp_t = toks(2, "ytmp")
    for g in range(64):
        gi = g % GB
        if gi == 0:
            s5_powers(P, C, W, th, rho, ex, g, tk)
        b = g % 2
        PrY = W["Pr"][:, gi, 63:71].unsqueeze(2).to_broadcast(shpY)
        PiY = W["Pi"][:, gi, 63:71].unsqueeze(2).to_broadcast(shpY)
        bR = bbr[:, g, :].unsqueeze(1).to_broadcast(shpY)
        bI = bbi[:, g, :].unsqueeze(1).to_broadcast(shpY)
        TT(P, "dve", ty[0], PrY, bR, ALU.mult, (tk,), (ty_t[0],))
        TT(P, "dve", ty[1], PiY, bI, ALU.mult, (tk,), (ty_t[1],))
        TT(P, "dve", ty[2], PrY, bI, ALU.mult, (tk,), (ty_t[2],))
        TT(P, "dve", ty[3], PiY, bR, ALU.mult, (tk,), (ty_t[3],))
        TT(P, "pool", Yrb[b].rearrange("p (e q) -> p e q", q=16), ty[0], ty[1], ALU.subtract,
           (ty_t[0], ty_t[1]), (Yp_t[b],))
        STT(P, Yib[b].rearrange("p (e q) -> p e q", q=16), ty[2], -1.0, ty[3], ALU.mult, ALU.subtract,
            (ty_t[2], ty_t[3]), (Yp_t[b],))
        PrC = W["Pr"][:, gi, NEA:NE].unsqueeze(2).to_broadcast(shpC)
        PiC = W["Pi"][:, gi, NEA:NE].unsqueeze(2).to_broadcast(shpC)
        cR = cr[:, g, :].unsqueeze(1).to_broadcast(shpC)
        cI = ci[:, g, :].unsqueeze(1).to_broadcast(shpC)
        TT(P, "dve", tc_[0], PrC, cR, ALU.mult, (tk,), (tc_t[0],))
        TT(P, "dve", tc_[1], PiC, cI, ALU.mult, (tk,), (tc_t[1],))
        TT(P, "dve", tc_[2], PrC, cI, ALU.mult, (tk,), (tc_t[2],))
        TT(P, "dve", tc_[3], PiC, cR, ALU.mult, (tk,), (tc_t[3],))
        TT(P, "pool", CAr[b], tc_[0], tc_[1], ALU.subtract, (tc_t[0], tc_t[1]), (CA_t[b],))
        TT(P, "pool", CAi[b], tc_[2], tc_[3], ALU.add, (tc_t[2], tc_t[3]), (CA_t[b],))
        for half in range(2):
            pT_, pT_t = ps[half], ps_t[half]
            MM(P, pT_, Yrb[b], CAr[b][:, 32 * half:32 * half + 32, :].rearrange("p e q -> p (e q)"),
               True, False, (Yp_t[b], CA_t[b]), (pT_t,))
            MM(P, pT_, Yib[b], CAi[b][:, 32 * half:32 * half + 32, :].rearrange("p e q -> p (e q)"),
               False, True, (Yp_t[b], CA_t[b]), (pT_t,))
            CP(P, "act", Tsb[b][:, 4 * half:4 * half + 4, :], pT_.rearrange("p (d c) -> p d c", d=4),
               (pT_t,), (T_t[b],))
        TT(P, "dve", Tsb[b][:, 0, :], Tsb[b][:, 0, :], mask0, ALU.mult, (tk,), (T_t[b],))
        pY, pY_t = ps[4 + g % 2], ps_t[4 + g % 2]
        for jji in range(8):
            o = pY[:, jji * 64:(jji + 1) * 64]
            MM(P, o, CAr[b][:, 8 * jji + 1:8 * jji + 9, :].rearrange("p e q -> p (e q)"), Hr[:, g, :], True, False,
               (CA_t[b], H_t), (pY_t,))
            MM(P, o, CAi[b][:, 8 * jji + 1:8 * jji + 9, :].rearrange("p e q -> p (e q)"), Hi[:, g, :], False, False,
               (CA_t[b], H_t), ())
            for jjo in range(jji + 1):
                MM(P, o, Tsb[b][:, jji - jjo, :], U[:, g, jjo::8], False, jjo == jji,
                   (T_t[b], U_t), (pY_t,) if (jji == 7 and jjo == 7) else ())
        yb = (g // 8) % 2
        CP(P, "act", ytmp[b], pY, (pY_t,), (ytmp_t[b],))
        CP(P, "pool", Ysb[yb][:, g % 8, :].rearrange("p (c j) -> p j c", j=8),
           ytmp[b].rearrange("p (j c) -> p j c", j=8), (ytmp_t[b],), (Yo_t[yb],))
        if S5_BARRIERS:
            P.barrier()
        if g % 8 == 7:
            g0 = g - 7
            P.dma("pool", Yd[g0:g0 + 8].rearrange("g i p m -> (i p) g m"), Ysb[yb], (Yo_t[yb],), (C.Yd_t,))
    A.release()
    A.release()
    P.barrier()


def s5_post_phase(P, C, prm, Yd, uT_d, yT, ntok):
    nc, A, ps, ps_t = C.nc, C.A, C.ps, C.ps_t
    A.mark()
    tk = Tok("s5q")
    dsk = A.alloc(8, F32)
    bgl = A.alloc(8, F32)
    P.dma("sp", dsk, prm["d"][0].rearrange("(a p) -> p a", p=128), (), (tk,), allow_slow_non_contiguous=True)
    P.dma("sp", bgl, prm["b_glu"][0].rearrange("(a p) -> p a", p=128), (), (tk,), allow_slow_non_contiguous=True)
    zT = A.alloc(8 * SEQ, BF16).rearrange("p (a t) -> p a t", a=8)
    z_t = toks(8, "z")
    ring = make_ring(A, nst=3, nwb=3, wb_cols=8 * 128)
    ysb = [A.alloc(SEQ, F32).rearrange("p (j m) -> p j m", j=8) for _ in range(2)]
    ys_t = toks(2, "ys")
    usb = [A.alloc(SEQ, BF16) for _ in range(2)]
    us_t = toks(2, "us")
    zp = [A.alloc(SEQ, F32) for _ in range(2)]
    zp_t = toks(2, "zp")
    w_ = [A.alloc(SEQ, F32) for _ in range(2)]
    w_t = toks(2, "w")
    gate = [A.alloc(512, BF16) for _ in range(2)]
    gate_t = toks(2, "gate")
    ybst = [A.alloc(SEQ, BF16) for _ in range(2)]
    yb_t = toks(2, "yb")
    ng = 0
    for s in range(ntok // SEQ):
        s0 = s * SEQ
        for a in range(8):
            b = a % 2
            for gg in range(8):
                P.dma("sp", ysb[b][gg * 16:(gg + 1) * 16, :, :],
                      Yd[a * 8 + gg, :, :, s * 256:(s + 1) * 256].rearrange("j p m -> p j m"), (C.Yd_t,), (ys_t[b],))
            P.dma("sp", usb[b], uT_d[a * 128:(a + 1) * 128, s0:s0 + SEQ], (C.uT_t,), (us_t[b],))
            STT(P, zp[b].rearrange("p (m j) -> p j m", j=8), usb[b].rearrange("p (m j) -> p j m", j=8),
                dsk[:, a:a + 1], ysb[b], ALU.mult, ALU.add, (us_t[b], ys_t[b], tk), (zp_t[b],))
            TT(P, "pool", w_[b], zp[b], zp[b], ALU.mult, (zp_t[b],), (w_t[b],))
            TS(P, "dve", w_[b], w_[b], 0.044715, 1.0, ALU.mult, ALU.add, (), (w_t[b],))
            TT(P, "pool", w_[b], w_[b], zp[b], ALU.mult, (zp_t[b],), (w_t[b],))
            ACTV(P, w_[b], w_[b], AF.Sigmoid, (), (w_t[b],), scale=1.5957691216057308)
            TT(P, "dve", zT[:, a, :], zp[b], w_[b], ALU.mult, (zp_t[b], w_t[b]), (z_t[a],))
        for oc in range(8):
            wb, wb_t = stream_w(P, C, prm["w_glu"][0][:, oc * 128:(oc + 1) * 128], 8, 128, ring)
            yb = oc % 2
            for blk in range(4):
                pp, pp_t = ps[blk % 2], ps_t[blk % 2]
                for k in range(8):
                    MM(P, pp, wb[:, k * 128:(k + 1) * 128], zT[:, k, blk * 512:(blk + 1) * 512], k == 0, k == 7,
                       (wb_t, z_t[k]), (pp_t,) if k in (0, 7) else ())
                gi = ng % 2
                ng += 1
                ACTV(P, gate[gi], pp, AF.Sigmoid, (pp_t, tk), (gate_t[gi],), bias=bgl[:, oc:oc + 1])
                TT(P, "dve", ybst[yb][:, blk * 512:(blk + 1) * 512], zT[:, oc, blk * 512:(blk + 1) * 512], gate[gi],
                   ALU.mult, (gate_t[gi], z_t[oc]), (yb_t[yb],))
            P.dma("pool", yT[1024 + oc * 128:1024 + (oc + 1) * 128, s0:s0 + SEQ], ybst[yb], (yb_t[yb],), (C.yT_t,))
    A.release()
    P.barrier()


def host_consts_s5(c):
    ex = np.concatenate([63.0 - np.arange(NEA), np.arange(NEC)]).astype(np.float32)
    c["s5_ex"] = ex[None, :]
    m = np.zeros((128, 128), np.float32)
    for j in range(8):
        for i in range(8):
            if i >= j:
                m[j * 16:(j + 1) * 16, i * 16:(i + 1) * 16] = 1.0
    c["s5_mask"] = m
    return c


CONST_SHAPES.update({"s5_ex": [1, NE], "s5_mask": [128, 128]})


def all_host_consts():
    c = host_consts()
    host_consts_ret(c)
    host_consts_s5(c)
    return c


def build_mix0_test(ntok, phases=(1, 1, 1)):
    nc = bass.Bass("TRN2", target_bir_lowering=False)
    hT = nc.dram_tensor("hT", [D, ntok], F32, kind="ExternalInput").ap()
    g = nc.dram_tensor("g", [D], F32, kind="ExternalInput").ap()
    w_in = nc.dram_tensor("w_in", [D, 5120], F32, kind="ExternalInput").ap()
    wo = nc.dram_tensor("wo", [D, D], F32, kind="ExternalInput").ap()
    prm = {}
    for n, shp in (("lam_re", [1, 64, 64]), ("lam_im", [1, 64, 64]), ("log_step", [1, 64]),
                   ("b_re", [1, 64, 64, 16]), ("b_im", [1, 64, 64, 16]), ("c_re", [1, 64, 16, 64]),
                   ("c_im", [1, 64, 16, 64]), ("d", [1, 1024]), ("w_glu", [1, 1024, 1024]), ("b_glu", [1, 1024])):
        prm[n] = nc.dram_tensor(n, shp, F32, kind="ExternalInput").ap()
    cst = {k: nc.dram_tensor(k, v, F32, kind="ExternalInput").ap() for k, v in CONST_SHAPES.items()
           if not k.startswith("att_")}
    yT = nc.dram_tensor("yT", [D, ntok], BF16, kind="ExternalOutput").ap()
    uT_d = nc.dram_tensor("uT_d", [1024, ntok], BF16, kind="ExternalOutput").ap()
    nm = ntok // 8
    Ud = nc.dram_tensor("Ud", [64, 8, 16, 512], BF16, kind="ExternalOutput").ap()
    Yd = nc.dram_tensor("Yd", [64, 8, 16, 512], F32, kind="ExternalOutput").ap()
    oT = nc.dram_tensor("oT", [D, ntok], F32, kind="ExternalOutput").ap()
    P = Prog(nc)
    C = setup_common(nc, P)
    C.yT_t, C.uT_t, C.Ud_t, C.Yd_t = Tok("yT"), Tok("uT"), Tok("Ud"), Tok("Yd")
    import os
    C.skip_ud = bool(os.environ.get("SKIP_UD"))
    C.skip_ret = bool(os.environ.get("SKIP_RET"))
    C.skip_u = bool(os.environ.get("SKIP_U"))
    def dd(n, shp, dt=F32):
        return nc.dram_tensor(n, shp, dt, kind="ExternalOutput").ap()
    C.dbg = {"pr": dd("d_pr", [64, GB, NE]), "pi": dd("d_pi", [64, GB, NE]), "fr": dd("d_fr", [64, 64]),
             "fi": dd("d_fi", [64, 64]), "bbr": dd("d_bbr", [64, 64, 16]), "cr": dd("d_cr", [64, 64, 16]),
             "S": dd("d_S", [64, 2, 64, 64]), "Hr": dd("d_Hr", [64, 64, 64], BF16), "Hi": dd("d_Hi", [64, 64, 64], BF16)}
    if phases[0]:
        mix0a_phase(P, C, hT, g, w_in, cst, yT, uT_d, Ud, ntok)
    if phases[1]:
        s5_phase(P, C, prm, cst, Ud, Yd)
    if phases[2]:
        s5_post_phase(P, C, prm, Yd, uT_d, yT, ntok)
    out_proj_phase(P, C, yT, wo, hT, oT, ntok)
    P.finish()
    P.emit()
    return nc, P


def final_norm_phase(P, C, h_in, gain_dram, outT, ntok):
    A, ps, ps_t = C.A, C.ps, C.ps_t
    A.mark()
    gain, gain_t = load_gain(P, C, gain_dram)
    xk = [A.alloc(2 * 512, F32) for _ in range(3)]
    xk_t = toks(3, "fxk")
    sq = [A.alloc(2 * 512, BF16) for _ in range(2)]
    sq_t = toks(2, "fsq")
    rstd = A.alloc(512, F32)
    rstd_t = Tok("frstd")
    ssq, ssq_t = ps[7][:, 0:512], ps_t[7]
    n = 0
    for blk in range(ntok // 512):
        c0 = blk * 512
        for k2 in range(KD // 2):
            b = n % 3
            n += 1
            P.dma("sp", xk[b].rearrange("p (k t) -> p k t", k=2),
                  h_in[k2 * 256:(k2 + 1) * 256, c0:c0 + 512].rearrange("(k p) t -> p k t", p=128), (), (xk_t[b],))
            s = k2 % 2
            ACTV(P, sq[s], xk[b], AF.Square, (xk_t[b],), (sq_t[s],))
            for j in range(2):
                k = k2 * 2 + j
                MM(P, ssq, C.ones_bf, sq[s][:, j * 512:(j + 1) * 512], k == 0, k == KD - 1,
                   (sq_t[s], C.const_t), (ssq_t,) if k in (0, KD - 1) else ())
        TS(P, "dve", rstd, ssq, 1.0 / D, EPS, ALU.mult, ALU.add, (ssq_t,), (rstd_t,))
        rstd_inplace(P, rstd, rstd_t)
        for k2 in range(KD // 2):
            b = n % 3
            n += 1
            P.dma("sp", xk[b].rearrange("p (k t) -> p k t", k=2),
                  h_in[k2 * 256:(k2 + 1) * 256, c0:c0 + 512].rearrange("(k p) t -> p k t", p=128), (), (xk_t[b],))
            for j in range(2):
                k = k2 * 2 + j
                STT(P, xk[b][:, j * 512:(j + 1) * 512], xk[b][:, j * 512:(j + 1) * 512], gain[:, k:k + 1], rstd,
                    ALU.mult, ALU.mult, (xk_t[b], rstd_t, gain_t), (xk_t[b],))
            P.dma("pool", outT[k2 * 256:(k2 + 1) * 256, c0:c0 + 512].rearrange("(k p) t -> p k t", p=128),
                  xk[b].rearrange("p (k t) -> p k t", k=2), (xk_t[b],), ())
    A.release()
    P.barrier()


IN_SHAPES = {
    "ffn_norm": [2, 2, D], "ffn_w_gate": [2, 2, D, DFF], "ffn_w_up": [2, 2, D, DFF], "ffn_w_down": [2, 2, DFF, D],
    "mix_norm": [2, D], "ab_w_in": [1, D, 5120], "ab_w_out": [1, D, D],
    "ssm_lambda_re": [1, 64, 64], "ssm_lambda_im": [1, 64, 64], "ssm_log_step": [1, 64],
    "ssm_b_re": [1, 64, 64, 16], "ssm_b_im": [1, 64, 64, 16], "ssm_c_re": [1, 64, 16, 64], "ssm_c_im": [1, 64, 16, 64],
    "ssm_d": [1, 1024], "ssm_w_glu": [1, 1024, 1024], "ssm_b_glu": [1, 1024],
    "c_w_qkv": [1, D, 3 * D], "c_w_out": [1, D, D], "c_lambda_q1": [1, 128], "c_lambda_k1": [1, 128],
    "c_lambda_q2": [1, 128], "c_lambda_k2": [1, 128], "c_subln": [1, 256], "final_norm": [D],
}
SCRATCH_KIND = "Internal"


def build_full(ntok=TOK):
    nc = bass.Bass("TRN2", target_bir_lowering=False)
    xT = nc.dram_tensor("xT", [D, ntok], F32, kind="ExternalInput").ap()
    w = {k: nc.dram_tensor(k, v, F32, kind="ExternalInput").ap() for k, v in IN_SHAPES.items()}
    cst = {k: nc.dram_tensor(k, v, F32, kind="ExternalInput").ap() for k, v in CONST_SHAPES.items()}
    outT = nc.dram_tensor("outT", [D, ntok], F32, kind="ExternalOutput").ap()
    hA = nc.dram_tensor("hA", [D, ntok], F32, kind=SCRATCH_KIND).ap()
    hB = nc.dram_tensor("hB", [D, ntok], F32, kind=SCRATCH_KIND).ap()
    yT = nc.dram_tensor("yT", [D, ntok], BF16, kind=SCRATCH_KIND).ap()
    uT_d = nc.dram_tensor("uT_d", [1024, ntok], BF16, kind=SCRATCH_KIND).ap()
    Ud = nc.dram_tensor("Ud", [64, 8, 16, 512], BF16, kind=SCRATCH_KIND).ap()
    Yd = nc.dram_tensor("Yd", [64, 8, 16, 512], F32, kind=SCRATCH_KIND).ap()
    P = Prog(nc)
    C = setup_common(nc, P)
    C.yT_t, C.uT_t, C.Ud_t, C.Yd_t = Tok("yT"), Tok("uT"), Tok("Ud"), Tok("Yd")
    prm = {"lam_re": w["ssm_lambda_re"], "lam_im": w["ssm_lambda_im"], "log_step": w["ssm_log_step"],
           "b_re": w["ssm_b_re"], "b_im": w["ssm_b_im"], "c_re": w["ssm_c_re"], "c_im": w["ssm_c_im"],
           "d": w["ssm_d"], "w_glu": w["ssm_w_glu"], "b_glu": w["ssm_b_glu"]}

    def ffn(l, j, src, dst):
        ffn_phase(P, C, src, dst, w["ffn_norm"][l, j], w["ffn_w_gate"][l, j], w["ffn_w_up"][l, j],
                  w["ffn_w_down"][l, j], ntok)

    ffn(0, 0, xT, hA)
    mix0a_phase(P, C, hA, w["mix_norm"][0], w["ab_w_in"][0], cst, yT, uT_d, Ud, ntok)
    s5_phase(P, C, prm, cst, Ud, Yd)
    s5_post_phase(P, C, prm, Yd, uT_d, yT, ntok)
    out_proj_phase(P, C, yT, w["ab_w_out"][0], hA, hB, ntok)
    ffn(0, 1, hB, hA)
    ffn(1, 0, hA, hB)
    attn_phase(P, C, hB, w["mix_norm"][1], w["c_w_qkv"][0], w["c_lambda_q1"], w["c_lambda_k1"], w["c_lambda_q2"],
               w["c_lambda_k2"], w["c_subln"], cst, yT, ntok)
    out_proj_phase(P, C, yT, w["c_w_out"][0], hB, hA, ntok)
    ffn(1, 1, hA, hB)
    final_norm_phase(P, C, hB, w["final_norm"], outT, ntok)
    P.finish()
    P.emit()
    return nc, P


_CACHE = {}


def kernel(**inputs):
    x = np.asarray(inputs["x"], dtype=np.float32)
    B = x.shape[0]
    per = B // NCORES
    if "nc" not in _CACHE:
        _CACHE["nc"] = build_full(per * SEQ)[0]
        _CACHE["cst"] = all_host_consts()
    nc = _CACHE["nc"]
    shared = {k: np.ascontiguousarray(np.asarray(inputs[k], dtype=np.float32)) for k in IN_SHAPES}
    shared.update(_CACHE["cst"])
    in_maps = []
    for c in range(NCORES):
        xc = x[c * per:(c + 1) * per].reshape(per * SEQ, D)
        m = dict(shared)
        m["xT"] = np.ascontiguousarray(xc.T)
        in_maps.append(m)
    res = run_bass_kernel_spmd(nc, in_maps, core_ids=list(range(NCORES)))
    out = np.empty((B, SEQ, D), np.float32)
    for c in range(NCORES):
        oT = np.asarray(res.results[c]["outT"])
        out[c * per:(c + 1) * per] = np.ascontiguousarray(oT.T).reshape(per, SEQ, D)
    return out
```

```python
import math
import contextlib
import numpy as np
import concourse.bass as bass
import concourse.mybir as mybir
from concourse.bass_utils import run_bass_kernel_spmd

F32 = mybir.dt.float32
BF16 = mybir.dt.bfloat16
I32 = mybir.dt.int32
U8 = mybir.dt.uint8
ALU = mybir.AluOpType
AF = mybir.ActivationFunctionType
AX = mybir.AxisListType

D = 2048
DFF = 5504
NFF = DFF // 128
KD = D // 128
SEQ = 2048
NCORES = 8
TOK = 2 * SEQ
EPS = 1e-6

COMPUTE = ("pe", "act", "dve", "pool")
ENG_ATTR = {"pe": "tensor", "act": "scalar", "dve": "vector", "pool": "gpsimd", "sp": "sync"}


class Tok:
    __slots__ = ("name", "w", "r")

    def __init__(self, name=""):
        self.name = name
        self.w = []
        self.r = []


def toks(n, name=""):
    return [Tok(f"{name}{i}") for i in range(n)]


class Op:
    __slots__ = ("eng", "fn", "deps", "dma", "signal", "sem", "val", "know")

    def __init__(self, eng, fn, dma):
        self.eng = eng
        self.fn = fn
        self.dma = dma
        self.deps = []
        self.signal = False
        self.sem = None
        self.val = 0
        self.know = None


class Prog:
    def __init__(self, nc):
        self.nc = nc
        self.ops = []
        self.last = {}
        self.pending = {}
        self.dma_ops = []
        self.n_dma_sems = {"sp": 30, "pool": 16, "act": 8}

    def op(self, eng, fn, reads=(), writes=(), dma=False):
        o = Op(eng, fn, dma)
        deps = []
        for t in reads:
            deps.extend(t.w)
        for t in writes:
            deps.extend(t.w)
            for x in t.r:
                if x.dma or dma or x.eng != eng:
                    deps.append(x)
        if eng in self.pending:
            deps.extend(self.pending.pop(eng))
        seen = set()
        for d in deps:
            if id(d) in seen:
                continue
            seen.add(id(d))
            if (not d.dma) and (not dma) and d.eng == "pe" and eng == "pe":
                continue
            o.deps.append(d)
        for t in reads:
            if not dma:
                t.r = [x for x in t.r if x.dma or x.eng != eng]
            t.r.append(o)
        for t in writes:
            t.w = [o]
            t.r = []
        self.ops.append(o)
        if dma:
            self.dma_ops.append(o)
        else:
            self.last[eng] = o
        return o

    def dma(self, q, out, in_, reads=(), writes=(), **kw):
        return self.op(q, lambda e: e.dma_start(out=out, in_=in_, **kw), reads, writes, dma=True)

    def barrier(self):
        deps = list(self.last.values()) + self.dma_ops
        self.dma_ops = []
        for e in ENG_ATTR:
            self.pending[e] = list(deps) + self.pending.get(e, [])

    def finish(self):
        self.barrier()
        self.op("sp", lambda e: e.nop(), (), ())

    def emit(self):
        nc = self.nc
        ops = self.ops
        with contextlib.ExitStack() as st:
            sems = {e: st.enter_context(nc.semaphore("s_" + e)) for e in COMPUTE}
            dsems = {q: [st.enter_context(nc.semaphore(f"d_{q}{i}")) for i in range(n)]
                     for q, n in self.n_dma_sems.items()}
            dcnt = {q: [0] * n for q, n in self.n_dma_sems.items()}
            dnext = {q: 0 for q in self.n_dma_sems}
            dprev = {q: [None] * n for q, n in self.n_dma_sems.items()}
            for o in ops:
                if o.dma:
                    q = o.eng
                    i = dnext[q]
                    dnext[q] = (i + 1) % len(dsems[q])
                    prev = dprev[q][i]
                    if prev is not None:
                        o.deps.append(prev)
                    dcnt[q][i] += 16
                    o.sem = dsems[q][i]
                    o.val = dcnt[q][i]
                    dprev[q][i] = o
            for o in ops:
                for d in o.deps:
                    d.signal = True
            cnt = {e: 0 for e in COMPUTE}
            for o in ops:
                if (not o.dma) and o.signal:
                    cnt[o.eng] += 1
                    o.sem = sems[o.eng]
                    o.val = cnt[o.eng]
            know = {e: {} for e in ENG_ATTR}
            plan = {e: [] for e in ENG_ATTR}
            nw = 0
            for o in ops:
                kn = know[o.eng]
                for d in o.deps:
                    key = id(d.sem)
                    if kn.get(key, 0) >= d.val:
                        continue
                    plan[o.eng].append(("w", d.sem, d.val))
                    nw += 1
                    if d.know is not None:
                        for k2, v2 in d.know.items():
                            if kn.get(k2, 0) < v2:
                                kn[k2] = v2
                    if kn.get(key, 0) < d.val:
                        kn[key] = d.val
                plan[o.eng].append(("o", o))
                if o.dma or o.signal:
                    o.know = dict(kn)
                    if not o.dma:
                        o.know[id(o.sem)] = o.val
            self.n_waits = nw
            with nc.Block() as block:
                for ename, attr in ENG_ATTR.items():
                    lst = plan[ename]
                    if not lst:
                        continue

                    def body(eng, lst=lst):
                        for item in lst:
                            if item[0] == "w":
                                eng.wait_ge(item[1], item[2])
                            else:
                                o = item[1]
                                inst = o.fn(eng)
                                if o.dma:
                                    inst.then_inc(o.sem, 16)
                                elif o.signal:
                                    inst.then_inc(o.sem, 1)
                    getattr(block, attr)(body)


class Arena:
    def __init__(self, nc, nbytes, name="arena"):
        self.t = nc.alloc_sbuf_tensor(name, [128, nbytes], U8)
        self.n = nbytes
        self.off = 0
        self.marks = []

    def alloc(self, cols, dtype, parts=128):
        sz = {F32: 4, BF16: 2, I32: 4, U8: 1}[dtype]
        nb = cols * sz
        off = (self.off + 63) // 64 * 64
        if off + nb > self.n:
            raise RuntimeError(f"arena overflow: want {nb} at {off} of {self.n}")
        self.off = off + nb
        ap = self.t[0:parts, off:off + nb]
        if dtype != U8:
            ap = ap.bitcast(dtype)
        return ap

    def mark(self):
        self.marks.append(self.off)

    def release(self):
        self.off = self.marks.pop()


class Ctx:
    pass


def cast_op(P, i, out, in_, reads, writes):
    if i % 2 == 0:
        return P.op("act", lambda e: e.copy(out=out, in_=in_), reads, writes)
    return P.op("dve", lambda e: e.tensor_copy(out=out, in_=in_), reads, writes)


def load_gain(P, C, gain_dram):
    A = C.A
    g = A.alloc(KD, F32)
    t = Tok("gain")
    with C.nc.allow_non_contiguous_dma(reason="tiny gain load"):
        pass
    P.dma("sp", g, gain_dram.rearrange("(k p) -> p k", p=128), (), (t,),
          allow_slow_non_contiguous=True)
    return g, t


def norm_T(P, C, src, tok0, T, gain, gain_t, xnT, xn_toks, src_tok=None):
    A, ps = C.A, C.ps
    A.mark()
    xk = [A.alloc(2 * 512, F32) for _ in range(2)]
    xk_t = toks(2, "xk")
    sq = [A.alloc(2 * 512, BF16) for _ in range(2)]
    sq_t = toks(2, "sq")
    rstd = A.alloc(512, F32)
    rstd_t = Tok("rstd")
    ssq = ps[7][:, 0:512]
    ssq_t = C.ps_t[7]
    srcr = [src_tok] if src_tok is not None else []
    n = 0
    for blk in range(T // 512):
        c0 = tok0 + blk * 512
        for k2 in range(KD // 2):
            b = n % 2
            n += 1
            P.dma("sp", xk[b].rearrange("p (k t) -> p k t", k=2),
                  src[k2 * 256:(k2 + 1) * 256, c0:c0 + 512].rearrange("(k p) t -> p k t", p=128),
                  srcr, (xk_t[b],))
            P.op("act", lambda e, b=b: e.activation(out=sq[b], in_=xk[b], func=AF.Square),
                 (xk_t[b],), (sq_t[b],))
            for j in range(2):
                k = k2 * 2 + j
                P.op("pe", lambda e, b=b, j=j, k=k: e.matmul(
                    ssq, lhsT=C.ones_bf, rhs=sq[b][:, j * 512:(j + 1) * 512],
                    start=(k == 0), stop=(k == KD - 1)),
                    (sq_t[b], C.const_t), (ssq_t,) if k in (0, KD - 1) else ())
        P.op("dve", lambda e: e.tensor_scalar(out=rstd, in0=ssq, scalar1=1.0 / D, scalar2=EPS,
                                              op0=ALU.mult, op1=ALU.add), (ssq_t,), (rstd_t,))
        P.op("act", lambda e: e.activation(out=rstd, in_=rstd, func=AF.Sqrt), (), (rstd_t,))
        P.op("dve", lambda e: e.reciprocal(out=rstd, in_=rstd), (), (rstd_t,))
        for k2 in range(KD // 2):
            b = n % 2
            n += 1
            P.dma("sp", xk[b].rearrange("p (k t) -> p k t", k=2),
                  src[k2 * 256:(k2 + 1) * 256, c0:c0 + 512].rearrange("(k p) t -> p k t", p=128),
                  srcr, (xk_t[b],))
            for j in range(2):
                k = k2 * 2 + j
                P.op("dve", lambda e, b=b, j=j, k=k, blk=blk: e.scalar_tensor_tensor(
                    out=xnT[:, k, blk * 512:(blk + 1) * 512], in0=xk[b][:, j * 512:(j + 1) * 512],
                    scalar=gain[:, k:k + 1], in1=rstd, op0=ALU.mult, op1=ALU.mult),
                    (xk_t[b], rstd_t, gain_t), (xn_toks[blk],))
    A.release()


def stream_weight(P, C, w_dram_cols, K, ring, n, rows_per_piece=16):
    wb, wb_t = ring["wb"][ring["nb"] % len(ring["wb"])], ring["wb_t"][ring["nb"] % len(ring["wb"])]
    ring["nb"] += 1
    k0 = 0
    while k0 < K:
        kk = min(rows_per_piece, K - k0)
        s = ring["ns"] % len(ring["st"])
        ring["ns"] += 1
        st, st_t = ring["st"][s], ring["st_t"][s]
        P.dma("sp", st[:, 0:kk * 128].rearrange("p (k c) -> p k c", k=kk),
              w_dram_cols[k0 * 128:(k0 + kk) * 128, :].rearrange("(k p) c -> p k c", p=128),
              (), (st_t,))
        cast_op(P, ring["ns"], wb[:, k0 * 128:(k0 + kk) * 128], st[:, 0:kk * 128], (st_t,), (wb_t,))
        k0 += kk
    return wb, wb_t


def ffn_phase(P, C, h_in, h_out, gain_dram, wg, wu, wd, ntok, final_gain=None, out_final=None):
    A, ps, ps_t = C.A, C.ps, C.ps_t
    T = 1024
    NB = T // 512
    A.mark()
    gain, gain_t = load_gain(P, C, gain_dram)
    xnT = A.alloc(KD * T, BF16).rearrange("p (k t) -> p k t", k=KD)
    xn_t = toks(NB, "xn")
    hT = A.alloc(NFF * T, BF16).rearrange("p (c t) -> p c t", c=NFF)
    hT_t = [[Tok(f"h{c}_{b}") for b in range(NB)] for c in range(NFF)]
    ringB = dict(st=[A.alloc(16 * 128, F32) for _ in range(3)], st_t=toks(3, "st"),
                 wb=[A.alloc(16 * 128, BF16) for _ in range(4)], wb_t=toks(4, "wb"), ns=0, nb=0)
    ringC = dict(st=ringB["st"], st_t=ringB["st_t"],
                 wb=[A.alloc(NFF * 128, BF16) for _ in range(2)], wb_t=toks(2, "wd"), ns=0, nb=0)
    sg = [A.alloc(512, BF16) for _ in range(2)]
    sg_t = toks(2, "sg")
    xr = [A.alloc(512, F32) for _ in range(3)]
    xr_t = toks(3, "xr")
    nxr = 0
    nsg = 0
    for tile in range(ntok // T):
        tok0 = tile * T
        norm_T(P, C, h_in, tok0, T, gain, gain_t, xnT, xn_t)
        for c in range(NFF):
            wgb, wgb_t = stream_weight(P, C, wg[:, c * 128:(c + 1) * 128], KD, ringB, None)
            wub, wub_t = stream_weight(P, C, wu[:, c * 128:(c + 1) * 128], KD, ringB, None)
            for blk in range(NB):
                pg, pg_t = ps[2 * blk], ps_t[2 * blk]
                pu, pu_t = ps[2 * blk + 1], ps_t[2 * blk + 1]
                for (wb, wb_t, pp, pp_t) in ((wgb, wgb_t, pg, pg_t), (wub, wub_t, pu, pu_t)):
                    for k in range(KD):
                        P.op("pe", lambda e, wb=wb, pp=pp, k=k, blk=blk: e.matmul(
                            pp, lhsT=wb[:, k * 128:(k + 1) * 128], rhs=xnT[:, k, blk * 512:(blk + 1) * 512],
                            start=(k == 0), stop=(k == KD - 1)),
                            (wb_t, xn_t[blk]), (pp_t,) if k in (0, KD - 1) else ())
                s = nsg % 2
                nsg += 1
                P.op("act", lambda e, s=s, pg=pg: e.activation(out=sg[s], in_=pg, func=AF.Silu),
                     (pg_t,), (sg_t[s],))
                P.op("dve", lambda e, s=s, pu=pu, c=c, blk=blk: e.tensor_tensor(
                    out=hT[:, c, blk * 512:(blk + 1) * 512], in0=pu, in1=sg[s], op=ALU.mult),
                    (pu_t, sg_t[s]), (hT_t[c][blk],))
        for dc in range(KD):
            wdb, wdb_t = stream_weight(P, C, wd[:, dc * 128:(dc + 1) * 128], NFF, ringC, None)
            for blk in range(NB):
                pb = 4 + (dc * NB + blk) % 3
                po, po_t = ps[pb], ps_t[pb]
                r = nxr % 3
                nxr += 1
                c0 = tok0 + blk * 512
                P.dma("sp", xr[r], h_in[dc * 128:(dc + 1) * 128, c0:c0 + 512], (), (xr_t[r],))
                for c in range(NFF):
                    P.op("pe", lambda e, wdb=wdb, po=po, c=c, blk=blk: e.matmul(
                        po, lhsT=wdb[:, c * 128:(c + 1) * 128], rhs=hT[:, c, blk * 512:(blk + 1) * 512],
                        start=(c == 0), stop=(c == NFF - 1)),
                        (wdb_t, hT_t[c][blk]), (po_t,) if c in (0, NFF - 1) else ())
                P.op("dve", lambda e, po=po, r=r: e.scalar_tensor_tensor(
                    out=xr[r], in0=po, scalar=0.5, in1=xr[r], op0=ALU.mult, op1=ALU.add),
                    (po_t,), (xr_t[r],))
                P.dma("pool", h_out[dc * 128:(dc + 1) * 128, c0:c0 + 512], xr[r], (xr_t[r],), ())
    A.release()
    P.barrier()


def setup_common(nc, P):
    C = Ctx()
    C.nc = nc
    C.A = Arena(nc, 206 * 1024)
    C.dbg = None
    C.ps = []
    C.ps_t = toks(8, "ps")
    for i in range(8):
        t = nc.alloc_psum_tensor(f"ps{i}", [128, 512], F32)
        C.ps.append(t[:, :])
    A = C.A
    C.const_t = Tok("const")
    C.ones_bf = A.alloc(128, BF16)
    P.op("dve", lambda e: e.memset(C.ones_bf, 1.0), (), (C.const_t,))
    return C


def build_ffn_test(ntok):
    nc = bass.Bass("TRN2", target_bir_lowering=False)
    xT = nc.dram_tensor("xT", [D, ntok], F32, kind="ExternalInput").ap()
    g = nc.dram_tensor("g", [D], F32, kind="ExternalInput").ap()
    wg = nc.dram_tensor("wg", [D, DFF], F32, kind="ExternalInput").ap()
    wu = nc.dram_tensor("wu", [D, DFF], F32, kind="ExternalInput").ap()
    wd = nc.dram_tensor("wd", [DFF, D], F32, kind="ExternalInput").ap()
    oT = nc.dram_tensor("oT", [D, ntok], F32, kind="ExternalOutput").ap()
    P = Prog(nc)
    C = setup_common(nc, P)
    ffn_phase(P, C, xT, oT, g, wg, wu, wd, ntok)
    P.finish()
    P.emit()
    return nc, P


def make_ring(A, nst=3, nwb=3, wb_cols=16 * 128):
    return dict(st=[A.alloc(2048, F32) for _ in range(nst)], st_t=toks(nst, "st"),
                wb=[A.alloc(wb_cols, BF16) for _ in range(nwb)], wb_t=toks(nwb, "wb"), ns=0, nb=0)


def stream_w(P, C, w_cols, K, ncols, ring):
    i = ring["nb"] % len(ring["wb"])
    ring["nb"] += 1
    wb, wb_t = ring["wb"][i], ring["wb_t"][i]
    per = 2048 // ncols
    k0 = 0
    while k0 < K:
        kk = min(per, K - k0)
        s = ring["ns"] % len(ring["st"])
        ring["ns"] += 1
        st, st_t = ring["st"][s], ring["st_t"][s]
        P.dma("sp", st[:, 0:kk * ncols].rearrange("p (k c) -> p k c", k=kk),
              w_cols[k0 * 128:(k0 + kk) * 128, :].rearrange("(k p) c -> p k c", p=128),
              (), (st_t,))
        cast_op(P, ring["ns"], wb[:, k0 * ncols:(k0 + kk) * ncols], st[:, 0:kk * ncols], (st_t,), (wb_t,))
        k0 += kk
    return wb, wb_t


def load_rep(P, C, dram_vec, n, tok):
    t = C.A.alloc(n, F32)
    P.dma("sp", t, dram_vec.partition_broadcast(128), (), (tok,))
    return t


def out_proj_phase(P, C, yT, w_out, h_in, h_out, ntok):
    A, ps, ps_t = C.A, C.ps, C.ps_t
    T = 1024
    NB = T // 512
    A.mark()
    yb = A.alloc(KD * T, BF16).rearrange("p (k t) -> p k t", k=KD)
    yb_t = Tok("yb")
    ring = make_ring(A)
    xr = [A.alloc(512, F32) for _ in range(3)]
    xr_t = toks(3, "xr")
    nxr = 0
    for tile in range(ntok // T):
        tok0 = tile * T
        for k4 in range(4):
            P.dma("sp", yb[:, k4 * 4:(k4 + 1) * 4, :],
                  yT[k4 * 512:(k4 + 1) * 512, tok0:tok0 + T].rearrange("(k p) t -> p k t", p=128),
                  (C.yT_t,), (yb_t,))
        for dc in range(KD):
            wb, wb_t = stream_w(P, C, w_out[:, dc * 128:(dc + 1) * 128], KD, 128, ring)
            for blk in range(NB):
                pb = 4 + (dc * NB + blk) % 3
                po, po_t = ps[pb], ps_t[pb]
                r = nxr % 3
                nxr += 1
                c0 = tok0 + blk * 512
                P.dma("sp", xr[r], h_in[dc * 128:(dc + 1) * 128, c0:c0 + 512], (), (xr_t[r],))
                for k in range(KD):
                    P.op("pe", lambda e, wb=wb, po=po, k=k, blk=blk: e.matmul(
                        po, lhsT=wb[:, k * 128:(k + 1) * 128], rhs=yb[:, k, blk * 512:(blk + 1) * 512],
                        start=(k == 0), stop=(k == KD - 1)),
                        (wb_t, yb_t), (po_t,) if k in (0, KD - 1) else ())
                P.op("dve", lambda e, po=po, r=r: e.tensor_tensor(
                    out=xr[r], in0=po, in1=xr[r], op=ALU.add), (po_t,), (xr_t[r],))
                P.dma("pool", h_out[dc * 128:(dc + 1) * 128, c0:c0 + 512], xr[r], (xr_t[r],), ())
    A.release()
    P.barrier()


LAMBDA_INIT1 = 0.8 - 0.6 * math.exp(-0.3 * 1)
ATT_SCALE = 128 ** -0.5


def attn_phase(P, C, h_in, gain_dram, w_qkv, lq1, lk1, lq2, lk2, subln, cst, yT, ntok):
    nc, A, ps, ps_t = C.nc, C.A, C.ps, C.ps_t
    A.mark()
    ct = Tok("attc")
    gain, gain_t = load_gain(P, C, gain_dram)
    cosf = A.alloc(SEQ, F32)
    sinf = A.alloc(SEQ, F32)
    P.dma("sp", cosf, cst["att_cos"], (), (ct,))
    P.dma("sp", sinf, cst["att_sin"], (), (ct,))
    pm32 = A.alloc(128, F32)
    P.dma("sp", pm32, cst["att_pm"], (), (ct,))
    pm = A.alloc(128, BF16)
    P.op("dve", lambda e: e.tensor_copy(out=pm, in_=pm32), (ct,), (ct,))
    id32 = A.alloc(128, F32)
    P.dma("sp", id32, cst["ident"], (), (ct,))
    ident = A.alloc(128, BF16)
    P.op("dve", lambda e: e.tensor_copy(out=ident, in_=id32), (ct,), (ct,))
    lt = Tok("lam")
    lv = [load_rep(P, C, v[0], 128, lt) for v in (lq1, lk1, lq2, lk2)]
    lsum = A.alloc(2, F32)
    junk = A.alloc(128, F32)
    for i in range(2):
        P.op("dve", lambda e, i=i: e.tensor_tensor(out=junk, in0=lv[2 * i], in1=lv[2 * i + 1], op=ALU.mult),
             (lt,), (lt,))
        P.op("dve", lambda e, i=i: e.reduce_sum(out=lsum[:, i:i + 1], in_=junk, axis=AX.X), (), (lt,))
    P.op("act", lambda e: e.activation(out=lsum, in_=lsum, func=AF.Exp), (), (lt,))
    neglam = A.alloc(1, F32)
    P.op("dve", lambda e: e.tensor_tensor(out=neglam, in0=lsum[:, 1:2], in1=lsum[:, 0:1], op=ALU.subtract),
         (), (lt,))
    P.op("dve", lambda e: e.tensor_scalar(out=neglam, in0=neglam, scalar1=-LAMBDA_INIT1, scalar2=None,
                                          op0=ALU.add), (), (lt,))
    sub_rep = load_rep(P, C, subln[0], 256, lt)
    P.op("dve", lambda e: e.tensor_scalar(out=sub_rep, in0=sub_rep, scalar1=1.0 - LAMBDA_INIT1, scalar2=None,
                                          op0=ALU.mult), (), (lt,))

    hnT = A.alloc(KD * SEQ, BF16).rearrange("p (k t) -> p k t", k=KD)
    hn_t = toks(SEQ // 512, "hn")
    ring = make_ring(A, nst=3, nwb=3, wb_cols=16 * 256)
    qk = [A.alloc(2 * SEQ, BF16).rearrange("p (c t) -> p c t", c=2) for _ in range(2)]
    qk_t = [[toks(4, "q0"), toks(4, "q1")], [toks(4, "k0"), toks(4, "k1")]]
    VW = 258
    v_sb = A.alloc(16 * VW, BF16).rearrange("p (b e) -> p b e", b=16)
    v_t = toks(16, "v")
    x_sb = [A.alloc(512, BF16) for _ in range(2)]
    x_t = toks(2, "x")
    t1 = [A.alloc(512, F32) for _ in range(2)]
    t1_t = toks(2, "t1")
    t2 = [A.alloc(512, F32) for _ in range(2)]
    t2_t = toks(2, "t2")
    pT = [A.alloc(256, BF16) for _ in range(6)]
    pT_t = toks(6, "pT")
    acc = [A.alloc(256, F32) for _ in range(2)]
    acc_t = toks(2, "acc")
    rs = A.alloc(8, F32)
    rs_t = Tok("rs")
    y_sb = [A.alloc(256, BF16) for _ in range(2)]
    y_t = toks(2, "y")
    yst = A.alloc(2 * SEQ, BF16).rearrange("p (j t) -> p j t", j=2)
    yst_t = Tok("yst")
    sq_junk = A.alloc(256, F32)
    P.op("pool", lambda e: e.memset(v_sb[:, :, 256:257], 1.0), (), tuple(v_t))
    nx = 0
    npt = 0
    for s in range(ntok // SEQ):
        s0 = s * SEQ
        norm_T(P, C, h_in, s0, SEQ, gain, gain_t, hnT, hn_t)
        for h in range(8):
            for qi in range(2):
                for comp in range(2):
                    col = qi * D + (2 * h + comp) * 128
                    wb, wb_t = stream_w(P, C, w_qkv[:, col:col + 128], KD, 128, ring)
                    for blk in range(4):
                        pp, pp_t = ps[blk % 2], ps_t[blk % 2]
                        for k in range(KD):
                            P.op("pe", lambda e, wb=wb, pp=pp, k=k, blk=blk: e.matmul(
                                pp, lhsT=wb[:, k * 128:(k + 1) * 128], rhs=hnT[:, k, blk * 512:(blk + 1) * 512],
                                start=(k == 0), stop=(k == KD - 1)),
                                (wb_t, hn_t[blk]), (pp_t,) if k in (0, KD - 1) else ())
                        b = nx % 2
                        nx += 1
                        P.op("act", lambda e, b=b, pp=pp: e.copy(out=x_sb[b], in_=pp), (pp_t,), (x_t[b],))
                        P.op("pe", lambda e, b=b: e.matmul(ps[2], lhsT=pm, rhs=x_sb[b], start=True, stop=True),
                             (x_t[b], ct), (ps_t[2],))
                        tsl = slice(blk * 512, (blk + 1) * 512)
                        P.op("dve", lambda e, b=b, tsl=tsl: e.tensor_tensor(
                            out=t1[b], in0=ps[2], in1=sinf[:, tsl], op=ALU.mult), (ps_t[2], ct), (t1_t[b],))
                        P.op("dve", lambda e, b=b, pp=pp, tsl=tsl: e.tensor_tensor(
                            out=t2[b], in0=pp, in1=cosf[:, tsl], op=ALU.mult), (pp_t, ct), (t2_t[b],))
                        P.op("pool", lambda e, b=b, qi=qi, comp=comp, tsl=tsl: e.tensor_tensor(
                            out=qk[qi][:, comp, tsl], in0=t1[b], in1=t2[b], op=ALU.add),
                            (t1_t[b], t2_t[b]), (qk_t[qi][comp][blk],))
            col = 2 * D + h * 256
            wv, wv_t = stream_w(P, C, w_qkv[:, col:col + 256], KD, 256, ring)
            if C.dbg is not None and h == 0 and s == 0:
                P.dma("pool", C.dbg["wv"], wv, (wv_t,), ())
            for tb in range(16):
                pp, pp_t = ps[tb % 2], ps_t[tb % 2]
                for k in range(KD):
                    P.op("pe", lambda e, pp=pp, k=k, tb=tb, wv=wv: e.matmul(
                        pp[:, 0:256], lhsT=hnT[:, k, tb * 128:(tb + 1) * 128], rhs=wv[:, k * 256:(k + 1) * 256],
                        start=(k == 0), stop=(k == KD - 1)),
                        (wv_t, hn_t[tb // 4]), (pp_t,) if k in (0, KD - 1) else ())
                P.op("act", lambda e, pp=pp, tb=tb: e.copy(out=v_sb[:, tb, 0:256], in_=pp[:, 0:256]),
                     (pp_t,), (v_t[tb],))
            if C.dbg is not None and h == 0 and s == 0:
                P.dma("pool", C.dbg["q"], qk[0], tuple(qk_t[0][0] + qk_t[0][1]), ())
                P.dma("pool", C.dbg["k"], qk[1], tuple(qk_t[1][0] + qk_t[1][1]), ())
                P.dma("pool", C.dbg["v"], v_sb, tuple(v_t), ())
                P.dma("pool", C.dbg["lam"], neglam, (lt,), ())
            tasks = [(qr, comp, jb) for qr in range(8) for comp in range(2) for jb in range(2 * qr + 2)]
            LA = 3
            SB = (0, 3, 4, 1)
            pvb = {0: (5, 6), 1: (7, 2)}

            def emit_S(i, h=h):
                qr, comp, jb = tasks[i]
                a0_ = max(0, jb - 2 * qr)
                x0 = a0_ * 128
                sb = SB[i % len(SB)]
                MM(P, ps[sb][:, x0:256], qk[1][:, comp, jb * 128:(jb + 1) * 128],
                   qk[0][:, comp, qr * 256 + x0:(qr + 1) * 256], True, True,
                   (qk_t[1][comp][jb // 4], qk_t[0][comp][qr // 2]), (ps_t[sb],))
                pi = i % len(pT)
                ACTV(P, pT[pi][:, x0:256], ps[sb][:, x0:256], AF.Exp, (ps_t[sb],), (pT_t[pi],), scale=ATT_SCALE)
                if jb >= 2 * qr:
                    MSET(P, "pool", pT[pi][64:128, x0:x0 + 64], 0.0, (), (pT_t[pi],))

            def emit_PV(i, h=h):
                qr, comp, jb = tasks[i]
                a0_ = max(0, jb - 2 * qr)
                pi = i % len(pT)
                pv = [ps[pvb[comp][0]], ps[pvb[comp][1]]]
                pv_t = [ps_t[pvb[comp][0]], ps_t[pvb[comp][1]]]
                for a in range(a0_, 2):
                    last = (jb == 2 * qr + a)
                    MM(P, pv[a][:, 0:257], pT[pi][:, a * 128:(a + 1) * 128], v_sb[:, jb, 0:257], jb == 0, last,
                       (pT_t[pi], v_t[jb]), (pv_t[a],) if (jb == 0 or last) else ())
                if jb != 2 * qr + 1:
                    return
                for a in range(2):
                    c = comp * 2 + a
                    P.op("dve", lambda e, a=a, c=c, pv=pv: e.reciprocal(out=rs[:, c:c + 1], in_=pv[a][:, 256:257]),
                         (pv_t[a],), (rs_t,))
                    if comp == 0:
                        ACTV(P, acc[a], pv[a][:, 0:256], AF.Copy, (pv_t[a], rs_t), (acc_t[a],), scale=rs[:, c:c + 1])
                    else:
                        TT(P, "dve", rs[:, c:c + 1], rs[:, c:c + 1], neglam, ALU.mult, (lt,), (rs_t,))
                        STT(P, acc[a], pv[a][:, 0:256], rs[:, c:c + 1], acc[a], ALU.mult, ALU.add,
                            (pv_t[a], rs_t), (acc_t[a],))
                if comp == 0:
                    return
                for a in range(2):
                    c = 4 + a
                    ACTV(P, sq_junk, acc[a], AF.Square, (acc_t[a],), (rs_t,), accum_out=rs[:, c:c + 1])
                    TS(P, "dve", rs[:, c:c + 1], rs[:, c:c + 1], 1.0 / 256, EPS, ALU.mult, ALU.add, (), (rs_t,))
                    rstd_inplace(P, rs[:, c:c + 1], rs_t)
                    STT(P, y_sb[a], acc[a], rs[:, c:c + 1], sub_rep, ALU.mult, ALU.mult, (acc_t[a], rs_t, lt), (y_t[a],))
                    ptr = ps[2].bitcast(BF16)
                    for j in range(2):
                        TR(P, ptr[:, 640 + j * 128:640 + (j + 1) * 128], y_sb[a][:, j * 128:(j + 1) * 128], ident,
                           (y_t[a], ct), (ps_t[2],))
                    t0 = (qr * 2 + a) * 128
                    CP(P, "dve", yst[:, :, t0:t0 + 128], ptr[:, 640:896].rearrange("p (j t) -> p j t", j=2),
                       (ps_t[2],), (yst_t,))

            for i in range(len(tasks) + LA):
                if i < len(tasks):
                    emit_S(i)
                if i >= LA:
                    emit_PV(i - LA)
            P.dma("pool", yT[h * 256:(h + 1) * 256, s0:s0 + SEQ].rearrange("(j p) t -> p j t", p=128), yst,
                  (yst_t,), (C.yT_t,))
    A.release()
    P.barrier()


def host_consts():
    c = {}
    t = np.arange(SEQ, dtype=np.float32)
    inv = (1.0 / (np.float32(500000.0) ** (np.arange(0, 32, 2, dtype=np.float32) / np.float32(32)))).astype(np.float32)
    ang = t[:, None] * inv[None, :]
    cos, sin = np.cos(ang).astype(np.float32), np.sin(ang).astype(np.float32)
    cf = np.ones((128, SEQ), np.float32)
    sf = np.zeros((128, SEQ), np.float32)
    cf[0:16] = cos.T
    cf[16:32] = cos.T
    sf[0:16] = sin.T
    sf[16:32] = sin.T
    c["att_cos"], c["att_sin"] = cf, sf
    pm = np.zeros((128, 128), np.float32)
    for m in range(16):
        pm[m + 16, m] = -1.0
        pm[m, m + 16] = 1.0
    c["att_pm"] = pm
    c["ident"] = np.eye(128, dtype=np.float32)
    return c


CONST_SHAPES = {"att_cos": [128, SEQ], "att_sin": [128, SEQ], "att_pm": [128, 128], "ident": [128, 128]}


def build_attn_test(ntok):
    nc = bass.Bass("TRN2", target_bir_lowering=False)
    hT = nc.dram_tensor("hT", [D, ntok], F32, kind="ExternalInput").ap()
    g = nc.dram_tensor("g", [D], F32, kind="ExternalInput").ap()
    wqkv = nc.dram_tensor("wqkv", [D, 3 * D], F32, kind="ExternalInput").ap()
    wo = nc.dram_tensor("wo", [D, D], F32, kind="ExternalInput").ap()
    lam = [nc.dram_tensor(n, [1, 128], F32, kind="ExternalInput").ap() for n in ("lq1", "lk1", "lq2", "lk2")]
    subln = nc.dram_tensor("subln", [1, 256], F32, kind="ExternalInput").ap()
    cst = {k: nc.dram_tensor(k, v, F32, kind="ExternalInput").ap() for k, v in CONST_SHAPES.items()
           if k in ("att_cos", "att_sin", "att_pm", "ident")}
    yT = nc.dram_tensor("yT", [D, ntok], BF16, kind="ExternalOutput").ap()
    oT = nc.dram_tensor("oT", [D, ntok], F32, kind="ExternalOutput").ap()
    P = Prog(nc)
    C = setup_common(nc, P)
    C.yT_t = Tok("yT")
    C.dbg = {"q": nc.dram_tensor("dq", [128, 2, SEQ], BF16, kind="ExternalOutput").ap(),
             "k": nc.dram_tensor("dk", [128, 2, SEQ], BF16, kind="ExternalOutput").ap(),
             "v": nc.dram_tensor("dv", [128, 16, 258], BF16, kind="ExternalOutput").ap(),
             "wv": nc.dram_tensor("dwv", [128, 4096], BF16, kind="ExternalOutput").ap(),
             "lam": nc.dram_tensor("dlam", [128, 1], F32, kind="ExternalOutput").ap()}
    attn_phase(P, C, hT, g, wqkv, lam[0], lam[1], lam[2], lam[3], subln, cst, yT, ntok)
    out_proj_phase(P, C, yT, wo, hT, oT, ntok)
    P.finish()
    P.emit()
    return nc, P


def TT(P, eng, out, in0, in1, op, r=(), w=()):
    return P.op(eng, lambda e: e.tensor_tensor(out=out, in0=in0, in1=in1, op=op), r, w)


def TS(P, eng, out, in0, s1, s2, op0, op1=None, r=(), w=()):
    if op1 is None:
        return P.op(eng, lambda e: e.tensor_scalar(out=out, in0=in0, scalar1=s1, scalar2=None, op0=op0), r, w)
    return P.op(eng, lambda e: e.tensor_scalar(out=out, in0=in0, scalar1=s1, scalar2=s2, op0=op0, op1=op1), r, w)


def STT(P, out, in0, scalar, in1, op0, op1, r=(), w=()):
    return P.op("dve", lambda e: e.scalar_tensor_tensor(out=out, in0=in0, scalar=scalar, in1=in1, op0=op0, op1=op1), r, w)


def ACTV(P, out, in_, func, r=(), w=(), **kw):
    return P.op("act", lambda e: e.activation(out=out, in_=in_, func=func, **kw), r, w)


def CP(P, eng, out, in_, r=(), w=()):
    if eng == "act":
        return P.op("act", lambda e: e.copy(out=out, in_=in_), r, w)
    return P.op(eng, lambda e: e.tensor_copy(out=out, in_=in_), r, w)


def MM(P, out, lhsT, rhs, start, stop, r=(), w=()):
    return P.op("pe", lambda e: e.matmul(out, lhsT=lhsT, rhs=rhs, start=start, stop=stop), r, w)


def TR(P, out, in_, ident, r=(), w=()):
    return P.op("pe", lambda e: e.transpose(out, in_, ident), r, w)


def MSET(P, eng, ap, val, r=(), w=()):
    return P.op(eng, lambda e: e.memset(ap, val), r, w)


def rstd_inplace(P, x, t):
    ACTV(P, x, x, AF.Sqrt, (), (t,))
    P.op("dve", lambda e: e.reciprocal(out=x, in_=x), (), (t,))


RET_GAMMA = [1.0 - 2.0 ** (-5.0 - h) for h in range(4)]


def mix0a_phase(P, C, h_in, gain_dram, w_in, cst, yT, uT_d, Ud, ntok):
    nc, A, ps, ps_t = C.nc, C.A, C.ps, C.ps_t
    A.mark()
    ct = Tok("m0c")
    gain, gain_t = load_gain(P, C, gain_dram)
    cosf = A.alloc(SEQ, F32)
    sinf = A.alloc(SEQ, F32)
    P.dma("sp", cosf, cst["ret_cos"], (), (ct,))
    P.dma("sp", sinf, cst["ret_sin"], (), (ct,))
    id32 = A.alloc(128, F32)
    P.dma("sp", id32, cst["ident"], (), (ct,))
    ident = A.alloc(128, BF16)
    CP(P, "dve", ident, id32, (ct,), (ct,))
    tab = A.alloc(4 * 384, F32).rearrange("p (h y) -> p h y", h=4)
    P.dma("sp", tab, cst["ret_tab"].rearrange("h p y -> p h y"), (), (ct,))
    hnT = A.alloc(KD * SEQ, BF16).rearrange("p (k t) -> p k t", k=KD)
    hn_t = toks(SEQ // 512, "hn")
    ring = make_ring(A, nst=3, nwb=2, wb_cols=16 * 256)
    qk = [A.alloc(2 * SEQ, BF16).rearrange("p (c t) -> p c t", c=2) for _ in range(2)]
    qk_t = [toks(4, "rq"), toks(4, "rk")]
    sgT = A.alloc(2 * SEQ, BF16).rearrange("p (c t) -> p c t", c=2)
    sg_t = toks(4, "sg")
    v_sb = A.alloc(16 * 256, BF16).rearrange("p (b e) -> p b e", b=16)
    v_t = toks(16, "rv")
    tmp = [A.alloc(512, F32) for _ in range(4)]
    tmp_t = toks(4, "tmp")
    pT = [A.alloc(256, BF16) for _ in range(5)]
    pT_t = toks(5, "rpT")
    rs = A.alloc(4, F32)
    rs_t = Tok("rrs")
    sq_junk = A.alloc(256, F32)
    y_sb = [A.alloc(256, BF16) for _ in range(2)]
    y_t = toks(2, "ry")
    yst = A.alloc(2 * SEQ, BF16).rearrange("p (j t) -> p j t", j=2)
    yst_t = Tok("ryst")
    u_sb = [A.alloc(SEQ, BF16) for _ in range(1)]
    u_t = toks(2, "u")
    up_sb = [A.alloc(SEQ, BF16).rearrange("p (j m) -> p j m", j=8) for _ in range(1)]
    up_t = toks(2, "up")
    npt = 0
    for s in range(ntok // SEQ):
        s0 = s * SEQ
        norm_T(P, C, h_in, s0, SEQ, gain, gain_t, hnT, hn_t)
        for a in range(8):
            if getattr(C, "skip_u", False):
                break
            col = 4096 + a * 128
            wb, wb_t = stream_w(P, C, w_in[:, col:col + 128], KD, 128, ring)
            ub = 0
            for blk in range(4):
                pp, pp_t = ps[blk % 2], ps_t[blk % 2]
                for k in range(KD):
                    MM(P, pp, wb[:, k * 128:(k + 1) * 128], hnT[:, k, blk * 512:(blk + 1) * 512], k == 0, k == KD - 1,
                       (wb_t, hn_t[blk]), (pp_t,) if k in (0, KD - 1) else ())
                CP(P, "act", u_sb[ub][:, blk * 512:(blk + 1) * 512], pp, (pp_t,), (u_t[ub],))
                CP(P, "dve", up_sb[ub][:, :, blk * 64:(blk + 1) * 64],
                   u_sb[ub][:, blk * 512:(blk + 1) * 512].rearrange("p (m j) -> p j m", j=8),
                   (u_t[ub],), (up_t[ub],))
            P.dma("pool", uT_d[a * 128:(a + 1) * 128, s0:s0 + SEQ], u_sb[ub], (u_t[ub],), (C.uT_t,))
            for gg in range(8):
                if getattr(C, "skip_ud", False):
                    break
                P.dma("pool", Ud[a * 8 + gg, :, :, s * 256:(s + 1) * 256].rearrange("j p m -> p j m"),
                      up_sb[ub][gg * 16:(gg + 1) * 16, :, :], (up_t[ub],), (C.Ud_t,))
        for h in range(4):
            if getattr(C, "skip_ret", False):
                break
            lng = math.log(RET_GAMMA[h])
            for qi in range(2):
                wb, wb_t = stream_w(P, C, w_in[:, qi * 1024 + h * 256: qi * 1024 + (h + 1) * 256], KD, 256, ring)
                for blk in range(4):
                    pa, pa_t = ps[0], ps_t[0]
                    pb, pb_t = ps[1], ps_t[1]
                    for (pp, pp_t, j) in ((pa, pa_t, 0), (pb, pb_t, 1)):
                        for k in range(KD):
                            MM(P, pp, wb[:, k * 256 + j * 128:k * 256 + (j + 1) * 128],
                               hnT[:, k, blk * 512:(blk + 1) * 512], k == 0, k == KD - 1,
                               (wb_t, hn_t[blk]), (pp_t,) if k in (0, KD - 1) else ())
                    tsl = slice(blk * 512, (blk + 1) * 512)
                    TT(P, "dve", tmp[0], pa, cosf[:, tsl], ALU.mult, (pa_t, ct), (tmp_t[0],))
                    TT(P, "dve", tmp[1], pb, sinf[:, tsl], ALU.mult, (pb_t, ct), (tmp_t[1],))
                    TT(P, "dve", tmp[2], pb, cosf[:, tsl], ALU.mult, (pb_t, ct), (tmp_t[2],))
                    TT(P, "dve", tmp[3], pa, sinf[:, tsl], ALU.mult, (pa_t, ct), (tmp_t[3],))
                    TT(P, "pool", qk[qi][:, 0, tsl], tmp[0], tmp[1], ALU.subtract, (tmp_t[0], tmp_t[1]), (qk_t[qi][blk],))
                    TT(P, "pool", qk[qi][:, 1, tsl], tmp[2], tmp[3], ALU.add, (tmp_t[2], tmp_t[3]), (qk_t[qi][blk],))
            wb, wb_t = stream_w(P, C, w_in[:, 3072 + h * 256: 3072 + (h + 1) * 256], KD, 256, ring)
            for blk in range(4):
                for j in range(2):
                    pp, pp_t = ps[j], ps_t[j]
                    for k in range(KD):
                        MM(P, pp, wb[:, k * 256 + j * 128:k * 256 + (j + 1) * 128],
                           hnT[:, k, blk * 512:(blk + 1) * 512], k == 0, k == KD - 1,
                           (wb_t, hn_t[blk]), (pp_t,) if k in (0, KD - 1) else ())
                    ACTV(P, sgT[:, j, blk * 512:(blk + 1) * 512], pp, AF.Silu, (pp_t,), (sg_t[blk],))
            wv, wv_t = stream_w(P, C, w_in[:, 2048 + h * 256: 2048 + (h + 1) * 256], KD, 256, ring)
            for tb in range(16):
                pp, pp_t = ps[tb % 2], ps_t[tb % 2]
                for k in range(KD):
                    MM(P, pp[:, 0:256], hnT[:, k, tb * 128:(tb + 1) * 128], wv[:, k * 256:(k + 1) * 256],
                       k == 0, k == KD - 1, (wv_t, hn_t[tb // 4]), (pp_t,) if k in (0, KD - 1) else ())
                CP(P, "act", v_sb[:, tb, :], pp[:, 0:256], (pp_t,), (v_t[tb],))
            rtasks = [(qr, jb) for qr in range(8) for jb in range(2 * qr + 2)]
            RLA = 2
            RSB = (3, 4, 0)
            rpvb = ((5, 6), (7, 1))

            def r_emit_S(i, h=h, lng=lng):
                qr, jb = rtasks[i]
                a0_ = max(0, jb - 2 * qr)
                x0 = a0_ * 128
                sb = RSB[i % len(RSB)]
                for j in range(2):
                    MM(P, ps[sb][:, x0:256], qk[1][:, j, jb * 128:(jb + 1) * 128],
                       qk[0][:, j, qr * 256 + x0:(qr + 1) * 256], j == 0, j == 1,
                       (qk_t[1][jb // 4], qk_t[0][qr // 2]), (ps_t[sb],))
                pi = i % len(pT)
                d0 = 2 * qr - jb
                if d0 >= 1:
                    sc, y0 = (1.0 / 16.0) * math.exp(lng * 128 * (d0 - 1)), 128
                else:
                    sc, y0 = 1.0 / 16.0, 0
                STT(P, pT[pi][:, x0:256], ps[sb][:, x0:256], sc, tab[:, h, y0:y0 + 256 - x0], ALU.mult, ALU.mult,
                    (ps_t[sb], ct), (pT_t[pi],))

            def r_emit_PV(i, h=h):
                qr, jb = rtasks[i]
                a0_ = max(0, jb - 2 * qr)
                pi = i % len(pT)
                bk = rpvb[qr % 2]
                pv = [ps[bk[0]], ps[bk[1]]]
                pv_t = [ps_t[bk[0]], ps_t[bk[1]]]
                for a in range(a0_, 2):
                    last = (jb == 2 * qr + a)
                    MM(P, pv[a][:, 0:256], pT[pi][:, a * 128:(a + 1) * 128], v_sb[:, jb, :], jb == 0, last,
                       (pT_t[pi], v_t[jb]), (pv_t[a],) if (jb == 0 or last) else ())
                if jb != 2 * qr + 1:
                    return
                for a in range(2):
                    ACTV(P, sq_junk, pv[a][:, 0:256], AF.Square, (pv_t[a],), (rs_t,), accum_out=rs[:, a:a + 1])
                    TS(P, "dve", rs[:, a:a + 1], rs[:, a:a + 1], 1.0 / 256, EPS, ALU.mult, ALU.add, (), (rs_t,))
                    rstd_inplace(P, rs[:, a:a + 1], rs_t)
                    ACTV(P, y_sb[a], pv[a][:, 0:256], AF.Copy, (pv_t[a], rs_t), (y_t[a],), scale=rs[:, a:a + 1])
                    ptr = ps[2].bitcast(BF16)
                    for j in range(2):
                        TR(P, ptr[:, j * 128:(j + 1) * 128], y_sb[a][:, j * 128:(j + 1) * 128], ident,
                           (y_t[a], ct), (ps_t[2],))
                    t0 = (qr * 2 + a) * 128
                    TT(P, "dve", yst[:, :, t0:t0 + 128], ptr[:, 0:256].rearrange("p (j t) -> p j t", j=2),
                       sgT[:, :, t0:t0 + 128], ALU.mult, (ps_t[2], sg_t[t0 // 512]), (yst_t,))

            for i in range(len(rtasks) + RLA):
                if i < len(rtasks):
                    r_emit_S(i)
                if i >= RLA:
                    r_emit_PV(i - RLA)
            P.dma("pool", yT[h * 256:(h + 1) * 256, s0:s0 + SEQ].rearrange("(j p) t -> p j t", p=128), yst,
                  (yst_t,), (C.yT_t,))
    A.release()
    P.barrier()


def host_consts_ret(c):
    t = np.arange(SEQ, dtype=np.float32)
    inv = (1.0 / (np.float32(10000.0) ** (np.arange(0, 256, 2, dtype=np.float32) / np.float32(256)))).astype(np.float32)
    ang = t[:, None] * inv[None, :]
    c["ret_cos"] = np.ascontiguousarray(np.cos(ang).astype(np.float32).T)
    c["ret_sin"] = np.ascontiguousarray(np.sin(ang).astype(np.float32).T)
    tab = np.zeros((4, 128, 384), np.float32)
    jj = np.arange(128)[:, None].astype(np.float64)
    for h in range(4):
        lg = math.log(RET_GAMMA[h])
        i = np.arange(128)[None, :].astype(np.float64)
        dg = np.exp(lg * np.abs(i - jj))
        dg[64:128, 0:64] = 0.0
        tab[h, :, 0:128] = dg
        y = np.arange(128, 384)[None, :].astype(np.float64)
        tab[h, :, 128:384] = np.exp(lg * (y - jj))
    c["ret_tab"] = tab
    return c


CONST_SHAPES.update({"ret_cos": [128, SEQ], "ret_sin": [128, SEQ], "ret_tab": [4, 128, 384]})


NEA, NEC = 71, 65
NE = NEA + NEC
GB = 4
S5_BARRIERS = False
PI = math.pi


def s5_powers(P, C, W, th, rho, ex, g0, tk, es=slice(0, None)):
    ne = len(range(*es.indices(NE)))
    shp = [64, GB, ne]
    thb = th[:, g0:g0 + GB].unsqueeze(2).to_broadcast(shp)
    rhb = rho[:, g0:g0 + GB].unsqueeze(2).to_broadcast(shp)
    exb = ex[:, es].unsqueeze(1).to_broadcast(shp)
    ang, r, fx, kf, ki, Pr, Pi, mag = [W[k][:, :, es] for k in ("ang", "r", "fx", "kf", "ki", "Pr", "Pi", "mag")]
    TT(P, "dve", ang, thb, exb, ALU.mult, (tk,), (tk,))
    TT(P, "pool", mag, rhb, exb, ALU.mult, (tk,), (tk,))
    ACTV(P, mag, mag, AF.Exp, (), (tk,))
    for (dst, off) in ((Pi, 0.0), (Pr, PI / 2)):
        if off != 0.0:
            TS(P, "dve", ang, ang, off, None, ALU.add, None, (), (tk,))
        TS(P, "dve", r, ang, 1.0 / (2 * PI), 32.5, ALU.mult, ALU.add, (), (tk,))
        CP(P, "dve", ki, r, (), (tk,))
        CP(P, "dve", kf, ki, (), (tk,))
        TS(P, "dve", kf, kf, -32.0, -2 * PI, ALU.add, ALU.mult, (), (tk,))
        TT(P, "dve", r, kf, ang, ALU.add, (), (tk,))
        TS(P, "dve", fx, r, -PI, 2 * PI, ALU.is_lt, ALU.mult, (), (tk,))
        TT(P, "dve", r, r, fx, ALU.add, (), (tk,))
        TS(P, "dve", fx, r, PI, -2 * PI, ALU.is_gt, ALU.mult, (), (tk,))
        TT(P, "dve", r, r, fx, ALU.add, (), (tk,))
        TS(P, "dve", r, r, -3.141592, 3.141592, ALU.max, ALU.min, (), (tk,))
        ACTV(P, r, r, AF.Sin, (), (tk,))
        TT(P, "dve", dst, r, mag, ALU.mult, (), (tk,))


def s5_phase(P, C, prm, cst, Ud, Yd):
    nc, A, ps, ps_t = C.nc, C.A, C.ps, C.ps_t
    A.mark()
    tk = Tok("s5p")
    N = 64

    def al(cols, dt=F32):
        return A.alloc(cols, dt)[0:64]

    lr, li, lst = al(64), al(64), al(64)
    for q4_ in range(4):
        gs = slice(q4_ * 16, (q4_ + 1) * 16)
        P.dma("sp", lr[:, gs], prm["lam_re"][0][gs, :].rearrange("g n -> n g"), (), (tk,), allow_slow_non_contiguous=True)
        P.dma("sp", li[:, gs], prm["lam_im"][0][gs, :].rearrange("g n -> n g"), (), (tk,), allow_slow_non_contiguous=True)
    P.dma("sp", lst, prm["log_step"][0].partition_broadcast(64), (), (tk,))
    ex = al(NE)
    P.dma("sp", ex, cst["s5_ex"][0].partition_broadcast(64), (), (tk,))
    id32 = A.alloc(128, F32)
    P.dma("sp", id32, cst["ident"], (), (tk,))
    ident = A.alloc(128, BF16)
    CP(P, "dve", ident, id32, (tk,), (tk,))
    mask0 = A.alloc(128, F32)
    P.dma("sp", mask0, cst["s5_mask"], (), (tk,))
    ACTV(P, lst, lst, AF.Exp, (tk,), (tk,))
    rho, th = al(64), al(64)
    TT(P, "dve", rho, lr, lst, ALU.mult, (), (tk,))
    TT(P, "dve", th, li, lst, ALU.mult, (), (tk,))
    bbr = al(1024).rearrange("p (g q) -> p g q", g=64)
    bbi = al(1024).rearrange("p (g q) -> p g q", g=64)
    cr = al(1024).rearrange("p (g q) -> p g q", g=64)
    ci = al(1024).rearrange("p (g q) -> p g q", g=64)
    A.mark()
    cn = A.alloc(8 * 64, F32).rearrange("p (t n) -> p t n", t=8)
    for (src, dst) in ((prm["c_re"], cr), (prm["c_im"], ci)):
        P.dma("sp", cn, src[0].rearrange("(t g) p n -> (g p) t n", t=8), (), (tk,))
        for tt in range(8):
            TR(P, ps[0][0:64, 0:128], cn[:, tt, :], id32, (tk,), (ps_t[0],))
            CP(P, "dve", dst[:, tt * 8:(tt + 1) * 8, :], ps[0][0:64, 0:128].rearrange("p (g q) -> p g q", g=8),
               (ps_t[0],), (tk,))
    A.release()
    W = {k: al(GB * NE).rearrange("p (g e) -> p g e", g=GB) for k in ("ang", "r", "fx", "kf", "Pr", "Pi", "mag")}
    W["ki"] = al(GB * NE, I32).rearrange("p (g e) -> p g e", g=GB)
    a1r, a1i, a64r, a64i = al(64), al(64), al(64), al(64)
    for gb in range(64 // GB):
        s5_powers(P, C, W, th, rho, ex, gb * GB, tk, slice(NEA + 1, NEA + 65, 63))
        CP(P, "dve", a1r[:, gb * GB:(gb + 1) * GB], W["Pr"][:, :, NEA + 1], (), (tk,))
        CP(P, "dve", a1i[:, gb * GB:(gb + 1) * GB], W["Pi"][:, :, NEA + 1], (), (tk,))
        CP(P, "dve", a64r[:, gb * GB:(gb + 1) * GB], W["Pr"][:, :, NEA + 64], (), (tk,))
        CP(P, "dve", a64i[:, gb * GB:(gb + 1) * GB], W["Pi"][:, :, NEA + 64], (), (tk,))
        if C.dbg is not None and gb == 0:
            P.dma("pool", C.dbg["pr"], W["Pr"], (tk,), ())
            P.dma("pool", C.dbg["pi"], W["Pi"], (tk,), ())
    den, t1, t2, fr, fi = al(64), al(64), al(64), al(64), al(64)
    TT(P, "dve", den, lr, lr, ALU.mult, (), (tk,))
    TT(P, "dve", t1, li, li, ALU.mult, (), (tk,))
    TT(P, "dve", den, den, t1, ALU.add, (), (tk,))
    P.op("dve", lambda e: e.reciprocal(out=den, in_=den), (), (tk,))
    TS(P, "dve", a1r, a1r, -1.0, None, ALU.add, None, (), (tk,))
    TT(P, "dve", t1, a1r, lr, ALU.mult, (), (tk,))
    TT(P, "dve", t2, a1i, li, ALU.mult, (), (tk,))
    TT(P, "dve", fr, t1, t2, ALU.add, (), (tk,))
    TT(P, "dve", fr, fr, den, ALU.mult, (), (tk,))
    TT(P, "dve", t1, a1i, lr, ALU.mult, (), (tk,))
    TT(P, "dve", t2, a1r, li, ALU.mult, (), (tk,))
    TT(P, "dve", fi, t1, t2, ALU.subtract, (), (tk,))
    TT(P, "dve", fi, fi, den, ALU.mult, (), (tk,))
    A.mark()
    br = al(1024).rearrange("p (g q) -> p g q", g=64)
    bi = al(1024).rearrange("p (g q) -> p g q", g=64)
    for q8_ in range(8):
        gs = slice(q8_ * 8, (q8_ + 1) * 8)
        P.dma("sp", br[:, gs, :], prm["b_re"][0][gs].rearrange("g n p -> n g p"), (), (tk,))
        P.dma("sp", bi[:, gs, :], prm["b_im"][0][gs].rearrange("g n p -> n g p"), (), (tk,))
    u1 = al(1024).rearrange("p (g q) -> p g q", g=64)
    u2 = al(1024).rearrange("p (g q) -> p g q", g=64)
    frb = fr.unsqueeze(2).to_broadcast([64, 64, 16])
    fib = fi.unsqueeze(2).to_broadcast([64, 64, 16])
    TT(P, "dve", u1, br, frb, ALU.mult, (), (tk,))
    TT(P, "dve", u2, bi, fib, ALU.mult, (), (tk,))
    TT(P, "dve", bbr, u1, u2, ALU.subtract, (), (tk,))
    TT(P, "dve", u1, bi, frb, ALU.mult, (), (tk,))
    TT(P, "dve", u2, br, fib, ALU.mult, (), (tk,))
    TT(P, "dve", bbi, u1, u2, ALU.add, (), (tk,))
    A.release()
    if C.dbg is not None:
        P.dma("pool", C.dbg["fr"], fr, (tk,), ())
        P.dma("pool", C.dbg["fi"], fi, (tk,), ())
        P.dma("pool", C.dbg["bbr"], bbr, (tk,), ())
        P.dma("pool", C.dbg["cr"], cr, (tk,), ())

    Hr = al(64 * 64, BF16).rearrange("p (g c) -> p g c", g=64)
    Hi = al(64 * 64, BF16).rearrange("p (g c) -> p g c", g=64)
    H_t = Tok("H")
    A.mark()
    Sall = al(2 * 64 * 64).rearrange("p (r g c) -> p r g c", r=2, g=64)
    S_t = Tok("Sall")
    A.mark()
    U = A.alloc(64 * 512, BF16).rearrange("p (g m) -> p g m", g=64)
    U_t = Tok("U")
    for g8 in range(8):
        P.dma("sp", U[:, g8 * 8:(g8 + 1) * 8, :], Ud[g8 * 8:(g8 + 1) * 8].rearrange("g j p m -> (j p) g m"),
              (C.Ud_t,), (U_t,))
    tm = [al(NEA * 16).rearrange("p (e q) -> p e q", q=16) for _ in range(4)]
    tm_t = toks(4, "tm")
    ABr = [al(NEA * 16, BF16).rearrange("p (e q) -> p e q", q=16) for _ in range(2)]
    ABi = [al(NEA * 16, BF16).rearrange("p (e q) -> p e q", q=16) for _ in range(2)]
    AB_t = toks(2, "AB")
    Gsb = [A.alloc(16 * 64, BF16).rearrange("p (j n) -> p j n", j=16) for _ in range(2)]
    G_t = toks(2, "G")
    shpA = [64, NEA, 16]
    late1 = []
    for g in range(64):
        gi = g % GB
        if gi == 0:
            s5_powers(P, C, W, th, rho, ex, g, tk, slice(0, NEA))
        b = g % 2
        PrA = W["Pr"][:, gi, 0:NEA].unsqueeze(2).to_broadcast(shpA)
        PiA = W["Pi"][:, gi, 0:NEA].unsqueeze(2).to_broadcast(shpA)
        bR = bbr[:, g, :].unsqueeze(1).to_broadcast(shpA)
        bI = bbi[:, g, :].unsqueeze(1).to_broadcast(shpA)
        TT(P, "dve", tm[0], PrA, bR, ALU.mult, (tk,), (tm_t[0],))
        TT(P, "dve", tm[1], PiA, bI, ALU.mult, (tk,), (tm_t[1],))
        TT(P, "dve", tm[2], PrA, bI, ALU.mult, (tk,), (tm_t[2],))
        TT(P, "dve", tm[3], PiA, bR, ALU.mult, (tk,), (tm_t[3],))
        for (o_, i_, t_) in late1:
            CP(P, "dve", o_, i_, (t_,), (S_t,))
        late1 = []
        TT(P, "pool", ABr[b], tm[0], tm[1], ALU.subtract, (tm_t[0], tm_t[1]), (AB_t[b],))
        TT(P, "pool", ABi[b], tm[2], tm[3], ALU.add, (tm_t[2], tm_t[3]), (AB_t[b],))
        ptr = ps[2].bitcast(BF16)
        for jj in range(8):
            for ri, AB in enumerate((ABr[b], ABi[b])):
                TR(P, ptr[:, (2 * jj + ri) * 64:(2 * jj + ri + 1) * 64],
                   AB[:, 8 * jj:8 * jj + 8, :].rearrange("p e q -> p (e q)"), ident[0:64, 0:64],
                   (AB_t[b], tk), (ps_t[2],))
        CP(P, "act", Gsb[b], ptr[:, 0:1024].rearrange("p (j n) -> p j n", j=16), (ps_t[2],), (G_t[b],))
        pS, pS_t = ps[3 + g % 2], ps_t[3 + g % 2]
        for ri in range(2):
            for jj in range(8):
                MM(P, pS[0:64, ri * 64:(ri + 1) * 64], Gsb[b][:, 2 * jj + ri, :], U[:, g, jj::8], jj == 0, jj == 7,
                   (G_t[b], U_t), (pS_t,))
        late1.append((Sall[:, :, g, :], pS[0:64, 0:128].rearrange("p (r c) -> p r c", r=2), pS_t))
    for (o_, i_, t_) in late1:
        CP(P, "dve", o_, i_, (t_,), (S_t,))
    late1 = []
    A.release()
    if C.dbg is not None:
        P.dma("pool", C.dbg["S"], Sall, (S_t,), ())
    S2 = al(2 * 64 * 64).rearrange("p (r g c) -> p r g c", r=2, g=64)
    q = [al(64 * 64).rearrange("p (g c) -> p g c", g=64) for _ in range(2)]
    cur, nxt = Sall, S2

    def v4(x, r, lo, hi):
        return x[:, r, :, :].rearrange("p g (s c) -> p g s c", s=2)[:, :, :, lo:hi]

    def q4(x, n):
        return x.rearrange("p g (s c) -> p g s c", s=2)[:, :, :, 0:n]

    Ar, Ai = a64r, a64i
    for lev in range(5):
        d = 1 << lev
        n = 32 - d
        shp = [64, 64, 2, n]
        Arb = Ar.unsqueeze(2).unsqueeze(3).to_broadcast(shp)
        Aib = Ai.unsqueeze(2).unsqueeze(3).to_broadcast(shp)
        CP(P, "dve", v4(nxt, 0, 0, d), v4(cur, 0, 0, d), (S_t, tk), (S_t,))
        CP(P, "dve", v4(nxt, 1, 0, d), v4(cur, 1, 0, d), (), (S_t,))
        TT(P, "dve", q4(q[0], n), v4(cur, 0, 0, n), Arb, ALU.mult, (), (S_t,))
        TT(P, "dve", q4(q[1], n), v4(cur, 1, 0, n), Aib, ALU.mult, (), (S_t,))
        TT(P, "dve", v4(nxt, 0, d, 32), v4(cur, 0, d, 32), q4(q[0], n), ALU.add, (), (S_t,))
        TT(P, "dve", v4(nxt, 0, d, 32), v4(nxt, 0, d, 32), q4(q[1], n), ALU.subtract, (), (S_t,))
        TT(P, "dve", q4(q[0], n), v4(cur, 1, 0, n), Arb, ALU.mult, (), (S_t,))
        TT(P, "dve", q4(q[1], n), v4(cur, 0, 0, n), Aib, ALU.mult, (), (S_t,))
        TT(P, "dve", v4(nxt, 1, d, 32), v4(cur, 1, d, 32), q4(q[0], n), ALU.add, (), (S_t,))
        TT(P, "dve", v4(nxt, 1, d, 32), v4(nxt, 1, d, 32), q4(q[1], n), ALU.add, (), (S_t,))
        cur, nxt = nxt, cur
        if lev < 4:
            TT(P, "dve", t1, Ar, Ar, ALU.mult, (), (tk,))
            TT(P, "dve", t2, Ai, Ai, ALU.mult, (), (tk,))
            TT(P, "dve", den, Ar, Ai, ALU.mult, (), (tk,))
            TT(P, "dve", Ar, t1, t2, ALU.subtract, (), (tk,))
            TS(P, "dve", Ai, den, 2.0, None, ALU.mult, None, (), (tk,))

    def h4(x, lo, hi):
        return x.rearrange("p g (s c) -> p g s c", s=2)[:, :, :, lo:hi]
    MSET(P, "pool", Hr, 0.0, (), (H_t,))
    MSET(P, "pool", Hi, 0.0, (), (H_t,))
    CP(P, "dve", h4(Hr, 1, 32), v4(cur, 0, 0, 31), (S_t,), (H_t,))
    TS(P, "dve", h4(Hi, 1, 32), v4(cur, 1, 0, 31), -1.0, None, ALU.mult, None, (S_t,), (H_t,))
    if C.dbg is not None:
        P.dma("pool", C.dbg["Hr"], Hr, (H_t,), ())
        P.dma("pool", C.dbg["Hi"], Hi, (H_t,), ())
    A.release()
    P.barrier()
    A.mark()
    U = A.alloc(64 * 512, BF16).rearrange("p (g m) -> p g m", g=64)
    U_t = Tok("U2")
    for g8 in range(8):
        P.dma("sp", U[:, g8 * 8:(g8 + 1) * 8, :], Ud[g8 * 8:(g8 + 1) * 8].rearrange("g j p m -> (j p) g m"),
              (C.Ud_t,), (U_t,))
    Yrb = [al(128, BF16) for _ in range(2)]
    Yib = [al(128, BF16) for _ in range(2)]
    Yp_t = toks(2, "Yp")
    ty = [al(128).rearrange("p (e q) -> p e q", q=16) for _ in range(4)]
    ty_t = toks(4, "ty")
    shpY = [64, 8, 16]
    shpC = [64, NEC, 16]
    tc_ = [al(NEC * 16).rearrange("p (e q) -> p e q", q=16) for _ in range(4)]
    tc_t = toks(4, "tc")
    CAr = [al(NEC * 16, BF16).rearrange("p (e q) -> p e q", q=16) for _ in range(2)]
    CAi = [al(NEC * 16, BF16).rearrange("p (e q) -> p e q", q=16) for _ in range(2)]
    CA_t = toks(2, "CA")
    Tsb = [A.alloc(8 * 128, BF16).rearrange("p (d c) -> p d c", d=8) for _ in range(2)]
    T_t = toks(2, "T")
    Ysb = [A.alloc(8 * 512, F32).rearrange("p (g m) -> p g m", g=8) for _ in range(2)]
    Yo_t = toks(2, "Yo")
    ytmp = [A.alloc(512, F32) for _ in range(2)]
    ytmp_t = toks(2, "ytmp")
    late2 = []

    def flush2():
        for (g_, b_, yb_) in late2:
            CP(P, "pool", Ysb[yb_][:, g_ % 8, :].rearrange("p (c j) -> p j c", j=8),
               ytmp[b_].rearrange("p (j c) -> p j c", j=8), (ytmp_t[b_],), (Yo_t[yb_],))
            if g_ % 8 == 7:
                g0_ = g_ - 7
                P.dma("pool", Yd[g0_:g0_ + 8].rearrange("g i p m -> (i p) g m"), Ysb[yb_], (Yo_t[yb_],), (C.Yd_t,))
        del late2[:]

    for g in range(64):
        gi = g % GB
        if gi == 0:
            s5_powers(P, C, W, th, rho, ex, g, tk, slice(63, NE))
        b = g % 2
        PrY = W["Pr"][:, gi, 63:71].unsqueeze(2).to_broadcast(shpY)
        PiY = W["Pi"][:, gi, 63:71].unsqueeze(2).to_broadcast(shpY)
        bR = bbr[:, g, :].unsqueeze(1).to_broadcast(shpY)
        bI = bbi[:, g, :].unsqueeze(1).to_broadcast(shpY)
        TT(P, "dve", ty[0], PrY, bR, ALU.mult, (tk,), (ty_t[0],))
        TT(P, "dve", ty[1], PiY, bI, ALU.mult, (tk,), (ty_t[1],))
        TT(P, "dve", ty[2], PrY, bI, ALU.mult, (tk,), (ty_t[2],))
        TT(P, "dve", ty[3], PiY, bR, ALU.mult, (tk,), (ty_t[3],))
        TT(P, "pool", Yrb[b].rearrange("p (e q) -> p e q", q=16), ty[0], ty[1], ALU.subtract,
           (ty_t[0], ty_t[1]), (Yp_t[b],))
        STT(P, Yib[b].rearrange("p (e q) -> p e q", q=16), ty[2], -1.0, ty[3], ALU.mult, ALU.subtract,
            (ty_t[2], ty_t[3]), (Yp_t[b],))
        PrC = W["Pr"][:, gi, NEA:NE].unsqueeze(2).to_broadcast(shpC)
        PiC = W["Pi"][:, gi, NEA:NE].unsqueeze(2).to_broadcast(shpC)
        cR = cr[:, g, :].unsqueeze(1).to_broadcast(shpC)
        cI = ci[:, g, :].unsqueeze(1).to_broadcast(shpC)
        TT(P, "dve", tc_[0], PrC, cR, ALU.mult, (tk,), (tc_t[0],))
        TT(P, "dve", tc_[1], PiC, cI, ALU.mult, (tk,), (tc_t[1],))
        TT(P, "dve", tc_[2], PrC, cI, ALU.mult, (tk,), (tc_t[2],))
        TT(P, "dve", tc_[3], PiC, cR, ALU.mult, (tk,), (tc_t[3],))
        TT(P, "pool", CAr[b], tc_[0], tc_[1], ALU.subtract, (tc_t[0], tc_t[1]), (CA_t[b],))
        TT(P, "pool", CAi[b], tc_[2], tc_[3], ALU.add, (tc_t[2], tc_t[3]), (CA_t[b],))
        flush2()
        for half in range(2):
            pT_, pT_t = ps[half], ps_t[half]
            MM(P, pT_, Yrb[b], CAr[b][:, 32 * half:32 * half + 32, :].rearrange("p e q -> p (e q)"),
               True, False, (Yp_t[b], CA_t[b]), (pT_t,))
            MM(P, pT_, Yib[b], CAi[b][:, 32 * half:32 * half + 32, :].rearrange("p e q -> p (e q)"),
               False, True, (Yp_t[b], CA_t[b]), (pT_t,))
            CP(P, "act", Tsb[b][:, 4 * half:4 * half + 4, :], pT_.rearrange("p (d c) -> p d c", d=4),
               (pT_t,), (T_t[b],))
        TT(P, "pool", Tsb[b][:, 0, :], Tsb[b][:, 0, :], mask0, ALU.mult, (tk,), (T_t[b],))
        pY, pY_t = ps[4 + g % 2], ps_t[4 + g % 2]
        for jji in range(8):
            o = pY[:, jji * 64:(jji + 1) * 64]
            MM(P, o, CAr[b][:, 8 * jji + 1:8 * jji + 9, :].rearrange("p e q -> p (e q)"), Hr[:, g, :], True, False,
               (CA_t[b], H_t), (pY_t,))
            MM(P, o, CAi[b][:, 8 * jji + 1:8 * jji + 9, :].rearrange("p e q -> p (e q)"), Hi[:, g, :], False, False,
               (CA_t[b], H_t), ())
            for jjo in range(jji + 1):
                MM(P, o, Tsb[b][:, jji - jjo, :], U[:, g, jjo::8], False, jjo == jji,
                   (T_t[b], U_t), (pY_t,) if (jji == 7 and jjo == 7) else ())
        yb = (g // 8) % 2
        CP(P, "act", ytmp[b], pY, (pY_t,), (ytmp_t[b],))
        late2.append((g, b, yb))
    flush2()
    A.release()
    A.release()
    P.barrier()


def s5_post_phase(P, C, prm, Yd, uT_d, yT, ntok):
    nc, A, ps, ps_t = C.nc, C.A, C.ps, C.ps_t
    A.mark()
    tk = Tok("s5q")
    dsk = A.alloc(8, F32)
    bgl = A.alloc(8, F32)
    P.dma("sp", dsk, prm["d"][0].rearrange("(a p) -> p a", p=128), (), (tk,), allow_slow_non_contiguous=True)
    P.dma("sp", bgl, prm["b_glu"][0].rearrange("(a p) -> p a", p=128), (), (tk,), allow_slow_non_contiguous=True)
    zT = A.alloc(8 * SEQ, BF16).rearrange("p (a t) -> p a t", a=8)
    z_t = toks(8, "z")
    ring = make_ring(A, nst=3, nwb=3, wb_cols=8 * 128)
    ysb = [A.alloc(SEQ, F32).rearrange("p (j m) -> p j m", j=8) for _ in range(2)]
    ys_t = toks(2, "ys")
    usb = [A.alloc(SEQ, BF16) for _ in range(2)]
    us_t = toks(2, "us")
    zp = [A.alloc(SEQ, F32) for _ in range(2)]
    zp_t = toks(2, "zp")
    w_ = [A.alloc(SEQ, F32) for _ in range(2)]
    w_t = toks(2, "w")
    gate = [A.alloc(512, BF16) for _ in range(2)]
    gate_t = toks(2, "gate")
    ybst = [A.alloc(SEQ, BF16) for _ in range(2)]
    yb_t = toks(2, "yb")
    ng = 0
    for s in range(ntok // SEQ):
        s0 = s * SEQ
        for a in range(8):
            b = a % 2
            for gg in range(8):
                P.dma("sp", ysb[b][gg * 16:(gg + 1) * 16, :, :],
                      Yd[a * 8 + gg, :, :, s * 256:(s + 1) * 256].rearrange("j p m -> p j m"), (C.Yd_t,), (ys_t[b],))
            P.dma("sp", usb[b], uT_d[a * 128:(a + 1) * 128, s0:s0 + SEQ], (C.uT_t,), (us_t[b],))
            STT(P, zp[b].rearrange("p (m j) -> p j m", j=8), usb[b].rearrange("p (m j) -> p j m", j=8),
                dsk[:, a:a + 1], ysb[b], ALU.mult, ALU.add, (us_t[b], ys_t[b], tk), (zp_t[b],))
            TT(P, "pool", w_[b], zp[b], zp[b], ALU.mult, (zp_t[b],), (w_t[b],))
            TS(P, "dve", w_[b], w_[b], 0.044715, 1.0, ALU.mult, ALU.add, (), (w_t[b],))
            TT(P, "pool", w_[b], w_[b], zp[b], ALU.mult, (zp_t[b],), (w_t[b],))
            ACTV(P, w_[b], w_[b], AF.Sigmoid, (), (w_t[b],), scale=1.5957691216057308)
            TT(P, "dve", zT[:, a, :], zp[b], w_[b], ALU.mult, (zp_t[b], w_t[b]), (z_t[a],))
        for oc in range(8):
            wb, wb_t = stream_w(P, C, prm["w_glu"][0][:, oc * 128:(oc + 1) * 128], 8, 128, ring)
            yb = oc % 2
            for blk in range(4):
                pp, pp_t = ps[blk % 2], ps_t[blk % 2]
                for k in range(8):
                    MM(P, pp, wb[:, k * 128:(k + 1) * 128], zT[:, k, blk * 512:(blk + 1) * 512], k == 0, k == 7,
                       (wb_t, z_t[k]), (pp_t,) if k in (0, 7) else ())
                gi = ng % 2
                ng += 1
                ACTV(P, gate[gi], pp, AF.Sigmoid, (pp_t, tk), (gate_t[gi],), bias=bgl[:, oc:oc + 1])
                TT(P, "dve", ybst[yb][:, blk * 512:(blk + 1) * 512], zT[:, oc, blk * 512:(blk + 1) * 512], gate[gi],
                   ALU.mult, (gate_t[gi], z_t[oc]), (yb_t[yb],))
            P.dma("pool", yT[1024 + oc * 128:1024 + (oc + 1) * 128, s0:s0 + SEQ], ybst[yb], (yb_t[yb],), (C.yT_t,))
    A.release()
    P.barrier()


def host_consts_s5(c):
    ex = np.concatenate([63.0 - np.arange(NEA), np.arange(NEC)]).astype(np.float32)
    c["s5_ex"] = ex[None, :]
    m = np.zeros((128, 128), np.float32)
    for j in range(8):
        for i in range(8):
            if i >= j:
                m[j * 16:(j + 1) * 16, i * 16:(i + 1) * 16] = 1.0
    c["s5_mask"] = m
    return c


CONST_SHAPES.update({"s5_ex": [1, NE], "s5_mask": [128, 128]})


def all_host_consts():
    c = host_consts()
    host_consts_ret(c)
    host_consts_s5(c)
    return c


def build_mix0_test(ntok, phases=(1, 1, 1)):
    nc = bass.Bass("TRN2", target_bir_lowering=False)
    hT = nc.dram_tensor("hT", [D, ntok], F32, kind="ExternalInput").ap()
    g = nc.dram_tensor("g", [D], F32, kind="ExternalInput").ap()
    w_in = nc.dram_tensor("w_in", [D, 5120], F32, kind="ExternalInput").ap()
    wo = nc.dram_tensor("wo", [D, D], F32, kind="ExternalInput").ap()
    prm = {}
    for n, shp in (("lam_re", [1, 64, 64]), ("lam_im", [1, 64, 64]), ("log_step", [1, 64]),
                   ("b_re", [1, 64, 64, 16]), ("b_im", [1, 64, 64, 16]), ("c_re", [1, 64, 16, 64]),
                   ("c_im", [1, 64, 16, 64]), ("d", [1, 1024]), ("w_glu", [1, 1024, 1024]), ("b_glu", [1, 1024])):
        prm[n] = nc.dram_tensor(n, shp, F32, kind="ExternalInput").ap()
    cst = {k: nc.dram_tensor(k, v, F32, kind="ExternalInput").ap() for k, v in CONST_SHAPES.items()
           if not k.startswith("att_")}
    yT = nc.dram_tensor("yT", [D, ntok], BF16, kind="ExternalOutput").ap()
    uT_d = nc.dram_tensor("uT_d", [1024, ntok], BF16, kind="ExternalOutput").ap()
    nm = ntok // 8
    Ud = nc.dram_tensor("Ud", [64, 8, 16, 512], BF16, kind="ExternalOutput").ap()
    Yd = nc.dram_tensor("Yd", [64, 8, 16, 512], F32, kind="ExternalOutput").ap()
    oT = nc.dram_tensor("oT", [D, ntok], F32, kind="ExternalOutput").ap()
    P = Prog(nc)
    C = setup_common(nc, P)
    C.yT_t, C.uT_t, C.Ud_t, C.Yd_t = Tok("yT"), Tok("uT"), Tok("Ud"), Tok("Yd")
    import os
    C.skip_ud = bool(os.environ.get("SKIP_UD"))
    C.skip_ret = bool(os.environ.get("SKIP_RET"))
    C.skip_u = bool(os.environ.get("SKIP_U"))
    def dd(n, shp, dt=F32):
        return nc.dram_tensor(n, shp, dt, kind="ExternalOutput").ap()
    C.dbg = {"pr": dd("d_pr", [64, GB, NE]), "pi": dd("d_pi", [64, GB, NE]), "fr": dd("d_fr", [64, 64]),
             "fi": dd("d_fi", [64, 64]), "bbr": dd("d_bbr", [64, 64, 16]), "cr": dd("d_cr", [64, 64, 16]),
             "S": dd("d_S", [64, 2, 64, 64]), "Hr": dd("d_Hr", [64, 64, 64], BF16), "Hi": dd("d_Hi", [64, 64, 64], BF16)}
    if phases[0]:
        mix0a_phase(P, C, hT, g, w_in, cst, yT, uT_d, Ud, ntok)
    if phases[1]:
        s5_phase(P, C, prm, cst, Ud, Yd)
    if phases[2]:
        s5_post_phase(P, C, prm, Yd, uT_d, yT, ntok)
    out_proj_phase(P, C, yT, wo, hT, oT, ntok)
    P.finish()
    P.emit()
    return nc, P


def final_norm_phase(P, C, h_in, gain_dram, outT, ntok):
    A, ps, ps_t = C.A, C.ps, C.ps_t
    A.mark()
    gain, gain_t = load_gain(P, C, gain_dram)
    xk = [A.alloc(2 * 512, F32) for _ in range(3)]
    xk_t = toks(3, "fxk")
    sq = [A.alloc(2 * 512, BF16) for _ in range(2)]
    sq_t = toks(2, "fsq")
    rstd = A.alloc(512, F32)
    rstd_t = Tok("frstd")
    ssq, ssq_t = ps[7][:, 0:512], ps_t[7]
    n = 0
    for blk in range(ntok // 512):
        c0 = blk * 512
        for k2 in range(KD // 2):
            b = n % 3
            n += 1
            P.dma("sp", xk[b].rearrange("p (k t) -> p k t", k=2),
                  h_in[k2 * 256:(k2 + 1) * 256, c0:c0 + 512].rearrange("(k p) t -> p k t", p=128), (), (xk_t[b],))
            s = k2 % 2
            ACTV(P, sq[s], xk[b], AF.Square, (xk_t[b],), (sq_t[s],))
            for j in range(2):
                k = k2 * 2 + j
                MM(P, ssq, C.ones_bf, sq[s][:, j * 512:(j + 1) * 512], k == 0, k == KD - 1,
                   (sq_t[s], C.const_t), (ssq_t,) if k in (0, KD - 1) else ())
        TS(P, "dve", rstd, ssq, 1.0 / D, EPS, ALU.mult, ALU.add, (ssq_t,), (rstd_t,))
        rstd_inplace(P, rstd, rstd_t)
        for k2 in range(KD // 2):
            b = n % 3
            n += 1
            P.dma("sp", xk[b].rearrange("p (k t) -> p k t", k=2),
                  h_in[k2 * 256:(k2 + 1) * 256, c0:c0 + 512].rearrange("(k p) t -> p k t", p=128), (), (xk_t[b],))
            for j in range(2):
                k = k2 * 2 + j
                STT(P, xk[b][:, j * 512:(j + 1) * 512], xk[b][:, j * 512:(j + 1) * 512], gain[:, k:k + 1], rstd,
                    ALU.mult, ALU.mult, (xk_t[b], rstd_t, gain_t), (xk_t[b],))
            P.dma("pool", outT[k2 * 256:(k2 + 1) * 256, c0:c0 + 512].rearrange("(k p) t -> p k t", p=128),
                  xk[b].rearrange("p (k t) -> p k t", k=2), (xk_t[b],), ())
    A.release()
    P.barrier()


IN_SHAPES = {
    "ffn_norm": [2, 2, D], "ffn_w_gate": [2, 2, D, DFF], "ffn_w_up": [2, 2, D, DFF], "ffn_w_down": [2, 2, DFF, D],
    "mix_norm": [2, D], "ab_w_in": [1, D, 5120], "ab_w_out": [1, D, D],
    "ssm_lambda_re": [1, 64, 64], "ssm_lambda_im": [1, 64, 64], "ssm_log_step": [1, 64],
    "ssm_b_re": [1, 64, 64, 16], "ssm_b_im": [1, 64, 64, 16], "ssm_c_re": [1, 64, 16, 64], "ssm_c_im": [1, 64, 16, 64],
    "ssm_d": [1, 1024], "ssm_w_glu": [1, 1024, 1024], "ssm_b_glu": [1, 1024],
    "c_w_qkv": [1, D, 3 * D], "c_w_out": [1, D, D], "c_lambda_q1": [1, 128], "c_lambda_k1": [1, 128],
    "c_lambda_q2": [1, 128], "c_lambda_k2": [1, 128], "c_subln": [1, 256], "final_norm": [D],
}
SCRATCH_KIND = "Internal"


def build_full(ntok=TOK):
    nc = bass.Bass("TRN2", target_bir_lowering=False)
    xT = nc.dram_tensor("xT", [D, ntok], F32, kind="ExternalInput").ap()
    w = {k: nc.dram_tensor(k, v, F32, kind="ExternalInput").ap() for k, v in IN_SHAPES.items()}
    cst = {k: nc.dram_tensor(k, v, F32, kind="ExternalInput").ap() for k, v in CONST_SHAPES.items()}
    outT = nc.dram_tensor("outT", [D, ntok], F32, kind="ExternalOutput").ap()
    hA = nc.dram_tensor("hA", [D, ntok], F32, kind=SCRATCH_KIND).ap()
    hB = nc.dram_tensor("hB", [D, ntok], F32, kind=SCRATCH_KIND).ap()
    yT = nc.dram_tensor("yT", [D, ntok], BF16, kind=SCRATCH_KIND).ap()
    uT_d = nc.dram_tensor("uT_d", [1024, ntok], BF16, kind=SCRATCH_KIND).ap()
    Ud = nc.dram_tensor("Ud", [64, 8, 16, 512], BF16, kind=SCRATCH_KIND).ap()
    Yd = nc.dram_tensor("Yd", [64, 8, 16, 512], F32, kind=SCRATCH_KIND).ap()
    P = Prog(nc)
    C = setup_common(nc, P)
    C.yT_t, C.uT_t, C.Ud_t, C.Yd_t = Tok("yT"), Tok("uT"), Tok("Ud"), Tok("Yd")
    prm = {"lam_re": w["ssm_lambda_re"], "lam_im": w["ssm_lambda_im"], "log_step": w["ssm_log_step"],
           "b_re": w["ssm_b_re"], "b_im": w["ssm_b_im"], "c_re": w["ssm_c_re"], "c_im": w["ssm_c_im"],
           "d": w["ssm_d"], "w_glu": w["ssm_w_glu"], "b_glu": w["ssm_b_glu"]}

    def ffn(l, j, src, dst):
        ffn_phase(P, C, src, dst, w["ffn_norm"][l, j], w["ffn_w_gate"][l, j], w["ffn_w_up"][l, j],
                  w["ffn_w_down"][l, j], ntok)

    ffn(0, 0, xT, hA)
    mix0a_phase(P, C, hA, w["mix_norm"][0], w["ab_w_in"][0], cst, yT, uT_d, Ud, ntok)
    s5_phase(P, C, prm, cst, Ud, Yd)
    s5_post_phase(P, C, prm, Yd, uT_d, yT, ntok)
    out_proj_phase(P, C, yT, w["ab_w_out"][0], hA, hB, ntok)
    ffn(0, 1, hB, hA)
    ffn(1, 0, hA, hB)
    attn_phase(P, C, hB, w["mix_norm"][1], w["c_w_qkv"][0], w["c_lambda_q1"], w["c_lambda_k1"], w["c_lambda_q2"],
               w["c_lambda_k2"], w["c_subln"], cst, yT, ntok)
    out_proj_phase(P, C, yT, w["c_w_out"][0], hB, hA, ntok)
    ffn(1, 1, hA, hB)
    final_norm_phase(P, C, hB, w["final_norm"], outT, ntok)
    P.finish()
    P.emit()
    return nc, P


_CACHE = {}


def kernel(**inputs):
    x = np.asarray(inputs["x"], dtype=np.float32)
    B = x.shape[0]
    per = B // NCORES
    if "nc" not in _CACHE:
        _CACHE["nc"] = build_full(per * SEQ)[0]
        _CACHE["cst"] = all_host_consts()
    nc = _CACHE["nc"]
    shared = {k: np.ascontiguousarray(np.asarray(inputs[k], dtype=np.float32)) for k in IN_SHAPES}
    shared.update(_CACHE["cst"])
    in_maps = []
    for c in range(NCORES):
        xc = x[c * per:(c + 1) * per].reshape(per * SEQ, D)
        m = dict(shared)
        m["xT"] = np.ascontiguousarray(xc.T)
        in_maps.append(m)
    res = run_bass_kernel_spmd(nc, in_maps, core_ids=list(range(NCORES)))
    out = np.empty((B, SEQ, D), np.float32)
    for c in range(NCORES):
        oT = np.asarray(res.results[c]["outT"])
        out[c * per:(c + 1) * per] = np.ascontiguousarray(oT.T).reshape(per, SEQ, D)
    return out
```
